# Optimizing a Trainium2 kernel written in Bass

```python
import jax, jax.numpy as jnp
from jax import lax
import numpy as np

D_MODEL = 1024
BATCH = 8
SEQ = 2048
DEPTH = 4
DEC_BATCH = 128
DEC_SEQ = 1
PAST_LEN = 16384
PAGE_SIZE = 128

D_MIX = D_MODEL
D_RET = D_MIX // 2
H_RET = 4
DK = D_RET // H_RET
DV = D_RET // H_RET
D_POOL = D_MIX - D_RET
POOL_WINDOWS = (2, 4, 8, 16)
N_POOL_GROUPS = len(POOL_WINDOWS)
POOL_GC = D_POOL // N_POOL_GROUPS
POOL_BUF = max(POOL_WINDOWS) - 1
D_IN = 4 * D_RET + D_POOL
D_FF = ((8 * D_MODEL // 3 + 255) // 256) * 256
CHUNK = 128
ROPE_THETA = 10000.0
EPS = 1e-6

kernel_name = "hybrid_retention_pool_decoder_step"


def rmsnorm(x, g):
    xf = x.astype(jnp.float32)
    r = xf * lax.rsqrt(jnp.mean(xf * xf, axis=-1, keepdims=True) + EPS)
    return (r * g.astype(jnp.float32)).astype(x.dtype)


def modulate(h, shift, scale):
    return h * (1.0 + scale[:, None, :]) + shift[:, None, :]


def rope(x, pos):
    d = x.shape[-1]
    inv = 1.0 / (ROPE_THETA ** (jnp.arange(0, d, 2, dtype=jnp.float32) / d))
    ang = pos.astype(jnp.float32)[:, None] * inv[None, :]
    cos = jnp.cos(ang)[None, :, None, :]
    sin = jnp.sin(ang)[None, :, None, :]
    xf = x.astype(jnp.float32)
    x1, x2 = xf[..., : d // 2], xf[..., d // 2:]
    return jnp.concatenate([x1 * cos - x2 * sin, x1 * sin + x2 * cos], axis=-1)


def retention(q, k, v, S0):
    B, T, H, _ = q.shape
    C = CHUNK if T % CHUNK == 0 else T
    NC = T // C
    log_gamma = jnp.log(1.0 - 2.0 ** (-5.0 - jnp.arange(H, dtype=jnp.float32)))
    idx = jnp.arange(C, dtype=jnp.float32)
    diff = idx[:, None] - idx[None, :]
    dmask = jnp.where(diff[None] >= 0,
                      jnp.exp(jnp.maximum(diff, 0.0)[None] * log_gamma[:, None, None]), 0.0)
    decay_q = jnp.exp((idx + 1.0)[None, :] * log_gamma[:, None])
    decay_k = jnp.exp((C - 1.0 - idx)[None, :] * log_gamma[:, None])
    decay_c = jnp.exp(C * log_gamma)

    def to_chunks(a):
        return jnp.moveaxis(a.astype(jnp.float32).reshape(B, NC, C, H, a.shape[-1]), 1, 0)

    def step(S, qkv):
        qc, kc, vc = qkv
        s = jnp.einsum('bihd,bjhd->bhij', qc, kc) * dmask[None]
        inner = jnp.einsum('bhij,bjhe->bihe', s, vc)
        cross = jnp.einsum('bihd,bhde->bihe', qc, S) * jnp.transpose(decay_q)[None, :, :, None]
        S_new = S * decay_c[None, :, None, None] + jnp.einsum('bjhd,bjhe,hj->bhde', kc, vc, decay_k)
        return S_new, inner + cross

    S_fin, o = lax.scan(step, S0.astype(jnp.float32), (to_chunks(q), to_chunks(k), to_chunks(v)))
    o = jnp.moveaxis(o, 0, 1).reshape(B, T, H, v.shape[-1])
    return o, S_fin


def pool_mixer(u, buf, n_past, pool_w, pool_scale):
    B, T, _ = u.shape
    xp = jnp.concatenate([buf.astype(u.dtype), u], axis=1).astype(jnp.float32)
    cs = jnp.concatenate([jnp.zeros((B, 1, D_POOL), jnp.float32), jnp.cumsum(xp, axis=1)], axis=1)
    end = cs[:, POOL_BUF + 1:]
    cur = xp[:, POOL_BUF:]
    t = jnp.arange(T)
    outs = []
    for g, w in enumerate(POOL_WINDOWS):
        lo, hi = g * POOL_GC, (g + 1) * POOL_GC
        start = cs[:, POOL_BUF + 1 - w: POOL_BUF + 1 - w + T, lo:hi]
        cnt = jnp.minimum(t + n_past + 1, w).astype(jnp.float32)
        outs.append((end[..., lo:hi] - start) / cnt[None, :, None] - cur[..., lo:hi])
    p = jnp.stack(outs, axis=2)
    y = jnp.einsum('btgc,gcd->btgd', p, pool_w.astype(jnp.float32)).reshape(B, T, D_POOL)
    y = y * pool_scale.astype(jnp.float32)
    new_buf = xp[:, -POOL_BUF:].astype(u.dtype)
    return y, new_buf


def layer(x, c, S0, buf, n_past, norm_mix, norm_ffn, w_ada, b_ada, w_in, ret_gn,
          pool_w, pool_scale, w_out, w_gu, w_down):
    B, T, _ = x.shape
    mod = jax.nn.silu(c) @ w_ada + b_ada
    shift_m, scale_m, gate_m, shift_f, scale_f, gate_f = jnp.split(mod, 6, axis=-1)

    h = modulate(rmsnorm(x, norm_mix), shift_m, scale_m)
    z = h @ w_in
    q, k, v, g, u = jnp.split(z, [D_RET, 2 * D_RET, 3 * D_RET, 4 * D_RET], axis=-1)
    pos = n_past + jnp.arange(T, dtype=jnp.int32)
    q = rope(q.reshape(B, T, H_RET, DK), pos)
    k = rope(k.reshape(B, T, H_RET, DK), pos) * (DK ** -0.5)
    v = v.reshape(B, T, H_RET, DV)
    o, S_new = retention(q, k, v, S0)
    o = rmsnorm(o, ret_gn.reshape(H_RET, DV)).reshape(B, T, D_RET)
    o = (o * jax.nn.silu(g.astype(jnp.float32))).astype(x.dtype)
    p, buf_new = pool_mixer(u, buf, n_past, pool_w, pool_scale)
    mix = jnp.concatenate([o, p.astype(x.dtype)], axis=-1) @ w_out
    x = x + gate_m[:, None, :] * mix

    h2 = modulate(rmsnorm(x, norm_ffn), shift_f, scale_f)
    a, b = jnp.split(h2 @ w_gu, 2, axis=-1)
    x = x + gate_f[:, None, :] * ((jax.nn.silu(a) * b) @ w_down)
    return x, S_new, buf_new


def setup_inputs(seed: int = 0) -> dict:
    key = jax.random.key(seed)
    ks = jax.random.split(key, 20)
    f32 = jnp.float32
    nrm = lambda k, s, sc: jax.random.normal(k, s, f32) * sc
    return {
        "x_prompt": nrm(ks[0], (BATCH, SEQ, D_MODEL), 1.0),
        "x_sample": nrm(ks[1], (DEC_BATCH, DEC_SEQ, D_MODEL), 1.0),
        "c_prompt": nrm(ks[2], (BATCH, D_MODEL), 1.0),
        "c_sample": nrm(ks[3], (DEC_BATCH, D_MODEL), 1.0),
        "state_ret": nrm(ks[4], (DEPTH, DEC_BATCH, H_RET, DK, DV), 1.0),
        "state_pool": nrm(ks[5], (DEPTH, DEC_BATCH, POOL_BUF, D_POOL), 1.0),
        "norm_mix": 1.0 + nrm(ks[6], (DEPTH, D_MODEL), 0.02),
        "norm_ffn": 1.0 + nrm(ks[7], (DEPTH, D_MODEL), 0.02),
        "w_ada": nrm(ks[8], (DEPTH, D_MODEL, 6 * D_MODEL), 0.5 * D_MODEL ** -0.5),
        "b_ada": nrm(ks[9], (DEPTH, 6 * D_MODEL), 0.01),
        "w_in": nrm(ks[10], (DEPTH, D_MODEL, D_IN), D_MODEL ** -0.5),
        "ret_gn": 1.0 + nrm(ks[11], (DEPTH, D_RET), 0.02),
        "pool_w": nrm(ks[12], (DEPTH, N_POOL_GROUPS, POOL_GC, POOL_GC), POOL_GC ** -0.5),
        "pool_scale": 1.0 + nrm(ks[13], (DEPTH, D_POOL), 0.02),
        "w_out": nrm(ks[14], (DEPTH, D_MIX, D_MODEL), D_MIX ** -0.5),
        "w_gu": nrm(ks[15], (DEPTH, D_MODEL, 2 * D_FF), D_MODEL ** -0.5),
        "w_down": nrm(ks[16], (DEPTH, D_FF, D_MODEL), D_FF ** -0.5),
        "final_norm": 1.0 + nrm(ks[17], (D_MODEL,), 0.02),
    }


def reference(x_prompt, x_sample, c_prompt, c_sample, state_ret, state_pool,
              norm_mix, norm_ffn, w_ada, b_ada, w_in, ret_gn, pool_w, pool_scale,
              w_out, w_gu, w_down, final_norm):
    xp, xs = x_prompt, x_sample
    S_p0 = jnp.zeros((BATCH, H_RET, DK, DV), jnp.float32)
    buf_p0 = jnp.zeros((BATCH, POOL_BUF, D_POOL), x_prompt.dtype)
    ret_p, pool_p, ret_s, pool_s = [], [], [], []
    for l in range(DEPTH):
        params = (norm_mix[l], norm_ffn[l], w_ada[l], b_ada[l], w_in[l], ret_gn[l],
                  pool_w[l], pool_scale[l], w_out[l], w_gu[l], w_down[l])
        xp, Sp, bp = layer(xp, c_prompt, S_p0, buf_p0, 0, *params)
        xs, Ss, bs = layer(xs, c_sample, state_ret[l], state_pool[l], PAST_LEN, *params)
        ret_p.append(Sp)
        pool_p.append(bp)
        ret_s.append(Ss)
        pool_s.append(bs)
    y_prompt = rmsnorm(xp, final_norm)
    y_sample = rmsnorm(xs, final_norm)
    new_ret_prompt = jnp.stack(ret_p, axis=0)
    new_pool_prompt = jnp.stack(pool_p, axis=0)
    new_ret_sample = jnp.stack(ret_s, axis=0)
    new_pool_sample = jnp.stack(pool_s, axis=0)
    return (y_prompt, y_sample, new_ret_prompt, new_pool_prompt, new_ret_sample, new_pool_sample)
```

```python
import numpy as np
from contextlib import ExitStack

import concourse.bass as bass
import concourse.mybir as mybir
from concourse.bass_utils import run_bass_kernel_spmd

F32 = mybir.dt.float32
BF16 = mybir.dt.bfloat16
ALU = mybir.AluOpType
AF = mybir.ActivationFunctionType

D = 1024
KC = 8
T = 2048
NS = 16
NCORES = 8
DEPTH = 4
H = 4
DK = 128
DIN = 2560
FF = 2816
FC = 22
EPS = 1e-6
BT = 256
NTB = BT // 128
NBLK = T // BT
HALF = 1024
FGROUPS = [(0, 8), (8, 16), (16, 22)]
POOL_W = (2, 4, 8, 16)
NRING = 4
RING_ADA, RING_WOUT, RING_GU, RING_DOWN = 0, 24, 28, 50
NRSLOT = 61
GAM = [1.0 - 2.0 ** (-5.0 - h) for h in range(H)]
GC = [g ** 128 for g in GAM]

C_ID, C_CAUS, C_DQTM, C_DKTM, C_ID16, C_SEL, C_ROPES = 0, 128, 256, 260, 264, 520, 648
C_TOT = 1160
NBAND = 1088
V_NM, V_NF, V_BADA, V_GN, V_PS, V_FN = 0, 32, 64, 256, 272, 288


def _host_consts():
    cst = np.zeros((128, C_TOT), np.float32)
    cst[:, C_ID:C_ID + 128] = np.eye(128, dtype=np.float32)
    j = np.arange(128)[:, None]
    i = np.arange(128)[None, :]
    cst[:, C_CAUS:C_CAUS + 128] = (i >= j).astype(np.float32)
    n = np.arange(128, dtype=np.float64)
    for h in range(H):
        lg = np.log(np.float64(np.float32(GAM[h])))
        cst[:, C_DQTM + h] = np.exp((n + 1.0) * lg)
        cst[:, C_DKTM + h] = (DK ** -0.5) * np.exp(-(n + 1.0) * lg)
    cst[:, C_ID16:C_ID16 + 256] = np.eye(16, dtype=np.float32).reshape(1, 256)
    sel = np.zeros((128, 2, 4, 16), np.float32)
    for t in range(2):
        for bl in range(8):
            for r in range(15):
                for g, w in enumerate(POOL_W):
                    if r >= 16 - w:
                        sel[bl * 15 + r, t, g, t * 8 + bl] = 1.0
    cst[:, C_SEL:C_SEL + 128] = sel.reshape(128, 128)
    inv = (1.0 / (np.float32(10000.0) ** (np.arange(0, 128, 2, dtype=np.float32) / np.float32(128)))).astype(np.float32)

    def tables(pos):
        ang = (np.asarray(pos, np.float32)[:, None] * inv[None, :]).astype(np.float32)
        c = np.cos(ang.astype(np.float64)).astype(np.float32)
        s = np.sin(ang.astype(np.float64)).astype(np.float32)
        return np.concatenate([c, c], -1), np.concatenate([-s, s], -1)

    c2, s2 = tables(np.array([16384.0], np.float32))
    sc = np.float32(DK ** -0.5)
    ropes = np.concatenate([c2, s2, c2 * sc, s2 * sc], -1)
    cst[:, C_ROPES:C_ROPES + 512] = ropes
    c2p, s2p = tables(np.arange(T, dtype=np.float32))
    ropep = np.concatenate([c2p, s2p], -1).reshape(T // 128, 128, 256).astype(np.float32)
    bands = np.zeros((128, NBAND), np.float32)
    sidx = np.arange(128)[:, None]
    tidx = np.arange(128)[None, :]
    for g, w in enumerate(POOL_W):
        cur = ((sidx <= tidx) & (sidx > tidx - w)).astype(np.float64) / w - (sidx == tidx)
        cnt = np.minimum(tidx + 1, w).astype(np.float64)
        cur0 = ((sidx <= tidx) & (sidx > tidx - w)).astype(np.float64) / cnt - (sidx == tidx)
        bands[:, g * 128:(g + 1) * 128] = cur
        bands[:, 512 + g * 128:512 + (g + 1) * 128] = cur0
        t16 = np.arange(16)[None, :]
        prev = ((sidx - 128) > (t16 - w)).astype(np.float64) / w
        bands[:, 1024 + g * 16:1024 + (g + 1) * 16] = prev
    return cst, ropep, bands


def _slot(w2d):
    r, c = w2d.shape
    k = r // 128
    return np.ascontiguousarray(w2d.reshape(k, 128, c).transpose(1, 0, 2)).reshape(128, k * c)


def _host_weights(w_in, w_out, w_gu, w_down, w_ada, pool_w, nl):
    win = np.empty((nl, 5, 128, 4096), np.float32)
    wring = np.empty((nl, NRSLOT, 128, 2048), np.float32)
    poolw = np.empty((nl, 128, 512), np.float32)
    for l in range(nl):
        for j in range(5):
            win[l, j] = _slot(w_in[l][:, j * 512:(j + 1) * 512])
        for j in range(24):
            wring[l, RING_ADA + j] = _slot(w_ada[l][:, j * 256:(j + 1) * 256])
        for j in range(4):
            wring[l, RING_WOUT + j] = _slot(w_out[l][:, j * 256:(j + 1) * 256])
        for f in range(FC):
            ab = np.concatenate([w_gu[l][:, f * 128:(f + 1) * 128],
                                 w_gu[l][:, FF + f * 128:FF + (f + 1) * 128]], axis=1)
            wring[l, RING_GU + f] = _slot(ab)
        for j in range(11):
            wring[l, RING_DOWN + j] = _slot(w_down[l][j * 256:(j + 1) * 256, :])
        poolw[l] = np.ascontiguousarray(pool_w[l].transpose(1, 0, 2)).reshape(128, 512)
    return win, wring, poolw


class Op:
    __slots__ = ("idx", "eng", "fn", "sem", "inc", "deps", "cost", "lat", "t0", "t1", "sig", "nun", "users", "ready", "tag", "vs", "phase")

    def __init__(self, idx, eng, fn, sem, inc, deps, cost, lat):
        self.idx = idx
        self.eng = eng
        self.fn = fn
        self.sem = sem
        self.inc = inc
        self.deps = deps
        self.cost = cost
        self.lat = lat
        self.sig = None


class Slot:
    __slots__ = ("name", "w", "r", "ops", "rng")

    def __init__(self, name=""):
        self.name = name
        self.w = None
        self.r = []
        self.ops = None
        self.rng = None


class Eng:
    def __init__(self, name, sem, is_pe=False):
        self.name = name
        self.sem = sem
        self.cnt = 0
        self.waited = {}
        self.prog = []
        self.is_pe = is_pe
        self.ops = []


class DSem:
    def __init__(self, sem):
        self.sem = sem
        self.cnt = 0


def _n(ap):
    r = 1
    for v in ap.shape[1:]:
        r *= v
    return r


def _c(f, cost):
    f.cost = cost
    return f


def tt(out, in0, in1, op):
    return _c(lambda h: h.tensor_tensor(out=out, in0=in0, in1=in1, op=op), 70 + 1.6 * _n(out))


def stt(out, in0, scalar, in1, op0, op1):
    return _c(lambda h: h.scalar_tensor_tensor(out=out, in0=in0, scalar=scalar, in1=in1, op0=op0, op1=op1), 70 + 1.6 * _n(out))


def ts(out, in0, s1, s2, op0, op1=None):
    if op1 is None:
        return _c(lambda h: h.tensor_scalar(out=out, in0=in0, scalar1=s1, scalar2=None, op0=op0), 70 + 1.05 * _n(out))
    return _c(lambda h: h.tensor_scalar(out=out, in0=in0, scalar1=s1, scalar2=s2, op0=op0, op1=op1), 70 + 1.05 * _n(out))


def vcp(out, in_):
    return _c(lambda h: h.tensor_copy(out=out, in_=in_), 70 + 1.05 * _n(out))


def acp(out, in_):
    return _c(lambda h: h.copy(out=out, in_=in_), 200 + 0.75 * _n(out))


def act(out, in_, func, bias=None, scale=None):
    kw = {}
    if bias is not None:
        kw["bias"] = bias
    if scale is not None:
        kw["scale"] = scale
    return _c(lambda h: h.activation(out=out, in_=in_, func=func, **kw), 220 + 0.75 * _n(out))


def mm(out, lhsT, rhs, start=True, stop=True):
    n = _n(rhs)
    c = 20 + max(n, 64) / 2.2
    if lhsT.dtype == F32:
        c *= 4
    return _c(lambda h: h.matmul(out, lhsT, rhs, start=start, stop=stop), c)


def tr(out, in_, identity):
    c = 70.0 * (4 if in_.dtype == F32 else 1)
    return _c(lambda h: h.transpose(out=out, in_=in_, identity=identity), c)


def seq(fs):
    fs = list(fs)

    def run(h):
        i = None
        for f in fs:
            i = f(h)
        return i
    run.cost = sum(f.cost for f in fs)
    return run


def recip(out, in_):
    return _c(lambda h: h.reciprocal(out=out, in_=in_), 70 + 8.4 * _n(out))


def rfast(out, in_):
    return _c(lambda h: h.reciprocal_approx_fast(out=out, in_=in_), 70 + 2.0 * _n(out))


def mset(ap, v):
    return _c(lambda h: h.memset(ap, v), 70 + 1.05 * _n(ap))


def tred(out, in_, op):
    return _c(lambda h: h.tensor_reduce(out=out, in_=in_, axis=mybir.AxisListType.X, op=op), 70 + 1.05 * _n(in_))


def _prod(s):
    r = 1
    for v in s:
        r *= v
    return r


class Builder:
    def __init__(self, nl, plan=None, bank_seq=None):
        self.nl = nl
        self.plan = plan
        self.bank_seq = bank_seq
        self.nalloc = 0
        self.valloc = []
        self.rng_q = []
        self.rng_last = None
        self.nc = bass.Bass("TRN2", target_bir_lowering=False)
        self.es = ExitStack()
        self.nsem = 0

    def sem(self, name):
        self.nsem += 1
        return self.es.enter_context(self.nc.semaphore(name))

    def dsem(self, name):
        return DSem(self.sem(name))

    def carve(self, shape, dtype, parts=128):
        esz = 4 if dtype == F32 else 2
        nb = _prod(shape) * esz
        nb_al = (nb + 31) // 32 * 32
        off = self.off
        self.off += nb_al
        self.rng_q.append((off, off + nb_al))
        self.rng_last = (off, off + nb_al)
        self.last_carve = (off, off + nb_al)
        self.peak = max(self.peak, self.off)
        assert self.off <= self.sb_bytes, f"SBUF overflow {self.off} > {self.sb_bytes}"
        w0 = off // 4
        ap = self.sb[0:parts, w0:w0 + nb_al // 4]
        if dtype != F32:
            ap = ap.bitcast(dtype)
        ap = ap[:, 0:_prod(shape)]
        if len(shape) == 2:
            ap = ap.rearrange("p (a b) -> p a b", a=shape[0])
        elif len(shape) == 3:
            ap = ap.rearrange("p (a b c) -> p a b c", a=shape[0], b=shape[1])
        return ap

    def _deps(self, reads, writes):
        deps = set()
        for s in reads:
            if s.w is not None:
                deps.add(s.w)
        for s in writes:
            if s.w is not None:
                deps.add(s.w)
            deps.update(s.r)
        return deps

    def _commit(self, o, reads, writes):
        for s in writes:
            s.w = o
            s.r = []
            if s.ops is not None:
                s.ops.append(o)
        for s in reads:
            s.r.append(o)
            if s.ops is not None:
                s.ops.append(o)

    def op(self, E, fn, reads=(), writes=()):
        import sys as _sys
        self.tag = 'L%d' % _sys._getframe(1).f_lineno
        o = Op(len(self.ops), E, fn, E, 1, self._deps(reads, writes), float(getattr(fn, "cost", 300.0)), 0.0)
        o.lat = o.cost + 60.0
        o.vs = [x for x in set(list(reads) + list(writes)) if x.ops is not None]
        o.phase = self.phase
        o.tag = getattr(fn, 'tag', '') or self.tag
        self.ops.append(o)
        E.ops.append(o)
        self._commit(o, reads, writes)
        return o

    def dma(self, Q, out, in_, ds, reads=(), writes=(), nbytes=None, **kw):
        if nbytes is None:
            nbytes = _n(out) * out.shape[0] * 4
        fn = (lambda h, o=out, i=in_, kw=kw: h.dma_start(out=o, in_=i, **kw))
        cost = 1200.0 if Q is self.POOL else 120.0
        o = Op(len(self.ops), Q, fn, ds, 16, self._deps(reads, writes), cost, 2200.0 + nbytes / 180.0)
        o.tag = 'dma:' + self.tag
        o.vs = []
        o.phase = self.phase
        self.ops.append(o)
        Q.ops.append(o)
        self._commit(o, reads, writes)
        return o

    def alias_handoff(self, old_slots, new_slots):
        for sn in new_slots:
            toks = set()
            for so in old_slots:
                if sn.rng is not None and so.rng is not None and (so.rng[1] <= sn.rng[0] or sn.rng[1] <= so.rng[0]):
                    continue
                if so.w is not None:
                    toks.add(so.w)
                toks.update(so.r)
            sn.r = list(sn.r) + list(toks)

    def schedule(self, window=96):
        engs = [self.PE, self.ACT, self.DVE, self.POOL, self.SP]
        if self.bank_seq is not None:
            for seqb in self.bank_seq:
                for x, y in zip(seqb[:-1], seqb[1:]):
                    xo = self.valloc[x].ops
                    for oy in self.valloc[y].ops:
                        oy.deps.update(xo)
        for o in self.ops:
            o.nun = len(o.deps)
            o.users = []
            o.ready = 0.0
            o.t0 = None
        for o in self.ops:
            for d in o.deps:
                d.users.append(o)
        pend = {E: list(E.ops) for E in engs}
        order = {E: [] for E in engs}
        etime = {E: 0.0 for E in engs}
        remaining = len(self.ops)
        INF = float("inf")
        bank_free = [0.0] * 8
        vbank = {}
        vleft = {}
        vend = {}
        for sl in self.valloc:
            vleft[id(sl)] = len(sl.ops)
            vend[id(sl)] = 0.0
        use_banks = self.plan is None
        self.n_overcommit = 0
        vorder = sorted([sl for sl in self.valloc if sl.ops], key=lambda sl: (sl.ops[0].idx, sl.name))
        vrank = {id(sl): i for i, sl in enumerate(vorder)}
        alloc_ptr = [0]
        done_ranks = set()
        bseq = [[] for _ in range(8)]
        vidx = {id(sl): i for i, sl in enumerate(self.valloc)}

        def pick(ignore_banks):
            best = None
            for E in engs:
                lst = pend[E]
                if not lst:
                    continue
                et = etime[E]
                cb = None
                for o in lst[:window]:
                    if o.nun:
                        continue
                    st = o.ready if o.ready > et else et
                    if use_banks and o.vs and not ignore_banks:
                        k = 0
                        mn = None
                        for v in o.vs:
                            if id(v) not in vbank:
                                k += 1
                                r_ = vrank[id(v)]
                                if mn is None or r_ < mn:
                                    mn = r_
                        if k and mn > alloc_ptr[0] + 64:
                            continue
                        if k:
                            bt = sorted(bank_free)[k - 1]
                            if bt > st:
                                st = bt
                    if st == INF:
                        continue
                    if cb is None or st < cb[0]:
                        cb = (st, o)
                        if st <= et:
                            break
                if cb is not None and (best is None or cb[0] < best[0]):
                    best = (cb[0], cb[1], E)
            return best

        while remaining:
            best = pick(False)
            if best is None:
                best = pick(True)
            assert best is not None, "scheduler stuck (cyclic deps?)"
            st, o, E = best
            o.t0 = st
            o.t1 = st + max(o.cost, o.lat)
            etime[E] = st + o.cost
            pend[E].remove(o)
            order[E].append(o)
            remaining -= 1
            if use_banks:
                for v in o.vs:
                    if id(v) not in vbank:
                        ok = [b for b in range(8) if bank_free[b] <= st]
                        if ok:
                            bsel = max(ok, key=lambda x: bank_free[x])
                        else:
                            bsel = min(range(8), key=lambda x: (bank_free[x] == INF, bank_free[x]))
                        if not ok:
                            self.n_overcommit += 1
                        vbank[id(v)] = bsel
                        bank_free[bsel] = INF
                        done_ranks.add(vrank[id(v)])
                        while alloc_ptr[0] in done_ranks:
                            alloc_ptr[0] += 1
                        bseq[bsel].append(vidx[id(v)])
                for v in o.vs:
                    vleft[id(v)] -= 1
                    if o.t1 > vend[id(v)]:
                        vend[id(v)] = o.t1
                    if vleft[id(v)] == 0:
                        bank_free[vbank[id(v)]] = vend[id(v)]
            for u in o.users:
                u.nun -= 1
                if o.t1 > u.ready:
                    u.ready = o.t1
        if use_banks:
            self.bank_plan = [vbank.get(id(sl), 0) for sl in self.valloc]
            self.bank_seq_out = bseq
        self.est_ns = max(etime.values())
        for E in engs:
            E.cnt = 0
        allops = sorted(self.ops, key=lambda o: (o.t0, o.idx))
        for o in allops:
            o.sem.cnt += o.inc
            o.sig = o.sem.cnt
        for E in engs:
            waited = {}
            for o in order[E]:
                best = {}
                for d in o.deps:
                    if E.is_pe and d.eng is E and d.sem is E:
                        continue
                    k = id(d.sem)
                    if k not in best or best[k].sig < d.sig:
                        best[k] = d
                for k, d in best.items():
                    if waited.get(k, 0) >= d.sig:
                        continue
                    waited[k] = d.sig
                    E.prog.append(("w", d.sem.sem, d.sig))
                E.prog.append(("o", o.fn, o.sem.sem, o.inc))

    def pbank(self, pin=False, pool=None):
        i = self.nalloc
        self.nalloc += 1
        b = (i % 8) if self.plan is None else self.plan[i]
        sl = Slot(f"vbank{i}")
        sl.ops = []
        self.valloc.append(sl)
        return self.ps[:, b, :], sl

    def fbank(self, b):
        return self.pbank()

    def color_banks(self):
        iv = []
        for i, sl in enumerate(self.valloc):
            if not sl.ops:
                iv.append((0.0, 0.0, i))
                continue
            iv.append((min(o.t0 for o in sl.ops), max(o.t1 for o in sl.ops), i))
        plan = [0] * len(iv)
        free = [0.0] * 8
        for st, en, i in sorted(iv):
            ok = [b for b in range(8) if free[b] <= st]
            if ok:
                b = max(ok, key=lambda x: free[x])
            else:
                b = min(range(8), key=lambda x: free[x])
            plan[i] = b
            free[b] = max(free[b], en)
        return plan

    def unpin(self, slot):
        pass

    def build(self, emit=True):
        nc = self.nc
        nl = self.nl
        es = self.es
        dt = lambda name, shape, kind: nc.dram_tensor(name, shape, F32, kind=kind).ap()
        I, O = "ExternalInput", "ExternalOutput"
        self.xp = dt("xp", [T, D], I)
        self.xs = dt("xs", [NS, D], I)
        self.cc = dt("cc", [NS + 1, D], I)
        self.sret = dt("sret", [max(nl, 1), NS, H, 128, 128], I)
        self.spool = dt("spool", [max(nl, 1), NS, 15, 512], I)
        self.win = dt("win", [max(nl, 1), 5, 128, 4096], I)
        self.wring = dt("wring", [max(nl, 1), NRSLOT, 128, 2048], I)
        self.poolw = dt("poolw", [max(nl, 1), 128, 512], I)
        self.vecs = dt("vecs", [384, 128], I)
        self.cstd = dt("cst", [128, C_TOT], I)
        self.ropep = dt("ropep", [T // 128, 128, 256], I)
        self.bandsd = dt("bands", [128, NBAND], I)
        self.yp = dt("yp", [T, D], O)
        self.ys = dt("ys", [NS, D], O)
        self.rp = dt("rp", [max(nl, 1), H, 128, 128], O)
        self.pp = dt("pp", [max(nl, 1), 15, 512], O)
        self.rs = dt("rs", [max(nl, 1), NS, H, 128, 128], O)
        self.pso = dt("pso", [max(nl, 1), NS, 15, 512], O)

        self.sb_bytes = 207 * 1024
        self.sbt = es.enter_context(nc.sbuf_tensor("sb", [128, self.sb_bytes // 4], F32))
        self.sb = self.sbt[:, :]
        self.off = 0
        self.peak = 0
        self.pst = es.enter_context(nc.psum_tensor("ps", [128, 8, 512], F32))
        self.ps = self.pst[:, :, :]
        self.pslot = [Slot(f"psum{b}") for b in range(8)]
        self.pb_next = 0
        self.pinned = set()
        self.pbm_next = 4

        self.ops = []
        self.tag = ''
        self.phase = 'init'
        self.PE = Eng("pe", self.sem("s_pe"), is_pe=True)
        self.ACT = Eng("act", self.sem("s_act"))
        self.DVE = Eng("dve", self.sem("s_dve"))
        self.POOL = Eng("pool", self.sem("s_pool"))
        self.SP = Eng("sp", self.sem("s_sp"))
        self.out_ds = self.dsem("d_out")
        self.rp_ds = self.dsem("d_rp")

        self.final_ds = []
        self.fin = None
        self.alloc_persistent()
        self.phase_init()
        for l in range(nl):
            self.layer(l)
        self.phase_final()
        self.schedule()
        if not emit:
            self.es.close()
            return None
        self.SP.prog.append(("w", self.out_ds.sem, self.out_ds.cnt))
        for ds in self.stage_ds + [self.rp_ds] + self.final_ds:
            if ds.cnt:
                self.SP.prog.append(("w", ds.sem, ds.cnt))

        def replay(E):
            def run(h):
                for it in E.prog:
                    if it[0] == "w":
                        h.wait_ge(it[1], it[2])
                    else:
                        ins = it[1](h)
                        ins.then_inc(it[2], it[3])
            return run

        with nc.Block() as block:
            block.tensor(replay(self.PE))
            block.scalar(replay(self.ACT))
            block.vector(replay(self.DVE))
            block.gpsimd(replay(self.POOL))
            block.sync(replay(self.SP))
        self.es.close()
        return nc

    def alloc_persistent(self):
        c = self.carve
        self.xT = c([KC, T], F32)
        self.xslot = [[Slot(f"x{k}_{b}") for b in range(NBLK)] for k in range(KC)]
        self.xsT = c([KC, NS], F32)
        self.xs_slot = Slot("xs")
        self.cst = c([C_TOT], F32)
        self.cst_slot = Slot("cst")
        self.bandb = c([NBAND], BF16)
        self.band_slot = Slot("bands")
        self.identb = c([128], BF16)
        self.onesb = c([128], BF16)
        self.zerob = c([H, 128], BF16)
        self.const2_slot = Slot("const2")
        self.vecT = c([384], F32)
        self.vec_slot = Slot("vecT")
        self.scT = c([KC, NS + 1], BF16)
        self.sc_slot = Slot("scT")
        self.modT = c([48, NS + 1], F32)
        self.mod_slots = [Slot("modT0"), Slot("modT1")]
        self.pm = c([6, KC], F32)
        self.pm_slots = [Slot("pm0"), Slot("pm1")]
        self.smG = c([2, KC, NS], F32)
        self.smG_slots = [Slot("smG0"), Slot("smG1")]
        self.hbuf = c([KC, HALF], BF16)
        self.hslot = [[Slot(f"h{k}_{i}") for i in range(HALF // BT)] for k in range(KC)]
        self.winb_off = self.off
        self.winb = c([5, KC, 512], BF16)
        self.win_slot = [Slot(f"win{j}") for j in range(5)]
        self.win_ds = [self.dsem(f"d_win{j}") for j in range(5)]
        self.adar = self.winb.rearrange("p j k c -> p (j k c)").rearrange("p (i x) -> p i x", i=10)
        self.ada_slot = [Slot(f"adar{i}") for i in range(10)]
        self.ada_ds = [self.dsem(f"d_adar{i}") for i in range(10)]
        self.pwb = c([H, 128], BF16)
        self.pw_slot = Slot("poolw")
        self.pw_ds = self.dsem("d_pw")
        self.ring = c([NRING, 2048], BF16)
        self.ring_slot = [Slot(f"ring{j}") for j in range(NRING)]
        self.ring_ds = [self.dsem(f"d_ring{j}") for j in range(NRING)]
        self.S = c([H, 128], F32)
        self.Sb = c([H, 128], BF16)
        self.S_slot = Slot("S")
        self.Sb_slot = Slot("Sb")
        self.stage_ds = [self.dsem(f"d_stage{i}") for i in range(4)]
        self.mixTs = c([KC, NS], BF16)
        self.mixTs_slot = Slot("mixTs")
        self.tw = c([KC, NS], F32)
        self.tw_slot = Slot("tw")
        self.arena0 = self.off
        self.arena_slots = []
        self.ring_seq = []
        for l in range(self.nl):
            if l == 0:
                self.ring_seq += [(l, RING_ADA + j) for j in range(12)]
            for b in range(NBLK):
                self.ring_seq += [(l, RING_WOUT + j) for j in range(4)]
                if l == 0 and b == 0:
                    self.ring_seq += [(l, RING_ADA + j) for j in range(12, 24)]
            for hf in range(2):
                for (f0, f1) in FGROUPS:
                    self.ring_seq += [(l, RING_GU + f) for f in range(f0, f1)]
        self.ring_issued = 0
        self.ring_used = 0

    def ring_issue(self):
        if self.ring_issued >= len(self.ring_seq):
            return
        i = self.ring_issued
        self.ring_issued += 1
        l, si = self.ring_seq[i]
        j = i % NRING
        self.dma(self.POOL, self.ring[:, j, :], self.wring[l, si], self.ring_ds[j],
                 writes=[self.ring_slot[j]], max_dma_last_dim=8192)

    def ring_next(self, l, si):
        i = self.ring_used
        assert self.ring_seq[i] == (l, si), (self.ring_seq[i], l, si)
        while self.ring_issued <= i:
            self.ring_issue()
        self.ring_used += 1
        j = i % NRING
        return self.ring[:, j, :], self.ring_slot[j]

    def ring_done(self):
        while self.ring_issued < min(len(self.ring_seq), self.ring_used + NRING):
            self.ring_issue()

    def load_win(self, l):
        for j in range(5):
            self.dma(self.POOL, self.winb[:, j, :, :].rearrange("p k c -> p (k c)"), self.win[l, j],
                     self.win_ds[j], writes=[self.win_slot[j]], max_dma_last_dim=8192)
        self.dma(self.POOL, self.pwb[:, :, :].rearrange("p g d -> p (g d)"), self.poolw[l], self.pw_ds,
                 writes=[self.pw_slot], max_dma_last_dim=8192)

    def rstd_from_psum(self, ps_ap, ps_slot, out_ap, out_slot, tmp_ap, tmp_slot, inv_n, fast=True):
        self.op(self.ACT, act(tmp_ap, ps_ap, AF.Ln, bias=EPS, scale=inv_n), reads=[ps_slot], writes=[tmp_slot])
        self.op(self.ACT, act(out_ap, tmp_ap, AF.Exp, scale=-0.5), reads=[tmp_slot], writes=[out_slot])

    def enter_arena(self, new_slots):
        self.alias_handoff(self.arena_slots, new_slots)
        self.arena_slots = new_slots

    def phase_init(self):
        PE, ACT, DVE, SP = self.PE, self.ACT, self.DVE, self.SP
        ds_c = self.dsem("d_const")
        ds_c2 = self.dsem("d_const2")
        ds_c3 = self.dsem("d_const3")
        self.dma(SP, self.cst, self.cstd, ds_c, writes=[self.cst_slot])
        ident = self.cst[:, C_ID:C_ID + 128]
        self.ident = ident
        self.op(DVE, vcp(self.identb, ident), reads=[self.cst_slot], writes=[self.const2_slot])
        self.op(DVE, mset(self.onesb, 1.0), writes=[self.const2_slot])
        self.op(DVE, mset(self.zerob.rearrange("p h d -> p (h d)"), 0.0), writes=[self.const2_slot])
        self.dma(self.POOL, self.bandb, self.bandsd, self.dsem("d_band"), writes=[self.band_slot], max_dma_last_dim=4096)
        if self.nl > 0:
            self.load_win(0)
        self.off = self.arena0
        st = [self.carve([D], F32) for _ in range(2)]
        st_slot = [Slot("st0"), Slot("st1")]
        st_ds = [self.dsem("d_st0"), self.dsem("d_st1")]
        vst = self.carve([3, 128], F32)
        vst_slot = Slot("vst")
        c17 = self.carve([D], F32)
        c17_slot = Slot("c17")
        sc17 = self.carve([D], F32)
        sc17_slot = Slot("sc17")
        self.arena_slots = st_slot + [vst_slot, c17_slot, sc17_slot]
        self.dma(SP, vst, self.vecs.rearrange("(t p) c -> p t c", p=128), ds_c2, writes=[vst_slot])
        pb, pslot = self.pbank()
        self.op(PE, seq([tr(pb[:, t * 128:(t + 1) * 128], vst[:, t, :], ident) for t in range(3)]),
                reads=[vst_slot, self.cst_slot], writes=[pslot])
        self.op(ACT, acp(self.vecT, pb[:, 0:384]), reads=[pslot], writes=[self.vec_slot])
        n1 = NS + 1
        self.dma(SP, c17[0:n1, :], self.cc, ds_c3, writes=[c17_slot])
        self.op(ACT, act(sc17[0:n1, :], c17[0:n1, :], AF.Silu), reads=[c17_slot], writes=[sc17_slot])
        pb2, pslot2 = self.pbank()
        self.op(PE, seq([tr(pb2[:, k * n1:(k + 1) * n1], sc17[0:n1, k * 128:(k + 1) * 128], ident[0:n1, 0:n1]) for k in range(KC)]),
                reads=[sc17_slot, self.cst_slot], writes=[pslot2])
        self.op(ACT, acp(self.scT.rearrange("p k n -> p (k n)"), pb2[:, 0:KC * n1]), reads=[pslot2], writes=[self.sc_slot])
        self.dma(SP, st[0][0:NS, :], self.xs, st_ds[0], writes=[st_slot[0]])
        pb3, pslot3 = self.pbank()
        self.op(PE, seq([tr(pb3[:, k * NS:(k + 1) * NS], st[0][0:NS, k * 128:(k + 1) * 128], ident[0:NS, 0:NS]) for k in range(KC)]),
                reads=[st_slot[0], self.cst_slot], writes=[pslot3])
        self.op(ACT, acp(self.xsT.rearrange("p k n -> p (k n)"), pb3[:, 0:KC * NS]), reads=[pslot3], writes=[self.xs_slot])
        for t in range(T // 128):
            b = (t + 1) % 2
            self.dma(SP, st[b], self.xp[t * 128:(t + 1) * 128, :], st_ds[b], writes=[st_slot[b]])
            blk = (t * 128) // BT
            for half in range(2):
                pbx, pslx = self.pbank()
                self.op(PE, seq([tr(pbx[:, kk * 128:(kk + 1) * 128], st[b][:, (half * 4 + kk) * 128:(half * 4 + kk + 1) * 128], ident)
                                 for kk in range(4)]), reads=[st_slot[b], self.cst_slot], writes=[pslx])
                outv = self.xT[:, half * 4:half * 4 + 4, t * 128:(t + 1) * 128]
                inv = pbx.rearrange("p (k c) -> p k c", k=4)
                wr = [self.xslot[k][blk] for k in range(half * 4, half * 4 + 4)]
                if half == 0:
                    self.op(ACT, acp(outv, inv), reads=[pslx], writes=wr)
                else:
                    self.op(DVE, vcp(outv, inv), reads=[pslx], writes=wr)

    def stats_block(self, xviews, xslots, n, sq_bufs, sq_slots, rstd_ap, rstd_slot, tmp_ap, tmp_slot, pool=None):
        PE, ACT = self.PE, self.ACT
        pb, pslot = self.pbank(pool=pool)
        for k in range(KC):
            sq = sq_bufs[k % len(sq_bufs)]
            sqs = sq_slots[k % len(sq_bufs)]
            self.op(ACT, act(sq[:, 0:n], xviews[k], AF.Square), reads=xslots[k], writes=[sqs])
            self.op(PE, mm(pb[:, 0:n], self.onesb, sq[:, 0:n], start=(k == 0), stop=(k == KC - 1)),
                    reads=[sqs, self.const2_slot], writes=[pslot])
        self.rstd_from_psum(pb[:, 0:n], pslot, rstd_ap, rstd_slot, tmp_ap, tmp_slot, 1.0 / D)

    def xs_block(self, k, t0, n):
        return [self.xslot[k][i] for i in range(t0 // BT, (t0 + n + BT - 1) // BT)]

    def final_setup(self):
        if getattr(self, "fin", None) is not None:
            return
        save = self.off
        self.off = self.winb_off
        c = self.carve
        Fn = type("Fn", (), {})()
        sl = []
        def S(name):
            x = Slot(name)
            sl.append(x)
            return x
        Fn.fnbc = c([D], F32); Fn.fnbc_s = S("fn_bc")
        Fn.sqf = [c([KC, 128], BF16) for _ in range(2)]; Fn.sqf_s = [S("fn_sq0"), S("fn_sq1")]
        Fn.stg = [c([D], F32) for _ in range(2)]; Fn.stg_s = [S("fn_stg0"), S("fn_stg1")]
        Fn.lnt = [c([4], F32) for _ in range(2)]; Fn.lnt_s = [S("fn_ln0"), S("fn_ln1")]
        Fn.rst = [c([4], F32) for _ in range(2)]; Fn.rst_s = [S("fn_rs0"), S("fn_rs1")]
        Fn.sqs = c([KC * NS], BF16); Fn.sqs_s = S("fn_sqs")
        Fn.rstd16 = c([NS], F32); Fn.rstd16_s = S("fn_rstd16")
        Fn.tmp16 = c([NS], F32); Fn.tmp16_s = S("fn_tmp16")
        Fn.ysn = c([KC, NS], F32); Fn.ysn_s = S("fn_ysn")
        Fn.so = [c([512], F32, parts=NS) for _ in range(2)]; Fn.so_s = [S("fn_so0"), S("fn_so1")]
        assert self.off <= self.winb_off + 40 * 1024
        self.off = save
        self.alias_handoff(self.win_slot, sl)
        self.fin = Fn
        self.dma(self.SP, Fn.fnbc, self.vecs[V_FN:V_FN + 8, :].rearrange("k c -> (k c)").partition_broadcast(128),
                 self.dsem("d_fnbc"), writes=[Fn.fnbc_s])

    def final_tiles(self, t_lo, t_hi):
        PE, ACT, DVE, SP = self.PE, self.ACT, self.DVE, self.SP
        self.phase = 'final'
        self.final_setup()
        Fn = self.fin
        ident = self.ident
        for t in range(t_lo, t_hi):
            r = t % 2
            blk = (t * 128) // BT
            tsl = slice(t * 128, (t + 1) * 128)
            xs_all = [self.xslot[k][blk] for k in range(KC)]
            self.op(ACT, act(Fn.sqf[r], self.xT[:, :, tsl], AF.Square), reads=xs_all, writes=[Fn.sqf_s[r]])
            pbs, pss = self.pbank()
            self.op(PE, seq([mm(pbs[:, 0:1], Fn.sqf[r][:, k, :], self.onesb[:, 0:1], start=(k == 0), stop=(k == KC - 1)) for k in range(KC)]),
                    reads=[Fn.sqf_s[r], self.const2_slot], writes=[pss])
            self.op(ACT, act(Fn.lnt[r][:, 0:1], pbs[:, 0:1], AF.Ln, bias=EPS, scale=1.0 / D), reads=[pss], writes=[Fn.lnt_s[r]])
            self.op(ACT, act(Fn.rst[r][:, 0:1], Fn.lnt[r][:, 0:1], AF.Exp, scale=-0.5), reads=[Fn.lnt_s[r]], writes=[Fn.rst_s[r]])
            for half in range(2):
                pbx, pslx = self.pbank()
                self.op(PE, seq([tr(pbx[:, kk * 128:(kk + 1) * 128], self.xT[:, half * 4 + kk, tsl], ident) for kk in range(4)]),
                        reads=[self.xslot[half * 4 + kk][blk] for kk in range(4)] + [self.cst_slot], writes=[pslx])
                self.op(DVE, stt(Fn.stg[r][:, half * 512:(half + 1) * 512], pbx, Fn.rst[r][:, 0:1], Fn.fnbc[:, half * 512:(half + 1) * 512],
                                 ALU.mult, ALU.mult), reads=[pslx, Fn.rst_s[r], Fn.fnbc_s], writes=[Fn.stg_s[r]])
            self.dma(SP, self.yp[t * 128:(t + 1) * 128, :], Fn.stg[r], self.stage_ds[r], reads=[Fn.stg_s[r]])

    def final_samples(self):
        PE, ACT, DVE, SP = self.PE, self.ACT, self.DVE, self.SP
        self.phase = 'final'
        self.final_setup()
        Fn = self.fin
        ident = self.ident
        pb, pslot = self.pbank()
        self.op(ACT, act(Fn.sqs, self.xsT.rearrange("p k n -> p (k n)"), AF.Square), reads=[self.xs_slot], writes=[Fn.sqs_s])
        self.op(PE, seq([mm(pb[:, 0:NS], self.onesb, Fn.sqs[:, k * NS:(k + 1) * NS], start=(k == 0), stop=(k == KC - 1)) for k in range(KC)]),
                reads=[Fn.sqs_s, self.const2_slot], writes=[pslot])
        self.rstd_from_psum(pb[:, 0:NS], pslot, Fn.rstd16, Fn.rstd16_s, Fn.tmp16, Fn.tmp16_s, 1.0 / D)
        ysn = Fn.ysn
        self.op(DVE, tt(ysn, self.xsT, Fn.rstd16.unsqueeze(1).broadcast_to([128, KC, NS]), ALU.mult),
                reads=[self.xs_slot, Fn.rstd16_s], writes=[Fn.ysn_s])
        self.op(DVE, tt(ysn, ysn, self.vecT[:, V_FN:V_FN + KC].unsqueeze(2).broadcast_to([128, KC, NS]), ALU.mult),
                reads=[self.vec_slot], writes=[Fn.ysn_s])
        pb2, pslot2 = self.pbank()
        pb3, pslot3 = self.pbank()
        self.op(PE, seq([tr((pb2 if k < 4 else pb3)[0:NS, (k % 4) * 128:(k % 4 + 1) * 128], ysn[:, k, :], ident) for k in range(KC)]),
                reads=[Fn.ysn_s, self.cst_slot], writes=[pslot2, pslot3])
        self.op(ACT, acp(Fn.so[0], pb2[0:NS, :]), reads=[pslot2], writes=[Fn.so_s[0]])
        self.op(ACT, acp(Fn.so[1], pb3[0:NS, :]), reads=[pslot3], writes=[Fn.so_s[1]])
        self.dma(SP, self.ys[:, 0:512], Fn.so[0], self.stage_ds[2], reads=[Fn.so_s[0]])
        self.dma(SP, self.ys[:, 512:1024], Fn.so[1], self.stage_ds[3], reads=[Fn.so_s[1]])

    def phase_final(self):
        if self.nl == 0:
            self.final_tiles(0, T // 128)
            self.final_samples()

    def alloc_layer_arenas(self):
        c = self.carve
        self.off = self.arena0
        A = type("A", (), {})()
        self.smx = A
        sl = []
        def S(name):
            s = Slot(name)
            s.rng = self.rng_q.pop(0) if self.rng_q else self.rng_last
            sl.append(s)
            return s
        self.rng_q = []
        A.sq = c([KC * NS], BF16); A.sq_s = S("s_sq")
        A.rstd = c([NS], F32); A.rstd_s = S("s_rstd")
        A.tmp16 = c([NS], F32); A.tmp16_s = S("s_tmp16")
        A.t1 = c([KC, NS], F32); A.t1_s = S("s_t1")
        A.hs = c([KC, NS], BF16); A.hs_s = S("s_hs")
        A.Rq = c([512], F32, parts=NS); A.Rq_s = S("s_Rq")
        A.Rk = c([512], F32, parts=NS); A.Rk_s = S("s_Rk")
        A.A = c([512], F32, parts=NS); A.A_s = S("s_A")
        A.B = c([512], F32, parts=NS); A.B_s = S("s_B")
        A.prod = c([512], F32, parts=NS); A.prod_s = S("s_prod")
        A.s = c([H], F32, parts=NS); A.s_s = S("s_s")
        A.inner = c([512], F32, parts=NS); A.inner_s = S("s_inner")
        A.qexp = c([H, NS, NS], F32); A.qexp_s = S("s_qexp")
        NST = 4
        A.NST = NST
        A.St = [c([8, 128], F32) for _ in range(NST)]
        A.St_s = [S(f"s_St{i}") for i in range(NST)]
        A.St_ds = [self.dsem(f"d_St{i}") for i in range(NST)]
        A.St_ods = [self.dsem(f"d_Sto{i}") for i in range(NST)]
        A.vexp = c([NS, 128], BF16, parts=NS); A.vexp_s = S("s_vexp")
        A.Rkb = c([512], BF16, parts=NS); A.Rkb_s = S("s_Rkb")
        A.o = c([512], F32, parts=NS); A.o_s = S("s_o")
        A.osq = c([512], F32, parts=NS); A.osq_s = S("s_osq")
        A.ssum = c([H], F32, parts=NS); A.ssum_s = S("s_ssum")
        A.tmp4 = c([H], F32, parts=NS); A.tmp4_s = S("s_tmp4")
        A.rstdo = c([H], F32, parts=NS); A.rstdo_s = S("s_rstdo")
        A.on = c([512], F32, parts=NS); A.on_s = S("s_on")
        A.sg = c([H * NS], F32); A.sg_s = S("s_sg")
        A.us = c([H * NS], F32); A.us_s = S("s_us")
        A.buf = c([2, 512], F32, parts=120); A.buf_s = S("s_buf")
        A.buf_ds = self.dsem("d_sbuf")
        A.tq = c([H * NS], F32); A.tq_s = S("s_tq")
        A.pTs = c([H, NS], BF16); A.pTs_s = S("s_pTs")
        A.ustage = c([512], F32, parts=NS); A.ustage_s = S("s_ustage")
        A.ustage_ds = self.dsem("d_ustage")
        assert not self.rng_q
        self.smx_slots = sl
        self.final_ds += A.St_ods + [A.ustage_ds]
        smx_end = self.off
        self.off = self.arena0
        self.rng_q = []
        M = type("M", (), {})()
        self.mx = M
        sl = []
        M.sq = [c([BT], BF16) for _ in range(2)]; M.sq_s = [S("m_sq0"), S("m_sq1")]
        M.rstd = c([BT], F32); M.rstd_s = S("m_rstd")
        M.tmpS = c([BT], F32); M.tmpS_s = S("m_tmpS")
        M.htmp = [c([BT], F32) for _ in range(2)]; M.htmp_s = [S("m_ht0"), S("m_ht1")]
        M.rope = [c([NTB, 256], F32) for _ in range(2)]; M.rope_s = [S("m_rope0"), S("m_rope1")]; M.rope_ds = [self.dsem("d_rope0"), self.dsem("d_rope1")]
        M.R = [c([512], F32) for _ in range(2)]; M.R_s = [S("m_Rq"), S("m_Rk")]
        M.A = c([512], F32); M.A_s = S("m_A")
        M.B = c([512], F32); M.B_s = S("m_B")
        M.v = c([NTB, 512], BF16); M.v_s = [S(f"m_v{t}") for t in range(NTB)]
        M.kd = c([NTB, 512], BF16); M.kd_s = [S(f"m_kd{t}") for t in range(NTB)]
        M.qd = [c([512], BF16) for _ in range(2)]; M.qd_s = [S("m_qd0"), S("m_qd1")]
        M.u = [c([512], BF16) for _ in range(3)]; M.u_s = [S("m_u0"), S("m_u1"), S("m_u2")]
        M.qkT = [c([1024], BF16) for _ in range(2)]; M.qkT_s = [S("m_qkT0"), S("m_qkT1")]
        M.sTm = [c([H, 128], BF16) for _ in range(2)]; M.sTm_s = [S("m_sTm0"), S("m_sTm1")]
        M.Stmp = c([H, 128], F32); M.Stmp_s = S("m_Stmp")
        M.osq = c([512], BF16); M.osq_s = S("m_osq")
        M.rstdo = c([512], F32); M.rstdo_s = S("m_rstdo")
        M.to = c([512], F32); M.to_s = S("m_to")
        M.sgt = c([BT], F32); M.sgt_s = S("m_sgt")
        M.sgg = []
        M.sgg_s = []
        for i in range(2):
            M.sgg.append(c([H, BT], BF16))
            M.sgg_s.append([S(f"m_sgg{i}_{h}") for h in range(H)])
        M.pTt = [c([H, 128], BF16) for _ in range(2)]; M.pTt_s = [S("m_pTt0"), S("m_pTt1")]
        M.mixT = c([KC, BT], BF16); M.mix_s = [[S(f"m_mix{k}_{t}") for t in range(NTB)] for k in range(KC)]
        M.pstage = M.A; M.pstage_s = M.A_s; M.pstage_ds = self.dsem("d_pstage")
        assert not self.rng_q
        self.mx_slots = sl
        self.final_ds.append(M.pstage_ds)
        mx_end = self.off
        self.off = self.arena0
        self.rng_q = []
        Fb = type("F", (), {})()
        self.ff = Fb
        sl = []
        Fb.sq = [c([512], BF16) for _ in range(2)]; Fb.sq_s = [S("f_sq0"), S("f_sq1")]
        Fb.rstd = [c([512], F32) for _ in range(2)]; Fb.rstd_s = [S("f_rstd0"), S("f_rstd1")]
        Fb.tmpS = c([512], F32); Fb.tmpS_s = S("f_tmpS")
        Fb.htmp = c([512], F32); Fb.htmp_s = S("f_ht")
        Fb.sa = [c([512], F32) for _ in range(2)]; Fb.sa_s = [S("f_sa0"), S("f_sa1")]
        Fb.gT = c([8, HALF], BF16); Fb.gT_s = [[S(f"f_g{f}_{tb}") for tb in range(2)] for f in range(8)]
        Fb.wd = c([4, 2048], BF16); Fb.wd_s = [S(f"f_wd{j}") for j in range(4)]
        Fb.wd_ds = [self.dsem(f"d_wd{j}") for j in range(4)]
        Fb.h2s = c([KC, NS], BF16); Fb.h2s_s = S("f_h2s")
        Fb.gTs = c([8, NS], BF16); Fb.gTs_s = S("f_gTs")
        Fb.sas = c([NS], F32); Fb.sas_s = S("f_sas")
        Fb.rstd16 = c([NS], F32); Fb.rstd16_s = S("f_rstd16")
        Fb.tmp16 = c([NS], F32); Fb.tmp16_s = S("f_tmp16")
        Fb.t1 = c([KC, NS], F32); Fb.t1_s = S("f_t1")
        Fb.sqs = c([KC * NS], BF16); Fb.sqs_s = S("f_sqs")
        Fb.modN = c([48, NS + 1], F32); Fb.modN_s = S("f_modN")
        assert not self.rng_q
        self.ff_slots = sl
        ff_end = self.off
        self.arena_peaks = (smx_end, mx_end, ff_end)

    def layer(self, l):
        if l == 0:
            self.alloc_layer_arenas()
        self.enter_arena(self.smx_slots)
        self.phase = f'L{l}.ada'
        self.ada(l)
        self.phase = f'L{l}.smx'
        self.sample_mixer(l)
        self.enter_arena(self.mx_slots)
        self.phase = f'L{l}.mix'
        tiles = [(b, t) for b in range(NBLK) for t in range(NTB)]
        self.blk_front(l, 0)
        self.blk_head(l, 0, 0)
        for i, (b, t) in enumerate(tiles):
            if i + 1 < len(tiles):
                nb, nt = tiles[i + 1]
                if nt == 0:
                    pass
                self.blk_head(l, nb, nt)
            self.blk_tail(l, b, t)
            if t == NTB - 1:
                self.blk_wout(l, b)
                if l == 0 and b == 0:
                    ph_ = self.phase
                    self.phase = 'L0.ada'
                    self.ada_mm(0, self.modT, self.mod_slots, parts=(1,))
                    self.ada_finish(0, 1)
                    self.phase = ph_
                if b + 2 < NBLK:
                    self.blk_front(l, b + 2)
            if i == 0 and NBLK > 1:
                self.blk_front(l, 1)
        if l + 1 < self.nl:
            self.alias_handoff(self.win_slot, self.ada_slot)
        self.enter_arena(self.ff_slots)
        for hf in range(2):
            self.phase = f'L{l}.ffn{hf}'
            self.ffn_half(l, hf)
            if l == self.nl - 1:
                nt = HALF // 128
                self.final_tiles(hf * nt, (hf + 1) * nt)
                if hf == 0:
                    self.final_samples()
            if hf == 0 and l + 1 < self.nl:
                self.alias_handoff(self.ada_slot, self.win_slot)
                self.load_win(l + 1)
        if l + 1 < self.nl:
            self.op(self.DVE, vcp(self.modT.rearrange("p j n -> p (j n)"), self.ff.modN.rearrange("p j n -> p (j n)")),
                    reads=[self.ff.modN_s], writes=list(self.mod_slots))

    def ada_mm(self, l, dst, dst_slots, parts=(0, 1)):
        PE, DVE = self.PE, self.DVE
        n1 = NS + 1
        bada = self.vecT[:, V_BADA + l * 48:V_BADA + (l + 1) * 48]
        for part in parts:
            pb, ps = self.pbank()
            for j in range(12 * part, 12 * part + 12):
                sap, ss = self.ring_next(l, RING_ADA + j)
                w = sap.rearrange("p (k c) -> p k c", k=KC)
                fs = []
                for cc in range(2):
                    jc = 2 * j + cc
                    d_ = pb[:, (jc % 24) * n1:(jc % 24 + 1) * n1]
                    for k in range(KC):
                        fs.append(mm(d_, w[:, k, cc * 128:(cc + 1) * 128], self.scT[:, k, :], start=(k == 0), stop=(k == KC - 1)))
                self.op(PE, seq(fs), reads=[ss, self.sc_slot], writes=[ps])
                self.ring_done()
            self.op(DVE, tt(dst[:, part * 24:(part + 1) * 24, :], pb[:, 0:24 * n1].rearrange("p (j n) -> p j n", j=24),
                            bada[:, part * 24:(part + 1) * 24].unsqueeze(2).broadcast_to([128, 24, n1]), ALU.add),
                    reads=[ps, self.vec_slot], writes=[dst_slots[part]])

    def ada_step_side(self, l, j, dst, dst_slot):
        PE, DVE, POOL = self.PE, self.DVE, self.POOL
        n1 = NS + 1
        i = j % 10
        self.dma(POOL, self.adar[:, i, :], self.wring[l, RING_ADA + j], self.ada_ds[i], writes=[self.ada_slot[i]], max_dma_last_dim=8192)
        w = self.adar[:, i, :].rearrange("p (k c) -> p k c", k=KC)
        pb, ps = self.pbank()
        fs = []
        for cc in range(2):
            for k in range(KC):
                fs.append(mm(pb[:, cc * n1:(cc + 1) * n1], w[:, k, cc * 128:(cc + 1) * 128], self.scT[:, k, :], start=(k == 0), stop=(k == KC - 1)))
        self.op(PE, seq(fs), reads=[self.ada_slot[i], self.sc_slot], writes=[ps])
        bada = self.vecT[:, V_BADA + l * 48 + 2 * j:V_BADA + l * 48 + 2 * j + 2]
        self.op(DVE, tt(dst[:, 2 * j:2 * j + 2, :], pb[:, 0:2 * n1].rearrange("p (j n) -> p j n", j=2),
                        bada.unsqueeze(2).broadcast_to([128, 2, n1]), ALU.add), reads=[ps, self.vec_slot], writes=[dst_slot])

    def ada_finish(self, l, part):
        DVE = self.DVE
        n1 = NS + 1
        m0 = lambda a: self.modT[:, a * 8:(a + 1) * 8, 0]
        nv = self.vecT[:, (V_NM if part == 0 else V_NF) + l * 8:(V_NM if part == 0 else V_NF) + (l + 1) * 8]
        pm = self.pm
        r0 = 3 * part
        fs = [vcp(pm[:, r0, :], m0(r0)), stt(pm[:, r0 + 1, :], m0(r0 + 1), 1.0, nv, ALU.add, ALU.mult), vcp(pm[:, r0 + 2, :], m0(r0 + 2))]
        self.op(DVE, seq(fs), reads=[self.mod_slots[part], self.vec_slot], writes=[self.pm_slots[part]])
        bc = lambda v: v.unsqueeze(2).broadcast_to([128, KC, NS])
        c0 = 8 + 24 * part
        self.op(DVE, stt(self.smG[:, part], self.modT[:, c0:c0 + 8, 1:n1], 1.0, bc(nv), ALU.add, ALU.mult),
                reads=[self.mod_slots[part], self.vec_slot], writes=[self.smG_slots[part]])

    def ada(self, l):
        if l == 0:
            self.ada_mm(0, self.modT, self.mod_slots, parts=(0,))
            self.ada_finish(0, 0)
        else:
            self.ada_finish(l, 0)
            self.ada_finish(l, 1)

    def sample_mixer(self, l):
        PE, ACT, DVE, SP = self.PE, self.ACT, self.DVE, self.SP
        A = self.smx
        n1 = NS + 1
        ident = self.ident
        cstS = self.cst_slot
        self.op(ACT, act(A.sq, self.xsT.rearrange("p k n -> p (k n)"), AF.Square), reads=[self.xs_slot], writes=[A.sq_s])
        pb, ps = self.pbank()
        self.op(PE, seq([mm(pb[:, 0:NS], self.onesb, A.sq[:, k * NS:(k + 1) * NS], start=(k == 0), stop=(k == KC - 1)) for k in range(KC)]),
                reads=[A.sq_s, self.const2_slot], writes=[ps])
        self.rstd_from_psum(pb[:, 0:NS], ps, A.rstd, A.rstd_s, A.tmp16, A.tmp16_s, 1.0 / D)
        self.op(DVE, tt(A.t1, self.xsT, A.rstd.unsqueeze(1).broadcast_to([128, KC, NS]), ALU.mult), reads=[self.xs_slot, A.rstd_s], writes=[A.t1_s])
        self.op(DVE, tt(A.t1, A.t1, self.smG[:, 0], ALU.mult), reads=[self.smG_slots[0]], writes=[A.t1_s])
        self.op(DVE, tt(A.hs, A.t1, self.modT[:, 0:8, 1:n1], ALU.add), reads=[A.t1_s, self.mod_slots[0]], writes=[A.hs_s])
        bq, bk, bv = self.pbank(), self.pbank(), self.pbank(pin=True)
        banks = [bq, bk, bv]
        fs = []
        for k in range(KC):
            for n in range(3):
                fs.append(mm(banks[n][0][0:NS, :], A.hs[:, k, :], self.winb[:, n, k, :], start=(k == 0), stop=(k == KC - 1)))
        self.op(PE, seq(fs), reads=[A.hs_s] + self.win_slot[0:3], writes=[b[1] for b in banks])
        pbg, psg = self.pbank(pin=True)
        fs = []
        for j in range(12, 20):
            for k in range(KC):
                fs.append(mm(pbg[:, (j - 12) * NS:(j - 11) * NS], self.winb[:, j // 4, k, (j % 4) * 128:(j % 4 + 1) * 128], A.hs[:, k, :],
                             start=(k == 0), stop=(k == KC - 1)))
        self.op(PE, seq(fs), reads=[A.hs_s, self.win_slot[3], self.win_slot[4]], writes=[psg])
        ropes = self.cst[0:NS, C_ROPES:C_ROPES + 512].rearrange("p (a d) -> p a d", a=4)
        h3 = lambda ap: ap.rearrange("p (h d) -> p h d", h=H)
        def rope_s(bank, ci, R, R_s):
            x3 = h3(bank[0][0:NS, :])
            self.op(DVE, tt(h3(A.A), x3, ropes[:, ci, :].unsqueeze(1).broadcast_to([NS, H, 128]), ALU.mult), reads=[bank[1], cstS], writes=[A.A_s])
            self.op(DVE, seq([tt(h3(A.B)[:, :, 0:64], x3[:, :, 64:128], ropes[:, ci + 1, 0:64].unsqueeze(1).broadcast_to([NS, H, 64]), ALU.mult),
                              tt(h3(A.B)[:, :, 64:128], x3[:, :, 0:64], ropes[:, ci + 1, 64:128].unsqueeze(1).broadcast_to([NS, H, 64]), ALU.mult)]),
                    reads=[bank[1], cstS], writes=[A.B_s])
            self.op(DVE, tt(R, A.A, A.B, ALU.add), reads=[A.A_s, A.B_s], writes=[R_s])
        rope_s(bq, 0, A.Rq, A.Rq_s)
        rope_s(bk, 2, A.Rk, A.Rk_s)
        self.op(DVE, tt(A.prod, A.Rq, A.Rk, ALU.mult), reads=[A.Rq_s, A.Rk_s], writes=[A.prod_s])
        self.op(DVE, tred(A.s, h3(A.prod), ALU.add), reads=[A.prod_s], writes=[A.s_s])
        self.op(DVE, tt(h3(A.inner), h3(bv[0][0:NS, :]), A.s.unsqueeze(2).broadcast_to([NS, H, 128]), ALU.mult), reads=[bv[1], A.s_s], writes=[A.inner_s])
        self.op(ACT, acp(A.Rkb, A.Rk), reads=[A.Rk_s], writes=[A.Rkb_s])
        pbt, pst = self.pbank()
        self.op(PE, seq([tr(pbt[:, h * NS:(h + 1) * NS], A.Rq[:, h * 128:(h + 1) * 128], ident[0:NS, 0:NS]) for h in range(H)]),
                reads=[A.Rq_s, cstS], writes=[pst])
        id16 = self.cst[:, C_ID16:C_ID16 + 256].rearrange("p (a b) -> p a b", a=NS)
        self.op(DVE, tt(A.qexp, pbt[:, 0:H * NS].rearrange("p (h b) -> p h b", h=H).unsqueeze(3).broadcast_to([128, H, NS, NS]),
                        id16.unsqueeze(1).broadcast_to([128, H, NS, NS]), ALU.mult), reads=[pst, cstS], writes=[A.qexp_s])
        pbc, psc = self.pbank(pin=True)
        vflat = A.vexp.rearrange("p b e -> p (b e)")
        it = 0
        for h in range(H):
            self.op(DVE, tt(A.vexp, bv[0][0:NS, h * 128:(h + 1) * 128].unsqueeze(1).broadcast_to([NS, NS, 128]),
                            ident[0:NS, 0:NS].unsqueeze(2).broadcast_to([NS, NS, 128]), ALU.mult), reads=[bv[1], cstS], writes=[A.vexp_s])
            for bh in range(2):
                r = it % A.NST
                it += 1
                St, St_s = A.St[r], A.St_s[r]
                self.dma(SP, St, self.sret[l, bh * 8:(bh + 1) * 8, h].rearrange("b d e -> d b e"), A.St_ds[r], writes=[St_s])
                self.op(PE, seq([mm(pbc[0:NS, h * 128:(h + 1) * 128], A.qexp[:, h, b, :], St[:, b - bh * 8, :], start=(b == 0), stop=(b == NS - 1))
                                 for b in range(bh * 8, bh * 8 + 8)]), reads=[A.qexp_s, St_s], writes=[psc])
                for q4 in range(2):
                    b0 = bh * 8 + q4 * 4
                    pbu, psu = self.pbank()
                    self.op(PE, mm(pbu, A.Rkb[:, h * 128:(h + 1) * 128], vflat[:, b0 * 128:(b0 + 4) * 128]), reads=[A.Rkb_s, A.vexp_s], writes=[psu])
                    self.op(DVE, stt(St[:, q4 * 4:(q4 + 1) * 4, :], St[:, q4 * 4:(q4 + 1) * 4, :], float(GAM[h]),
                                     pbu.rearrange("p (b e) -> p b e", b=4), ALU.mult, ALU.add), reads=[psu], writes=[St_s])
                self.dma(SP, self.rs[l, bh * 8:(bh + 1) * 8, h].rearrange("b d e -> d b e"), St, A.St_ods[r], reads=[St_s])
        self.unpin(bv[1])
        self.op(DVE, seq([stt(h3(A.o)[:, h, :], pbc[0:NS, h * 128:(h + 1) * 128], float(GAM[h]), h3(A.inner)[:, h, :], ALU.mult, ALU.add) for h in range(H)]),
                reads=[psc, A.inner_s], writes=[A.o_s])
        self.unpin(psc)
        self.op(DVE, tt(A.osq, A.o, A.o, ALU.mult), reads=[A.o_s], writes=[A.osq_s])
        self.op(DVE, tred(A.ssum, h3(A.osq), ALU.add), reads=[A.osq_s], writes=[A.ssum_s])
        self.rstd_from_psum(A.ssum, A.ssum_s, A.rstdo, A.rstdo_s, A.tmp4, A.tmp4_s, 1.0 / 128)
        self.op(DVE, tt(h3(A.on), h3(A.o), A.rstdo.unsqueeze(2).broadcast_to([NS, H, 128]), ALU.mult), reads=[A.o_s, A.rstdo_s], writes=[A.on_s])
        pbo, pso_ = self.pbank()
        self.op(PE, seq([tr(pbo[:, h * NS:(h + 1) * NS], A.on[:, h * 128:(h + 1) * 128], ident[0:NS, 0:NS]) for h in range(H)]),
                reads=[A.on_s, cstS], writes=[pso_])
        self.op(ACT, act(A.sg, pbg[:, 0:H * NS], AF.Silu), reads=[psg], writes=[A.sg_s])
        self.op(ACT, acp(A.us, pbg[:, H * NS:2 * H * NS]), reads=[psg], writes=[A.us_s])
        self.unpin(psg)
        gn = lambda h: self.vecT[:, V_GN + l * 4 + h:V_GN + l * 4 + h + 1]
        self.op(DVE, seq([stt(self.mixTs[:, h, :], pbo[:, h * NS:(h + 1) * NS], gn(h), A.sg[:, h * NS:(h + 1) * NS], ALU.mult, ALU.mult) for h in range(H)]),
                reads=[pso_, A.sg_s, self.vec_slot], writes=[self.mixTs_slot])
        self.dma(SP, A.buf, self.spool[l].rearrange("(t b) r c -> (b r) t c", t=2), A.buf_ds, writes=[A.buf_s])
        self.dma(SP, self.pso[l, :, 0:14, :], self.spool[l, :, 1:15, :], self.out_ds)
        sel = self.cst[0:120, C_SEL:C_SEL + 128].rearrange("p (t g b) -> p t g b", t=2, g=4)
        pbs, pss = self.pbank()
        self.op(PE, seq([mm(pbs[:, g * NS:(g + 1) * NS], A.buf[:, t, g * 128:(g + 1) * 128], sel[:, t, g, :], start=(t == 0), stop=(t == 1))
                         for g in range(4) for t in range(2)]), reads=[A.buf_s, cstS], writes=[pss])
        fs = []
        for g, w in enumerate(POOL_W):
            fs.append(ts(A.tq[:, g * NS:(g + 1) * NS], A.us[:, g * NS:(g + 1) * NS], float(1.0 / w - 1.0), None, ALU.mult))
        self.op(DVE, seq(fs), reads=[A.us_s], writes=[A.tq_s])
        self.op(DVE, seq([stt(A.pTs[:, g, :], pbs[:, g * NS:(g + 1) * NS], float(1.0 / w), A.tq[:, g * NS:(g + 1) * NS], ALU.mult, ALU.add)
                          for g, w in enumerate(POOL_W)]), reads=[pss, A.tq_s], writes=[A.pTs_s])
        pby, psy = self.pbank()
        self.op(PE, seq([mm(pby[:, g * NS:(g + 1) * NS], self.pwb[:, g, :], A.pTs[:, g, :]) for g in range(4)]),
                reads=[A.pTs_s, self.pw_slot], writes=[psy])
        pscale = self.vecT[:, V_PS + l * 4:V_PS + (l + 1) * 4]
        self.op(DVE, tt(self.mixTs[:, 4:8, :], pby[:, 0:4 * NS].rearrange("p (g n) -> p g n", g=4), pscale.unsqueeze(2).broadcast_to([128, 4, NS]), ALU.mult),
                reads=[psy, self.vec_slot], writes=[self.mixTs_slot])
        pbr, psr = self.pbank()
        self.op(PE, seq([tr(pbr[0:NS, g * 128:(g + 1) * 128], A.us[:, g * NS:(g + 1) * NS], ident) for g in range(4)]),
                reads=[A.us_s, cstS], writes=[psr])
        self.op(ACT, acp(A.ustage, pbr[0:NS, :]), reads=[psr], writes=[A.ustage_s])
        self.dma(SP, self.pso[l, :, 14, :], A.ustage, A.ustage_ds, reads=[A.ustage_s])

    def blk_front(self, l, b):
        PE, ACT, DVE, SP = self.PE, self.ACT, self.DVE, self.SP
        M = self.mx
        cstS = self.cst_slot
        t0 = b * BT
        hb = b % 2
        hT = lambda k: self.hbuf[:, k, hb * BT:(hb + 1) * BT]
        hS = lambda k: self.hslot[k][hb]
        pmv = lambda i, k: self.pm[:, i, k:k + 1]
        h3 = lambda ap: ap.rearrange("p (h d) -> p h d", h=H)
        hreads = [hS(k) for k in range(KC)]
        gn = lambda h: self.vecT[:, V_GN + l * 4 + h:V_GN + l * 4 + h + 1]
        sgg = M.sgg[b % 2]
        sgg_s = M.sgg_s[b % 2]
        rope = M.rope[b % 2]
        rope_s = M.rope_s[b % 2]
        self.stats_block([self.xT[:, k, t0:t0 + BT] for k in range(KC)], [[self.xslot[k][b]] for k in range(KC)],
                         BT, M.sq, M.sq_s, M.rstd, M.rstd_s, M.tmpS, M.tmpS_s, pool="m")
        for k in range(KC):
            r = k % 2
            self.op(DVE, stt(M.htmp[r], self.xT[:, k, t0:t0 + BT], pmv(1, k), M.rstd, ALU.mult, ALU.mult),
                    reads=[self.xslot[k][b], M.rstd_s, self.pm_slots[0]], writes=[M.htmp_s[r]])
            self.op(ACT, act(hT(k), M.htmp[r], AF.Identity, bias=pmv(0, k)), reads=[M.htmp_s[r], self.pm_slots[0]], writes=[hS(k)])
        self.dma(SP, rope, self.ropep[b * NTB:(b + 1) * NTB].rearrange("t p c -> p t c"), M.rope_ds[b % 2], writes=[rope_s])
        if b == 0:
            self.op(DVE, mset(self.S.rearrange("p h d -> p (h d)"), 0.0), writes=[self.S_slot])
            self.op(DVE, mset(self.Sb.rearrange("p h d -> p (h d)"), 0.0), writes=[self.Sb_slot])
        for j in range(12, 16):
            h = j - 12
            pb, ps = self.pbank(pool="m")
            self.op(PE, seq([mm(pb[:, 0:BT], self.winb[:, j // 4, k, (j % 4) * 128:(j % 4 + 1) * 128], hT(k), start=(k == 0), stop=(k == KC - 1))
                             for k in range(KC)]), reads=hreads + [self.win_slot[j // 4]], writes=[ps])
            self.op(ACT, act(M.sgt, pb[:, 0:BT], AF.Silu), reads=[ps], writes=[M.sgt_s])
            self.op(DVE, ts(sgg[:, h, :], M.sgt, gn(h), None, ALU.mult), reads=[M.sgt_s, self.vec_slot], writes=[sgg_s[h]])

    def blk_head(self, l, b, t):
        PE, ACT, DVE, SP = self.PE, self.ACT, self.DVE, self.SP
        M = self.mx
        cstS = self.cst_slot
        t0 = b * BT
        hb = b % 2
        hT = lambda k: self.hbuf[:, k, hb * BT:(hb + 1) * BT]
        hS = lambda k: self.hslot[k][hb]
        pmv = lambda i, k: self.pm[:, i, k:k + 1]
        h3 = lambda ap: ap.rearrange("p (h d) -> p h d", h=H)
        hreads = [hS(k) for k in range(KC)]
        gn = lambda h: self.vecT[:, V_GN + l * 4 + h:V_GN + l * 4 + h + 1]
        sgg = M.sgg[b % 2]
        sgg_s = M.sgg_s[b % 2]
        rope = M.rope[b % 2]
        rope_s = M.rope_s[b % 2]
        caus = self.cst[:, C_CAUS:C_CAUS + 128]
        bcur = self.bandb[:, 0:512].rearrange("p (g t) -> p g t", g=4)
        bcur0 = self.bandb[:, 512:1024].rearrange("p (g t) -> p g t", g=4)
        bprev = self.bandb[:, 1024:1088].rearrange("p (g t) -> p g t", g=4)
        pscale = lambda g: self.vecT[:, V_PS + l * 4 + g:V_PS + l * 4 + g + 1]
        wsl = [0, 1, 2, 4]
        c = b * NTB + t
        r2 = c % 2
        r3 = c % 3
        tc = slice(t * 128, (t + 1) * 128)
        banks = [self.fbank(i) for i in range(4)]
        bq, bk, bv, bu = banks
        fs = []
        for k in range(KC):
            for n in range(4):
                fs.append(mm(banks[n][0], self.hbuf[:, k, hb * BT + t * 128:hb * BT + (t + 1) * 128], self.winb[:, wsl[n], k, :],
                             start=(k == 0), stop=(k == KC - 1)))
        self.op(PE, seq(fs), reads=hreads + [self.win_slot[i] for i in wsl], writes=[x[1] for x in banks])
        self.op(ACT, acp(M.v[:, t, :], bv[0]), reads=[bv[1]], writes=[M.v_s[t]])
        self.op(ACT, acp(M.u[r3], bu[0]), reads=[bu[1]], writes=[M.u_s[r3]])
        if c == T // 128 - 1:
            self.op(ACT, acp(M.pstage[96:128, :], bu[0][96:128, :]), reads=[bu[1]], writes=[M.pstage_s])
            self.dma(SP, self.pp[l], M.pstage[113:128, :], M.pstage_ds, reads=[M.pstage_s])
        cos2 = rope[:, t, 0:128].unsqueeze(1).broadcast_to([128, H, 128])
        sinlo = rope[:, t, 128:192].unsqueeze(1).broadcast_to([128, H, 64])
        sinhi = rope[:, t, 192:256].unsqueeze(1).broadcast_to([128, H, 64])
        for qi, bank in enumerate((bq, bk)):
            x3 = h3(bank[0])
            self.op(DVE, tt(h3(M.A), x3, cos2, ALU.mult), reads=[bank[1], rope_s], writes=[M.A_s])
            self.op(DVE, seq([tt(h3(M.B)[:, :, 0:64], x3[:, :, 64:128], sinlo, ALU.mult),
                              tt(h3(M.B)[:, :, 64:128], x3[:, :, 0:64], sinhi, ALU.mult)]), reads=[bank[1], rope_s], writes=[M.B_s])
            self.op(DVE, tt(M.R[qi], M.A, M.B, ALU.add), reads=[M.A_s, M.B_s], writes=[M.R_s[qi]])
        dq3 = self.cst[:, C_DQTM:C_DQTM + 4].unsqueeze(2).broadcast_to([128, H, 128])
        dk3 = self.cst[:, C_DKTM:C_DKTM + 4].unsqueeze(2).broadcast_to([128, H, 128])
        fq_ = tt(h3(M.qd[r2]), h3(M.R[0]), dq3, ALU.mult)
        fq_.cost = 2000.0
        self.op(self.POOL, fq_, reads=[M.R_s[0], cstS], writes=[M.qd_s[r2]])
        fk_ = tt(h3(M.kd[:, t, :]), h3(M.R[1]), dk3, ALU.mult)
        fk_.cost = 2000.0
        self.op(self.POOL, fk_, reads=[M.R_s[1], cstS], writes=[M.kd_s[t]])

    def blk_tail(self, l, b, t):
        PE, ACT, DVE, SP = self.PE, self.ACT, self.DVE, self.SP
        M = self.mx
        cstS = self.cst_slot
        t0 = b * BT
        hb = b % 2
        hT = lambda k: self.hbuf[:, k, hb * BT:(hb + 1) * BT]
        hS = lambda k: self.hslot[k][hb]
        pmv = lambda i, k: self.pm[:, i, k:k + 1]
        h3 = lambda ap: ap.rearrange("p (h d) -> p h d", h=H)
        hreads = [hS(k) for k in range(KC)]
        gn = lambda h: self.vecT[:, V_GN + l * 4 + h:V_GN + l * 4 + h + 1]
        sgg = M.sgg[b % 2]
        sgg_s = M.sgg_s[b % 2]
        rope = M.rope[b % 2]
        rope_s = M.rope_s[b % 2]
        caus = self.cst[:, C_CAUS:C_CAUS + 128]
        bcur = self.bandb[:, 0:512].rearrange("p (g t) -> p g t", g=4)
        bcur0 = self.bandb[:, 512:1024].rearrange("p (g t) -> p g t", g=4)
        bprev = self.bandb[:, 1024:1088].rearrange("p (g t) -> p g t", g=4)
        pscale = lambda g: self.vecT[:, V_PS + l * 4 + g:V_PS + l * 4 + g + 1]
        wsl = [0, 1, 2, 4]
        c = b * NTB + t
        r2 = c % 2
        r3 = c % 3
        tc = slice(t * 128, (t + 1) * 128)
        pbt, pst = self.pbank(pool="m")
        pbt16 = pbt.bitcast(BF16)
        fs = [tr(pbt16[:, h * 128:(h + 1) * 128], M.qd[r2][:, h * 128:(h + 1) * 128], self.identb) for h in range(H)]
        fs += [tr(pbt16[:, 512 + h * 128:512 + (h + 1) * 128], M.kd[:, t, h * 128:(h + 1) * 128], self.identb) for h in range(H)]
        self.op(PE, seq(fs), reads=[M.qd_s[r2], M.kd_s[t], self.const2_slot], writes=[pst])
        self.op(ACT, acp(M.qkT[r2], pbt16), reads=[pst], writes=[M.qkT_s[r2]])
        qdT = M.qkT[r2][:, 0:512].rearrange("p (h d) -> p h d", h=H)
        kdT = M.qkT[r2][:, 512:1024].rearrange("p (h d) -> p h d", h=H)
        pbs, pss = self.pbank(pool="m")
        self.op(PE, seq([mm(pbs[:, h * 128:(h + 1) * 128], kdT[:, h, :], qdT[:, h, :]) for h in range(H)]),
                reads=[M.qkT_s[r2]], writes=[pss])
        sTm = M.sTm[r2]
        self.op(DVE, tt(sTm, h3(pbs), caus.unsqueeze(1).broadcast_to([128, H, 128]), ALU.mult), reads=[pss, cstS], writes=[M.sTm_s[r2]])
        pbo, pso_ = self.pbank(pool="m")
        fs = []
        for h in range(H):
            fs.append(mm(pbo[:, h * 128:(h + 1) * 128], M.v[:, t, h * 128:(h + 1) * 128], sTm[:, h, :], start=True, stop=False))
            fs.append(mm(pbo[:, h * 128:(h + 1) * 128], self.Sb[:, h, :], qdT[:, h, :], start=False, stop=True))
        self.op(PE, seq(fs), reads=[M.v_s[t], M.sTm_s[r2], self.Sb_slot, M.qkT_s[r2]], writes=[pso_])
        pbu, psu = self.pbank(pool="m")
        self.op(PE, seq([mm(pbu[:, h * 128:(h + 1) * 128], M.kd[:, t, h * 128:(h + 1) * 128], M.v[:, t, h * 128:(h + 1) * 128]) for h in range(H)]),
                reads=[M.kd_s[t], M.v_s[t]], writes=[psu])
        self.op(DVE, tt(M.Stmp, h3(pbu), self.S, ALU.add), reads=[psu, self.S_slot], writes=[M.Stmp_s])
        self.op(DVE, seq([ts(self.S[:, h, :], M.Stmp[:, h, :], float(GC[h]), None, ALU.mult) for h in range(H)]),
                reads=[M.Stmp_s], writes=[self.S_slot])
        self.op(ACT, acp(self.Sb.rearrange("p h d -> p (h d)"), self.S.rearrange("p h d -> p (h d)")), reads=[self.S_slot], writes=[self.Sb_slot])
        self.op(ACT, act(M.osq, pbo, AF.Square), reads=[pso_], writes=[M.osq_s])
        pbn, psn = self.pbank(pool="m")
        self.op(PE, mm(pbn, self.onesb, M.osq), reads=[M.osq_s, self.const2_slot], writes=[psn])
        self.rstd_from_psum(pbn, psn, M.rstdo, M.rstdo_s, M.to, M.to_s, 1.0 / 128)
        self.op(DVE, tt(M.to, pbo, M.rstdo, ALU.mult), reads=[pso_, M.rstdo_s], writes=[M.to_s])
        self.op(DVE, tt(M.mixT[:, 0:4, tc], h3(M.to), sgg[:, :, tc], ALU.mult),
                reads=[M.to_s] + sgg_s, writes=[M.mix_s[k][t] for k in range(4)])
        pbp, psp = self.pbank(pool="m")
        fs = []
        for g in range(4):
            gs = slice(g * 128, (g + 1) * 128)
            fs.append(mm(pbp[:, gs], M.u[r3][:, gs], (bcur0 if c == 0 else bcur)[:, g, :], start=True, stop=(c == 0)))
            if c > 0:
                fs.append(mm(pbp[:, g * 128:g * 128 + 16], M.u[(c - 1) % 3][64:128, gs], bprev[64:128, g, :], start=False, stop=True))
        self.op(PE, seq(fs), reads=[M.u_s[r3], self.band_slot] + ([M.u_s[(c - 1) % 3]] if c > 0 else []), writes=[psp])
        self.op(ACT, acp(M.pTt[r2].rearrange("p g t -> p (g t)"), pbp), reads=[psp], writes=[M.pTt_s[r2]])
        pby, psy = self.pbank(pool="m")
        self.op(PE, seq([mm(pby[:, g * 128:(g + 1) * 128], self.pwb[:, g, :], M.pTt[r2][:, g, :]) for g in range(4)]),
                reads=[M.pTt_s[r2], self.pw_slot], writes=[psy])
        ps4 = self.vecT[:, V_PS + l * 4:V_PS + (l + 1) * 4].unsqueeze(2).broadcast_to([128, 4, 128])
        self.op(DVE, tt(M.mixT[:, 4:8, tc], pby.rearrange("p (g t) -> p g t", g=4), ps4, ALU.mult),
                reads=[psy, self.vec_slot], writes=[M.mix_s[4 + g][t] for g in range(4)])

    def blk_wout(self, l, b):
        PE, ACT, DVE, SP = self.PE, self.ACT, self.DVE, self.SP
        M = self.mx
        cstS = self.cst_slot
        t0 = b * BT
        hb = b % 2
        hT = lambda k: self.hbuf[:, k, hb * BT:(hb + 1) * BT]
        hS = lambda k: self.hslot[k][hb]
        pmv = lambda i, k: self.pm[:, i, k:k + 1]
        h3 = lambda ap: ap.rearrange("p (h d) -> p h d", h=H)
        hreads = [hS(k) for k in range(KC)]
        gn = lambda h: self.vecT[:, V_GN + l * 4 + h:V_GN + l * 4 + h + 1]
        sgg = M.sgg[b % 2]
        sgg_s = M.sgg_s[b % 2]
        rope = M.rope[b % 2]
        rope_s = M.rope_s[b % 2]
        if b == NBLK - 1:
            self.dma(SP, self.rp[l].rearrange("h d e -> d h e"), self.S, self.rp_ds, reads=[self.S_slot])
        mreads = [M.mix_s[k][t] for k in range(KC) for t in range(NTB)]
        if b == 0:
            pbw, psw = self.pbank(pin=True, pool="m")
        for jq in range(4):
            sap, ss = self.ring_next(l, RING_WOUT + jq)
            w = sap.rearrange("p (k c) -> p k c", k=KC)
            for cc in range(2):
                cch = 2 * jq + cc
                pb, ps = self.pbank(pool="m")
                self.op(PE, seq([mm(pb[:, 0:BT], w[:, k, cc * 128:(cc + 1) * 128], M.mixT[:, k, :], start=(k == 0), stop=(k == KC - 1))
                                 for k in range(KC)]), reads=mreads + [ss], writes=[ps])
                xv = self.xT[:, cch, t0:t0 + BT]
                self.op(DVE, stt(xv, pb[:, 0:BT], pmv(2, cch), xv, ALU.mult, ALU.add), reads=[ps, self.pm_slots[0]], writes=[self.xslot[cch][b]])
                if b == 0:
                    self.op(PE, seq([mm(pbw[:, cch * NS:(cch + 1) * NS], w[:, k, cc * 128:(cc + 1) * 128], self.mixTs[:, k, :],
                                        start=(k == 0), stop=(k == KC - 1)) for k in range(KC)]), reads=[self.mixTs_slot, ss], writes=[psw])
            self.ring_done()
        if b == 0:
            n1 = NS + 1
            self.op(DVE, tt(self.tw, pbw[:, 0:KC * NS].rearrange("p (k n) -> p k n", k=KC), self.modT[:, 16:24, 1:n1], ALU.mult),
                    reads=[psw, self.mod_slots[0]], writes=[self.tw_slot])
            self.unpin(psw)
            self.op(DVE, tt(self.xsT, self.xsT, self.tw, ALU.add), reads=[self.tw_slot], writes=[self.xs_slot])

    def ffn_half(self, l, hf):
        PE, ACT, DVE, SP, POOL = self.PE, self.ACT, self.DVE, self.SP, self.POOL
        Fb = self.ff
        pmv = lambda i, k: self.pm[:, i, k:k + 1]
        n1 = NS + 1
        t0 = hf * HALF
        nbb = 512 // BT
        for tb in range(2):
            tt0 = t0 + tb * 512
            self.stats_block([self.xT[:, k, tt0:tt0 + 512] for k in range(KC)], [self.xs_block(k, tt0, 512) for k in range(KC)],
                             512, Fb.sq, Fb.sq_s, Fb.rstd[tb], Fb.rstd_s[tb], Fb.tmpS, Fb.tmpS_s)
        for tb in range(2):
            tt0 = t0 + tb * 512
            for k in range(KC):
                self.op(DVE, stt(Fb.htmp, self.xT[:, k, tt0:tt0 + 512], pmv(4, k), Fb.rstd[tb], ALU.mult, ALU.mult),
                        reads=self.xs_block(k, tt0, 512) + [Fb.rstd_s[tb], self.pm_slots[1]], writes=[Fb.htmp_s])
                self.op(ACT, act(self.hbuf[:, k, tb * 512:(tb + 1) * 512], Fb.htmp, AF.Identity, bias=pmv(3, k)),
                        reads=[Fb.htmp_s, self.pm_slots[1]], writes=[self.hslot[k][tb * nbb + i] for i in range(nbb)])
        hreads = [[self.hslot[k][tb * nbb + i] for k in range(KC) for i in range(nbb)] for tb in range(2)]
        samp = (hf == 0)
        if samp:
            self.op(ACT, act(Fb.sqs, self.xsT.rearrange("p k n -> p (k n)"), AF.Square), reads=[self.xs_slot], writes=[Fb.sqs_s])
            pb, ps = self.pbank()
            self.op(PE, seq([mm(pb[:, 0:NS], self.onesb, Fb.sqs[:, k * NS:(k + 1) * NS], start=(k == 0), stop=(k == KC - 1)) for k in range(KC)]),
                    reads=[Fb.sqs_s, self.const2_slot], writes=[ps])
            self.rstd_from_psum(pb[:, 0:NS], ps, Fb.rstd16, Fb.rstd16_s, Fb.tmp16, Fb.tmp16_s, 1.0 / D)
            self.op(DVE, tt(Fb.t1, self.xsT, Fb.rstd16.unsqueeze(1).broadcast_to([128, KC, NS]), ALU.mult), reads=[self.xs_slot, Fb.rstd16_s], writes=[Fb.t1_s])
            self.op(DVE, tt(Fb.t1, Fb.t1, self.smG[:, 1], ALU.mult), reads=[self.smG_slots[1]], writes=[Fb.t1_s])
            self.op(DVE, tt(Fb.h2s, Fb.t1, self.modT[:, 24:32, 1:n1], ALU.add), reads=[Fb.t1_s, self.mod_slots[1]], writes=[Fb.h2s_s])
        sai = 0
        for (f0, f1) in FGROUPS:
            nf = f1 - f0
            for jp in range(f0 // 2, f1 // 2):
                jl = jp - f0 // 2
                self.dma(POOL, Fb.wd[:, jl, :], self.wring[l, RING_DOWN + jp], Fb.wd_ds[jl], writes=[Fb.wd_s[jl]], max_dma_last_dim=8192)
            for f in range(f0, f1):
                fl = f - f0
                sap, ss = self.ring_next(l, RING_GU + f)
                w = sap.rearrange("p (k c) -> p k c", k=KC)
                pa = [self.pbank() for _ in range(2)]
                pbk = [self.pbank() for _ in range(2)]
                if samp:
                    pbsm, pssm = self.pbank()
                for tb in range(2):
                    for half_ab, banks in ((0, pa), (1, pbk)):
                        fs = []
                        for k in range(KC):
                            lw = w[:, k, half_ab * 128:(half_ab + 1) * 128]
                            fs.append(mm(banks[tb][0], lw, self.hbuf[:, k, tb * 512:(tb + 1) * 512], start=(k == 0), stop=(k == KC - 1)))
                        if samp and tb == 1:
                            for k in range(KC):
                                lw = w[:, k, half_ab * 128:(half_ab + 1) * 128]
                                fs.append(mm(pbsm[:, half_ab * NS:(half_ab + 1) * NS], lw, Fb.h2s[:, k, :], start=(k == 0), stop=(k == KC - 1)))
                        self.op(PE, seq(fs), reads=hreads[tb] + [ss] + ([Fb.h2s_s] if (samp and tb == 1) else []),
                                writes=[banks[tb][1]] + ([pssm] if (samp and tb == 1) else []))
                self.ring_done()
                if hf == 0 and l + 1 < self.nl:
                    ph_ = self.phase
                    self.phase = f'L{l + 1}.ada'
                    if f >= 8:
                        g_ = f - 8
                        js = [2 * g_, 2 * g_ + 1] if g_ < 10 else [20 + (g_ - 10)]
                        for j in js:
                            self.ada_step_side(l + 1, j, self.ff.modN, self.ff.modN_s)
                    self.phase = ph_
                for tb in range(2):
                    r = sai % 2
                    sai += 1
                    self.op(ACT, act(Fb.sa[r], pa[tb][0], AF.Silu), reads=[pa[tb][1]], writes=[Fb.sa_s[r]])
                    self.op(DVE, tt(Fb.gT[:, fl, tb * 512:(tb + 1) * 512], pbk[tb][0], Fb.sa[r], ALU.mult), reads=[pbk[tb][1], Fb.sa_s[r]],
                            writes=[Fb.gT_s[fl][tb]])
                if samp:
                    self.op(ACT, act(Fb.sas, pbsm[:, 0:NS], AF.Silu), reads=[pssm], writes=[Fb.sas_s])
                    self.op(DVE, tt(Fb.gTs[:, fl, :], pbsm[:, NS:2 * NS], Fb.sas, ALU.mult), reads=[pssm, Fb.sas_s], writes=[Fb.gTs_s])
            wdv = Fb.wd.rearrange("p j (i c) -> p j i c", i=2)
            if samp:
                pbd, psd = self.pbank(pin=True)
            for cch in range(KC):
                for tb in range(2):
                    pb, ps = self.pbank()
                    self.op(PE, seq([mm(pb, wdv[:, fl // 2, fl % 2, cch * 128:(cch + 1) * 128], Fb.gT[:, fl, tb * 512:(tb + 1) * 512],
                                        start=(fl == 0), stop=(fl == nf - 1)) for fl in range(nf)]),
                            reads=[Fb.gT_s[fl][tb] for fl in range(nf)] + [Fb.wd_s[j] for j in range(nf // 2)], writes=[ps])
                    tt0 = t0 + tb * 512
                    xv = self.xT[:, cch, tt0:tt0 + 512]
                    self.op(DVE, stt(xv, pb, pmv(5, cch), xv, ALU.mult, ALU.add), reads=[ps, self.pm_slots[1]], writes=self.xs_block(cch, tt0, 512))
                if samp:
                    self.op(PE, seq([mm(pbd[:, cch * NS:(cch + 1) * NS], wdv[:, fl // 2, fl % 2, cch * 128:(cch + 1) * 128], Fb.gTs[:, fl, :],
                                        start=(fl == 0), stop=(fl == nf - 1)) for fl in range(nf)]),
                            reads=[Fb.gTs_s] + [Fb.wd_s[j] for j in range(nf // 2)], writes=[psd])
            if samp:
                self.op(DVE, tt(self.tw, pbd[:, 0:KC * NS].rearrange("p (k n) -> p k n", k=KC), self.modT[:, 40:48, 1:n1], ALU.mult),
                        reads=[psd, self.mod_slots[1]], writes=[self.tw_slot])
                self.unpin(psd)
                self.op(DVE, tt(self.xsT, self.xsT, self.tw, ALU.add), reads=[self.tw_slot], writes=[self.xs_slot])


_CACHE = {}


def _get_nc(nl):
    if nl not in _CACHE:
        b1 = Builder(nl)
        b1.build(emit=False)
        assert b1.n_overcommit == 0, b1.n_overcommit
        b = Builder(nl, plan=b1.bank_plan, bank_seq=b1.bank_seq_out)
        _CACHE[nl] = b.build()
    return _CACHE[nl]


def run(inputs, nl=DEPTH):
    g = lambda k: np.asarray(inputs[k], dtype=np.float32)
    x_prompt, x_sample, c_prompt, c_sample = g("x_prompt"), g("x_sample"), g("c_prompt"), g("c_sample")
    state_ret, state_pool = g("state_ret"), g("state_pool")
    nle = max(nl, 1)
    cst, ropep, bands = _host_consts()
    win, wring, poolw = _host_weights(g("w_in"), g("w_out"), g("w_gu"), g("w_down"), g("w_ada"), g("pool_w"), nle)
    vecs = np.zeros((384, 128), np.float32)
    vecs[V_NM:V_NM + 32] = g("norm_mix").reshape(32, 128)
    vecs[V_NF:V_NF + 32] = g("norm_ffn").reshape(32, 128)
    vecs[V_BADA:V_BADA + 192] = g("b_ada").reshape(192, 128)
    vecs[V_GN:V_GN + 16] = g("ret_gn").reshape(16, 128)
    vecs[V_PS:V_PS + 16] = g("pool_scale").reshape(16, 128)
    vecs[V_FN:V_FN + 8] = g("final_norm").reshape(8, 128)
    in_maps = []
    for c in range(NCORES):
        sl = slice(c * NS, (c + 1) * NS)
        in_maps.append({
            "xp": np.ascontiguousarray(x_prompt[c]),
            "xs": np.ascontiguousarray(x_sample[sl, 0, :]),
            "cc": np.ascontiguousarray(np.concatenate([c_prompt[c:c + 1], c_sample[sl]], 0)),
            "sret": np.ascontiguousarray(state_ret[:nle, sl]),
            "spool": np.ascontiguousarray(state_pool[:nle, sl]),
            "win": win, "wring": wring, "poolw": poolw, "vecs": vecs, "cst": cst, "ropep": ropep, "bands": bands,
        })
    nc = _get_nc(nl)
    res = run_bass_kernel_spmd(nc, in_maps, core_ids=list(range(NCORES)))
    R = res.results
    y_prompt = np.stack([np.asarray(R[c]["yp"]) for c in range(NCORES)], 0).astype(np.float32)
    y_sample = np.concatenate([np.asarray(R[c]["ys"]) for c in range(NCORES)], 0).reshape(NCORES * NS, 1, D).astype(np.float32)
    new_ret_prompt = np.stack([np.asarray(R[c]["rp"]) for c in range(NCORES)], 1).astype(np.float32)
    new_pool_prompt = np.stack([np.asarray(R[c]["pp"]) for c in range(NCORES)], 1).astype(np.float32)
    new_ret_sample = np.concatenate([np.asarray(R[c]["rs"]) for c in range(NCORES)], 1).astype(np.float32)
    new_pool_sample = np.concatenate([np.asarray(R[c]["pso"]) for c in range(NCORES)], 1).astype(np.float32)
    return (y_prompt, y_sample, new_ret_prompt, new_pool_prompt, new_ret_sample, new_pool_sample)


def kernel(**inputs):
    return run(inputs, DEPTH)
```

```python
import numpy as np
from contextlib import ExitStack

import concourse.bass as bass
import concourse.mybir as mybir
from concourse.bass_utils import run_bass_kernel_spmd

F32 = mybir.dt.float32
BF16 = mybir.dt.bfloat16
ALU = mybir.AluOpType
AF = mybir.ActivationFunctionType

D = 1024
KC = 8
T = 2048
NS = 16
NCORES = 8
DEPTH = 4
H = 4
DK = 128
DIN = 2560
FF = 2816
FC = 22
EPS = 1e-6
BT = 256
NTB = BT // 128
NBLK = T // BT
HALF = 1024
FGROUPS = [(0, 8), (8, 16), (16, 22)]
POOL_W = (2, 4, 8, 16)
NRING = 4
RING_ADA, RING_WOUT, RING_GU, RING_DOWN = 0, 24, 28, 50
NRSLOT = 61
GAM = [1.0 - 2.0 ** (-5.0 - h) for h in range(H)]
GC = [g ** 128 for g in GAM]

C_ID, C_CAUS, C_DQTM, C_DKTM, C_ID16, C_SEL, C_ROPES = 0, 128, 256, 260, 264, 520, 648
C_TOT = 1160
NBAND = 1088
V_NM, V_NF, V_BADA, V_GN, V_PS, V_FN = 0, 32, 64, 256, 272, 288


def _host_consts():
    cst = np.zeros((128, C_TOT), np.float32)
    cst[:, C_ID:C_ID + 128] = np.eye(128, dtype=np.float32)
    j = np.arange(128)[:, None]
    i = np.arange(128)[None, :]
    cst[:, C_CAUS:C_CAUS + 128] = (i >= j).astype(np.float32)
    n = np.arange(128, dtype=np.float64)
    for h in range(H):
        lg = np.log(np.float64(np.float32(GAM[h])))
        cst[:, C_DQTM + h] = np.exp((n + 1.0) * lg)
        cst[:, C_DKTM + h] = (DK ** -0.5) * np.exp(-(n + 1.0) * lg)
    cst[:, C_ID16:C_ID16 + 256] = np.eye(16, dtype=np.float32).reshape(1, 256)
    sel = np.zeros((128, 2, 4, 16), np.float32)
    for t in range(2):
        for bl in range(8):
            for r in range(15):
                for g, w in enumerate(POOL_W):
                    if r >= 16 - w:
                        sel[bl * 15 + r, t, g, t * 8 + bl] = 1.0
    cst[:, C_SEL:C_SEL + 128] = sel.reshape(128, 128)
    inv = (1.0 / (np.float32(10000.0) ** (np.arange(0, 128, 2, dtype=np.float32) / np.float32(128)))).astype(np.float32)

    def tables(pos):
        ang = (np.asarray(pos, np.float32)[:, None] * inv[None, :]).astype(np.float32)
        c = np.cos(ang.astype(np.float64)).astype(np.float32)
        s = np.sin(ang.astype(np.float64)).astype(np.float32)
        return np.concatenate([c, c], -1), np.concatenate([-s, s], -1)

    c2, s2 = tables(np.array([16384.0], np.float32))
    sc = np.float32(DK ** -0.5)
    ropes = np.concatenate([c2, s2, c2 * sc, s2 * sc], -1)
    cst[:, C_ROPES:C_ROPES + 512] = ropes
    c2p, s2p = tables(np.arange(T, dtype=np.float32))
    ropep = np.concatenate([c2p, s2p], -1).reshape(T // 128, 128, 256).astype(np.float32)
    bands = np.zeros((128, NBAND), np.float32)
    sidx = np.arange(128)[:, None]
    tidx = np.arange(128)[None, :]
    for g, w in enumerate(POOL_W):
        cur = ((sidx <= tidx) & (sidx > tidx - w)).astype(np.float64) / w - (sidx == tidx)
        cnt = np.minimum(tidx + 1, w).astype(np.float64)
        cur0 = ((sidx <= tidx) & (sidx > tidx - w)).astype(np.float64) / cnt - (sidx == tidx)
        bands[:, g * 128:(g + 1) * 128] = cur
        bands[:, 512 + g * 128:512 + (g + 1) * 128] = cur0
        t16 = np.arange(16)[None, :]
        prev = ((sidx - 128) > (t16 - w)).astype(np.float64) / w
        bands[:, 1024 + g * 16:1024 + (g + 1) * 16] = prev
    return cst, ropep, bands


def _slot(w2d):
    r, c = w2d.shape
    k = r // 128
    return np.ascontiguousarray(w2d.reshape(k, 128, c).transpose(1, 0, 2)).reshape(128, k * c)


def _host_weights(w_in, w_out, w_gu, w_down, w_ada, pool_w, nl):
    win = np.empty((nl, 5, 128, 4096), np.float32)
    wring = np.empty((nl, NRSLOT, 128, 2048), np.float32)
    poolw = np.empty((nl, 128, 512), np.float32)
    for l in range(nl):
        for j in range(5):
            win[l, j] = _slot(w_in[l][:, j * 512:(j + 1) * 512])
        for j in range(24):
            wring[l, RING_ADA + j] = _slot(w_ada[l][:, j * 256:(j + 1) * 256])
        for j in range(4):
            wring[l, RING_WOUT + j] = _slot(w_out[l][:, j * 256:(j + 1) * 256])
        for f in range(FC):
            ab = np.concatenate([w_gu[l][:, f * 128:(f + 1) * 128],
                                 w_gu[l][:, FF + f * 128:FF + (f + 1) * 128]], axis=1)
            wring[l, RING_GU + f] = _slot(ab)
        for j in range(11):
            wring[l, RING_DOWN + j] = _slot(w_down[l][j * 256:(j + 1) * 256, :])
        poolw[l] = np.ascontiguousarray(pool_w[l].transpose(1, 0, 2)).reshape(128, 512)
    return win, wring, poolw


class Op:
    __slots__ = ("idx", "eng", "fn", "sem", "inc", "deps", "cost", "lat", "t0", "t1", "sig", "nun", "users", "ready", "tag", "vs", "phase")

    def __init__(self, idx, eng, fn, sem, inc, deps, cost, lat):
        self.idx = idx
        self.eng = eng
        self.fn = fn
        self.sem = sem
        self.inc = inc
        self.deps = deps
        self.cost = cost
        self.lat = lat
        self.sig = None


class Slot:
    __slots__ = ("name", "w", "r", "ops", "rng")

    def __init__(self, name=""):
        self.name = name
        self.w = None
        self.r = []
        self.ops = None
        self.rng = None


class Eng:
    def __init__(self, name, sem, is_pe=False):
        self.name = name
        self.sem = sem
        self.cnt = 0
        self.waited = {}
        self.prog = []
        self.is_pe = is_pe
        self.ops = []


class DSem:
    def __init__(self, sem):
        self.sem = sem
        self.cnt = 0


def _n(ap):
    r = 1
    for v in ap.shape[1:]:
        r *= v
    return r


def _c(f, cost):
    f.cost = cost
    return f


def tt(out, in0, in1, op):
    return _c(lambda h: h.tensor_tensor(out=out, in0=in0, in1=in1, op=op), 70 + 1.6 * _n(out))


def stt(out, in0, scalar, in1, op0, op1):
    return _c(lambda h: h.scalar_tensor_tensor(out=out, in0=in0, scalar=scalar, in1=in1, op0=op0, op1=op1), 70 + 1.6 * _n(out))


def ts(out, in0, s1, s2, op0, op1=None):
    if op1 is None:
        return _c(lambda h: h.tensor_scalar(out=out, in0=in0, scalar1=s1, scalar2=None, op0=op0), 70 + 1.05 * _n(out))
    return _c(lambda h: h.tensor_scalar(out=out, in0=in0, scalar1=s1, scalar2=s2, op0=op0, op1=op1), 70 + 1.05 * _n(out))


def vcp(out, in_):
    return _c(lambda h: h.tensor_copy(out=out, in_=in_), 70 + 1.05 * _n(out))


def acp(out, in_):
    return _c(lambda h: h.copy(out=out, in_=in_), 200 + 0.75 * _n(out))


def act(out, in_, func, bias=None, scale=None):
    kw = {}
    if bias is not None:
        kw["bias"] = bias
    if scale is not None:
        kw["scale"] = scale
    return _c(lambda h: h.activation(out=out, in_=in_, func=func, **kw), 220 + 0.75 * _n(out))


def mm(out, lhsT, rhs, start=True, stop=True):
    n = _n(rhs)
    c = 20 + max(n, 64) / 2.2
    if lhsT.dtype == F32:
        c *= 4
    return _c(lambda h: h.matmul(out, lhsT, rhs, start=start, stop=stop), c)


def tr(out, in_, identity):
    c = 70.0 * (4 if in_.dtype == F32 else 1)
    return _c(lambda h: h.transpose(out=out, in_=in_, identity=identity), c)


def seq(fs):
    fs = list(fs)

    def run(h):
        i = None
        for f in fs:
            i = f(h)
        return i
    run.cost = sum(f.cost for f in fs)
    return run


def recip(out, in_):
    return _c(lambda h: h.reciprocal(out=out, in_=in_), 70 + 8.4 * _n(out))


def rfast(out, in_):
    return _c(lambda h: h.reciprocal_approx_fast(out=out, in_=in_), 70 + 2.0 * _n(out))


def mset(ap, v):
    return _c(lambda h: h.memset(ap, v), 70 + 1.05 * _n(ap))


def tred(out, in_, op):
    return _c(lambda h: h.tensor_reduce(out=out, in_=in_, axis=mybir.AxisListType.X, op=op), 70 + 1.05 * _n(in_))


def _prod(s):
    r = 1
    for v in s:
        r *= v
    return r


class Builder:
    def __init__(self, nl, plan=None, bank_seq=None):
        self.nl = nl
        self.plan = plan
        self.bank_seq = bank_seq
        self.nalloc = 0
        self.valloc = []
        self.rng_q = []
        self.rng_last = None
        self.nc = bass.Bass("TRN2", target_bir_lowering=False)
        self.es = ExitStack()
        self.nsem = 0

    def sem(self, name):
        self.nsem += 1
        return self.es.enter_context(self.nc.semaphore(name))

    def dsem(self, name):
        return DSem(self.sem(name))

    def carve(self, shape, dtype, parts=128):
        esz = 4 if dtype == F32 else 2
        nb = _prod(shape) * esz
        nb_al = (nb + 31) // 32 * 32
        off = self.off
        self.off += nb_al
        self.rng_q.append((off, off + nb_al))
        self.rng_last = (off, off + nb_al)
        self.last_carve = (off, off + nb_al)
        self.peak = max(self.peak, self.off)
        assert self.off <= self.sb_bytes, f"SBUF overflow {self.off} > {self.sb_bytes}"
        w0 = off // 4
        ap = self.sb[0:parts, w0:w0 + nb_al // 4]
        if dtype != F32:
            ap = ap.bitcast(dtype)
        ap = ap[:, 0:_prod(shape)]
        if len(shape) == 2:
            ap = ap.rearrange("p (a b) -> p a b", a=shape[0])
        elif len(shape) == 3:
            ap = ap.rearrange("p (a b c) -> p a b c", a=shape[0], b=shape[1])
        return ap

    def _deps(self, reads, writes):
        deps = set()
        for s in reads:
            if s.w is not None:
                deps.add(s.w)
        for s in writes:
            if s.w is not None:
                deps.add(s.w)
            deps.update(s.r)
        return deps

    def _commit(self, o, reads, writes):
        for s in writes:
            s.w = o
            s.r = []
            if s.ops is not None:
                s.ops.append(o)
        for s in reads:
            s.r.append(o)
            if s.ops is not None:
                s.ops.append(o)

    def op(self, E, fn, reads=(), writes=()):
        import sys as _sys
        self.tag = 'L%d' % _sys._getframe(1).f_lineno
        o = Op(len(self.ops), E, fn, E, 1, self._deps(reads, writes), float(getattr(fn, "cost", 300.0)), 0.0)
        o.lat = o.cost + 60.0
        o.vs = [x for x in set(list(reads) + list(writes)) if x.ops is not None]
        o.phase = self.phase
        o.tag = getattr(fn, 'tag', '') or self.tag
        self.ops.append(o)
        E.ops.append(o)
        self._commit(o, reads, writes)
        return o

    def dma(self, Q, out, in_, ds, reads=(), writes=(), nbytes=None, **kw):
        if nbytes is None:
            nbytes = _n(out) * out.shape[0] * 4
        fn = (lambda h, o=out, i=in_, kw=kw: h.dma_start(out=o, in_=i, **kw))
        cost = 1200.0 if Q is self.POOL else 120.0
        o = Op(len(self.ops), Q, fn, ds, 16, self._deps(reads, writes), cost, 2200.0 + nbytes / 180.0)
        o.tag = 'dma:' + self.tag
        o.vs = []
        o.phase = self.phase
        self.ops.append(o)
        Q.ops.append(o)
        self._commit(o, reads, writes)
        return o

    def alias_handoff(self, old_slots, new_slots):
        for sn in new_slots:
            toks = set()
            for so in old_slots:
                if sn.rng is not None and so.rng is not None and (so.rng[1] <= sn.rng[0] or sn.rng[1] <= so.rng[0]):
                    continue
                if so.w is not None:
                    toks.add(so.w)
                toks.update(so.r)
            sn.r = list(sn.r) + list(toks)

    def schedule(self, window=96):
        engs = [self.PE, self.ACT, self.DVE, self.POOL, self.SP]
        if self.bank_seq is not None:
            for seqb in self.bank_seq:
                for x, y in zip(seqb[:-1], seqb[1:]):
                    xo = self.valloc[x].ops
                    for oy in self.valloc[y].ops:
                        oy.deps.update(xo)
        for o in self.ops:
            o.nun = len(o.deps)
            o.users = []
            o.ready = 0.0
            o.t0 = None
        for o in self.ops:
            for d in o.deps:
                d.users.append(o)
        pend = {E: list(E.ops) for E in engs}
        order = {E: [] for E in engs}
        etime = {E: 0.0 for E in engs}
        remaining = len(self.ops)
        INF = float("inf")
        bank_free = [0.0] * 8
        vbank = {}
        vleft = {}
        vend = {}
        for sl in self.valloc:
            vleft[id(sl)] = len(sl.ops)
            vend[id(sl)] = 0.0
        use_banks = self.plan is None
        self.n_overcommit = 0
        vorder = sorted([sl for sl in self.valloc if sl.ops], key=lambda sl: (sl.ops[0].idx, sl.name))
        vrank = {id(sl): i for i, sl in enumerate(vorder)}
        alloc_ptr = [0]
        done_ranks = set()
        bseq = [[] for _ in range(8)]
        vidx = {id(sl): i for i, sl in enumerate(self.valloc)}

        def pick(ignore_banks):
            best = None
            for E in engs:
                lst = pend[E]
                if not lst:
                    continue
                et = etime[E]
                cb = None
                for o in lst[:window]:
                    if o.nun:
                        continue
                    st = o.ready if o.ready > et else et
                    if use_banks and o.vs and not ignore_banks:
                        k = 0
                        mn = None
                        for v in o.vs:
                            if id(v) not in vbank:
                                k += 1
                                r_ = vrank[id(v)]
                                if mn is None or r_ < mn:
                                    mn = r_
                        if k and mn > alloc_ptr[0] + 40:
                            continue
                        if k:
                            bt = sorted(bank_free)[k - 1]
                            if bt > st:
                                st = bt
                    if st == INF:
                        continue
                    if cb is None or st < cb[0]:
                        cb = (st, o)
                        if st <= et:
                            break
                if cb is not None and (best is None or cb[0] < best[0]):
                    best = (cb[0], cb[1], E)
            return best

        while remaining:
            best = pick(False)
            if best is None:
                best = pick(True)
            assert best is not None, "scheduler stuck (cyclic deps?)"
            st, o, E = best
            o.t0 = st
            o.t1 = st + max(o.cost, o.lat)
            etime[E] = st + o.cost
            pend[E].remove(o)
            order[E].append(o)
            remaining -= 1
            if use_banks:
                for v in o.vs:
                    if id(v) not in vbank:
                        ok = [b for b in range(8) if bank_free[b] <= st]
                        if ok:
                            bsel = max(ok, key=lambda x: bank_free[x])
                        else:
                            bsel = min(range(8), key=lambda x: (bank_free[x] == INF, bank_free[x]))
                        if not ok:
                            self.n_overcommit += 1
                        vbank[id(v)] = bsel
                        bank_free[bsel] = INF
                        done_ranks.add(vrank[id(v)])
                        while alloc_ptr[0] in done_ranks:
                            alloc_ptr[0] += 1
                        bseq[bsel].append(vidx[id(v)])
                for v in o.vs:
                    vleft[id(v)] -= 1
                    if o.t1 > vend[id(v)]:
                        vend[id(v)] = o.t1
                    if vleft[id(v)] == 0:
                        bank_free[vbank[id(v)]] = vend[id(v)]
            for u in o.users:
                u.nun -= 1
                if o.t1 > u.ready:
                    u.ready = o.t1
        if use_banks:
            self.bank_plan = [vbank.get(id(sl), 0) for sl in self.valloc]
            self.bank_seq_out = bseq
        self.est_ns = max(etime.values())
        for E in engs:
            E.cnt = 0
        allops = sorted(self.ops, key=lambda o: (o.t0, o.idx))
        for o in allops:
            o.sem.cnt += o.inc
            o.sig = o.sem.cnt
        for E in engs:
            waited = {}
            for o in order[E]:
                best = {}
                for d in o.deps:
                    if E.is_pe and d.eng is E and d.sem is E:
                        continue
                    k = id(d.sem)
                    if k not in best or best[k].sig < d.sig:
                        best[k] = d
                for k, d in best.items():
                    if waited.get(k, 0) >= d.sig:
                        continue
                    waited[k] = d.sig
                    E.prog.append(("w", d.sem.sem, d.sig))
                E.prog.append(("o", o.fn, o.sem.sem, o.inc))

    def pbank(self, pin=False, pool=None):
        i = self.nalloc
        self.nalloc += 1
        b = (i % 8) if self.plan is None else self.plan[i]
        sl = Slot(f"vbank{i}")
        sl.ops = []
        self.valloc.append(sl)
        return self.ps[:, b, :], sl

    def fbank(self, b):
        return self.pbank()

    def color_banks(self):
        iv = []
        for i, sl in enumerate(self.valloc):
            if not sl.ops:
                iv.append((0.0, 0.0, i))
                continue
            iv.append((min(o.t0 for o in sl.ops), max(o.t1 for o in sl.ops), i))
        plan = [0] * len(iv)
        free = [0.0] * 8
        for st, en, i in sorted(iv):
            ok = [b for b in range(8) if free[b] <= st]
            if ok:
                b = max(ok, key=lambda x: free[x])
            else:
                b = min(range(8), key=lambda x: free[x])
            plan[i] = b
            free[b] = max(free[b], en)
        return plan

    def unpin(self, slot):
        pass

    def build(self, emit=True):
        nc = self.nc
        nl = self.nl
        es = self.es
        dt = lambda name, shape, kind: nc.dram_tensor(name, shape, F32, kind=kind).ap()
        I, O = "ExternalInput", "ExternalOutput"
        self.xp = dt("xp", [T, D], I)
        self.xs = dt("xs", [NS, D], I)
        self.cc = dt("cc", [NS + 1, D], I)
        self.sret = dt("sret", [max(nl, 1), NS, H, 128, 128], I)
        self.spool = dt("spool", [max(nl, 1), NS, 15, 512], I)
        self.win = dt("win", [max(nl, 1), 5, 128, 4096], I)
        self.wring = dt("wring", [max(nl, 1), NRSLOT, 128, 2048], I)
        self.poolw = dt("poolw", [max(nl, 1), 128, 512], I)
        self.vecs = dt("vecs", [384, 128], I)
        self.cstd = dt("cst", [128, C_TOT], I)
        self.ropep = dt("ropep", [T // 128, 128, 256], I)
        self.bandsd = dt("bands", [128, NBAND], I)
        self.yp = dt("yp", [T, D], O)
        self.ys = dt("ys", [NS, D], O)
        self.rp = dt("rp", [max(nl, 1), H, 128, 128], O)
        self.pp = dt("pp", [max(nl, 1), 15, 512], O)
        self.rs = dt("rs", [max(nl, 1), NS, H, 128, 128], O)
        self.pso = dt("pso", [max(nl, 1), NS, 15, 512], O)

        self.sb_bytes = 207 * 1024
        self.sbt = es.enter_context(nc.sbuf_tensor("sb", [128, self.sb_bytes // 4], F32))
        self.sb = self.sbt[:, :]
        self.off = 0
        self.peak = 0
        self.pst = es.enter_context(nc.psum_tensor("ps", [128, 8, 512], F32))
        self.ps = self.pst[:, :, :]
        self.pslot = [Slot(f"psum{b}") for b in range(8)]
        self.pb_next = 0
        self.pinned = set()
        self.pbm_next = 4

        self.ops = []
        self.tag = ''
        self.phase = 'init'
        self.PE = Eng("pe", self.sem("s_pe"), is_pe=True)
        self.ACT = Eng("act", self.sem("s_act"))
        self.DVE = Eng("dve", self.sem("s_dve"))
        self.POOL = Eng("pool", self.sem("s_pool"))
        self.SP = Eng("sp", self.sem("s_sp"))
        self.out_ds = self.dsem("d_out")
        self.rp_ds = self.dsem("d_rp")

        self.final_ds = []
        self.fin = None
        self.alloc_persistent()
        self.phase_init()
        for l in range(nl):
            self.layer(l)
        self.phase_final()
        self.schedule()
        if not emit:
            self.es.close()
            return None
        self.SP.prog.append(("w", self.out_ds.sem, self.out_ds.cnt))
        for ds in self.stage_ds + [self.rp_ds] + self.final_ds:
            if ds.cnt:
                self.SP.prog.append(("w", ds.sem, ds.cnt))

        def replay(E):
            def run(h):
                for it in E.prog:
                    if it[0] == "w":
                        h.wait_ge(it[1], it[2])
                    else:
                        ins = it[1](h)
                        ins.then_inc(it[2], it[3])
            return run

        with nc.Block() as block:
            block.tensor(replay(self.PE))
            block.scalar(replay(self.ACT))
            block.vector(replay(self.DVE))
            block.gpsimd(replay(self.POOL))
            block.sync(replay(self.SP))
        self.es.close()
        return nc

    def alloc_persistent(self):
        c = self.carve
        self.xT = c([KC, T], F32)
        self.xslot = [[Slot(f"x{k}_{b}") for b in range(NBLK)] for k in range(KC)]
        self.xsT = c([KC, NS], F32)
        self.xs_slot = Slot("xs")
        self.cst = c([C_TOT], F32)
        self.cst_slot = Slot("cst")
        self.bandb = c([NBAND], BF16)
        self.band_slot = Slot("bands")
        self.identb = c([128], BF16)
        self.onesb = c([128], BF16)
        self.zerob = c([H, 128], BF16)
        self.const2_slot = Slot("const2")
        self.vecT = c([384], F32)
        self.vec_slot = Slot("vecT")
        self.scT = c([KC, NS + 1], BF16)
        self.sc_slot = Slot("scT")
        self.modT = c([48, NS + 1], F32)
        self.mod_slots = [Slot("modT0"), Slot("modT1")]
        self.pm = c([6, KC], F32)
        self.pm_slots = [Slot("pm0"), Slot("pm1")]
        self.smG = c([2, KC, NS], F32)
        self.smG_slots = [Slot("smG0"), Slot("smG1")]
        self.hbuf = c([KC, HALF], BF16)
        self.hslot = [[Slot(f"h{k}_{i}") for i in range(HALF // BT)] for k in range(KC)]
        self.winb_off = self.off
        self.winb = c([5, KC, 512], BF16)
        self.win_slot = [Slot(f"win{j}") for j in range(5)]
        self.win_ds = [self.dsem(f"d_win{j}") for j in range(5)]
        self.adar = self.winb.rearrange("p j k c -> p (j k c)").rearrange("p (i x) -> p i x", i=10)
        self.ada_slot = [Slot(f"adar{i}") for i in range(10)]
        self.ada_ds = [self.dsem(f"d_adar{i}") for i in range(10)]
        self.pwb = c([H, 128], BF16)
        self.pw_slot = Slot("poolw")
        self.pw_ds = self.dsem("d_pw")
        self.ring = c([NRING, 2048], BF16)
        self.ring_slot = [Slot(f"ring{j}") for j in range(NRING)]
        self.ring_ds = [self.dsem(f"d_ring{j}") for j in range(NRING)]
        self.S = c([H, 128], F32)
        self.Sb = c([H, 128], BF16)
        self.S_slot = Slot("S")
        self.Sb_slot = Slot("Sb")
        self.stage_ds = [self.dsem(f"d_stage{i}") for i in range(4)]
        self.mixTs = c([KC, NS], BF16)
        self.mixTs_slot = Slot("mixTs")
        self.tw = c([KC, NS], F32)
        self.tw_slot = Slot("tw")
        self.arena0 = self.off
        self.arena_slots = []
        self.ring_seq = []
        for l in range(self.nl):
            if l == 0:
                self.ring_seq += [(l, RING_ADA + j) for j in range(12)]
            for b in range(NBLK):
                self.ring_seq += [(l, RING_WOUT + j) for j in range(4)]
                if l == 0 and b == 0:
                    self.ring_seq += [(l, RING_ADA + j) for j in range(12, 24)]
            for hf in range(2):
                for (f0, f1) in FGROUPS:
                    self.ring_seq += [(l, RING_GU + f) for f in range(f0, f1)]
        self.ring_issued = 0
        self.ring_used = 0

    def ring_issue(self):
        if self.ring_issued >= len(self.ring_seq):
            return
        i = self.ring_issued
        self.ring_issued += 1
        l, si = self.ring_seq[i]
        j = i % NRING
        self.dma(self.POOL, self.ring[:, j, :], self.wring[l, si], self.ring_ds[j],
                 writes=[self.ring_slot[j]], max_dma_last_dim=8192)

    def ring_next(self, l, si):
        i = self.ring_used
        assert self.ring_seq[i] == (l, si), (self.ring_seq[i], l, si)
        while self.ring_issued <= i:
            self.ring_issue()
        self.ring_used += 1
        j = i % NRING
        return self.ring[:, j, :], self.ring_slot[j]

    def ring_done(self):
        while self.ring_issued < min(len(self.ring_seq), self.ring_used + NRING):
            self.ring_issue()

    def load_win(self, l):
        for j in range(5):
            self.dma(self.POOL, self.winb[:, j, :, :].rearrange("p k c -> p (k c)"), self.win[l, j],
                     self.win_ds[j], writes=[self.win_slot[j]], max_dma_last_dim=8192)
        self.dma(self.POOL, self.pwb[:, :, :].rearrange("p g d -> p (g d)"), self.poolw[l], self.pw_ds,
                 writes=[self.pw_slot], max_dma_last_dim=8192)

    def rstd_from_psum(self, ps_ap, ps_slot, out_ap, out_slot, tmp_ap, tmp_slot, inv_n, fast=True):
        self.op(self.ACT, act(tmp_ap, ps_ap, AF.Ln, bias=EPS, scale=inv_n), reads=[ps_slot], writes=[tmp_slot])
        self.op(self.ACT, act(out_ap, tmp_ap, AF.Exp, scale=-0.5), reads=[tmp_slot], writes=[out_slot])

    def enter_arena(self, new_slots):
        self.alias_handoff(self.arena_slots, new_slots)
        self.arena_slots = new_slots

    def phase_init(self):
        PE, ACT, DVE, SP = self.PE, self.ACT, self.DVE, self.SP
        ds_c = self.dsem("d_const")
        ds_c2 = self.dsem("d_const2")
        ds_c3 = self.dsem("d_const3")
        self.dma(SP, self.cst, self.cstd, ds_c, writes=[self.cst_slot])
        ident = self.cst[:, C_ID:C_ID + 128]
        self.ident = ident
        self.op(DVE, vcp(self.identb, ident), reads=[self.cst_slot], writes=[self.const2_slot])
        self.op(DVE, mset(self.onesb, 1.0), writes=[self.const2_slot])
        self.op(DVE, mset(self.zerob.rearrange("p h d -> p (h d)"), 0.0), writes=[self.const2_slot])
        self.dma(self.POOL, self.bandb, self.bandsd, self.dsem("d_band"), writes=[self.band_slot], max_dma_last_dim=4096)
        if self.nl > 0:
            self.load_win(0)
        self.off = self.arena0
        st = [self.carve([D], F32) for _ in range(2)]
        st_slot = [Slot("st0"), Slot("st1")]
        st_ds = [self.dsem("d_st0"), self.dsem("d_st1")]
        vst = self.carve([3, 128], F32)
        vst_slot = Slot("vst")
        c17 = self.carve([D], F32)
        c17_slot = Slot("c17")
        sc17 = self.carve([D], F32)
        sc17_slot = Slot("sc17")
        self.arena_slots = st_slot + [vst_slot, c17_slot, sc17_slot]
        self.dma(SP, vst, self.vecs.rearrange("(t p) c -> p t c", p=128), ds_c2, writes=[vst_slot])
        pb, pslot = self.pbank()
        self.op(PE, seq([tr(pb[:, t * 128:(t + 1) * 128], vst[:, t, :], ident) for t in range(3)]),
                reads=[vst_slot, self.cst_slot], writes=[pslot])
        self.op(ACT, acp(self.vecT, pb[:, 0:384]), reads=[pslot], writes=[self.vec_slot])
        n1 = NS + 1
        self.dma(SP, c17[0:n1, :], self.cc, ds_c3, writes=[c17_slot])
        self.op(ACT, act(sc17[0:n1, :], c17[0:n1, :], AF.Silu), reads=[c17_slot], writes=[sc17_slot])
        pb2, pslot2 = self.pbank()
        self.op(PE, seq([tr(pb2[:, k * n1:(k + 1) * n1], sc17[0:n1, k * 128:(k + 1) * 128], ident[0:n1, 0:n1]) for k in range(KC)]),
                reads=[sc17_slot, self.cst_slot], writes=[pslot2])
        self.op(ACT, acp(self.scT.rearrange("p k n -> p (k n)"), pb2[:, 0:KC * n1]), reads=[pslot2], writes=[self.sc_slot])
        self.dma(SP, st[0][0:NS, :], self.xs, st_ds[0], writes=[st_slot[0]])
        pb3, pslot3 = self.pbank()
        self.op(PE, seq([tr(pb3[:, k * NS:(k + 1) * NS], st[0][0:NS, k * 128:(k + 1) * 128], ident[0:NS, 0:NS]) for k in range(KC)]),
                reads=[st_slot[0], self.cst_slot], writes=[pslot3])
        self.op(ACT, acp(self.xsT.rearrange("p k n -> p (k n)"), pb3[:, 0:KC * NS]), reads=[pslot3], writes=[self.xs_slot])
        for t in range(T // 128):
            b = (t + 1) % 2
            self.dma(SP, st[b], self.xp[t * 128:(t + 1) * 128, :], st_ds[b], writes=[st_slot[b]])
            blk = (t * 128) // BT
            for half in range(2):
                pbx, pslx = self.pbank()
                self.op(PE, seq([tr(pbx[:, kk * 128:(kk + 1) * 128], st[b][:, (half * 4 + kk) * 128:(half * 4 + kk + 1) * 128], ident)
                                 for kk in range(4)]), reads=[st_slot[b], self.cst_slot], writes=[pslx])
                outv = self.xT[:, half * 4:half * 4 + 4, t * 128:(t + 1) * 128]
                inv = pbx.rearrange("p (k c) -> p k c", k=4)
                wr = [self.xslot[k][blk] for k in range(half * 4, half * 4 + 4)]
                if half == 0:
                    self.op(ACT, acp(outv, inv), reads=[pslx], writes=wr)
                else:
                    self.op(DVE, vcp(outv, inv), reads=[pslx], writes=wr)

    def stats_block(self, xviews, xslots, n, sq_bufs, sq_slots, rstd_ap, rstd_slot, tmp_ap, tmp_slot, pool=None):
        PE, ACT = self.PE, self.ACT
        pb, pslot = self.pbank(pool=pool)
        for k in range(KC):
            sq = sq_bufs[k % len(sq_bufs)]
            sqs = sq_slots[k % len(sq_bufs)]
            self.op(ACT, act(sq[:, 0:n], xviews[k], AF.Square), reads=xslots[k], writes=[sqs])
            self.op(PE, mm(pb[:, 0:n], self.onesb, sq[:, 0:n], start=(k == 0), stop=(k == KC - 1)),
                    reads=[sqs, self.const2_slot], writes=[pslot])
        self.rstd_from_psum(pb[:, 0:n], pslot, rstd_ap, rstd_slot, tmp_ap, tmp_slot, 1.0 / D)

    def xs_block(self, k, t0, n):
        return [self.xslot[k][i] for i in range(t0 // BT, (t0 + n + BT - 1) // BT)]

    def final_setup(self):
        if getattr(self, "fin", None) is not None:
            return
        save = self.off
        self.off = self.winb_off
        c = self.carve
        Fn = type("Fn", (), {})()
        sl = []
        def S(name):
            x = Slot(name)
            sl.append(x)
            return x
        Fn.fnbc = c([D], F32); Fn.fnbc_s = S("fn_bc")
        Fn.sqf = [c([KC, 128], BF16) for _ in range(2)]; Fn.sqf_s = [S("fn_sq0"), S("fn_sq1")]
        Fn.stg = [c([D], F32) for _ in range(2)]; Fn.stg_s = [S("fn_stg0"), S("fn_stg1")]
        Fn.lnt = [c([4], F32) for _ in range(2)]; Fn.lnt_s = [S("fn_ln0"), S("fn_ln1")]
        Fn.rst = [c([4], F32) for _ in range(2)]; Fn.rst_s = [S("fn_rs0"), S("fn_rs1")]
        Fn.sqs = c([KC * NS], BF16); Fn.sqs_s = S("fn_sqs")
        Fn.rstd16 = c([NS], F32); Fn.rstd16_s = S("fn_rstd16")
        Fn.tmp16 = c([NS], F32); Fn.tmp16_s = S("fn_tmp16")
        Fn.ysn = c([KC, NS], F32); Fn.ysn_s = S("fn_ysn")
        Fn.so = [c([512], F32, parts=NS) for _ in range(2)]; Fn.so_s = [S("fn_so0"), S("fn_so1")]
        assert self.off <= self.winb_off + 40 * 1024
        self.off = save
        self.alias_handoff(self.win_slot, sl)
        self.fin = Fn
        self.dma(self.SP, Fn.fnbc, self.vecs[V_FN:V_FN + 8, :].rearrange("k c -> (k c)").partition_broadcast(128),
                 self.dsem("d_fnbc"), writes=[Fn.fnbc_s])

    def final_tiles(self, t_lo, t_hi):
        PE, ACT, DVE, SP = self.PE, self.ACT, self.DVE, self.SP
        self.phase = 'final'
        self.final_setup()
        Fn = self.fin
        ident = self.ident
        for t in range(t_lo, t_hi):
            r = t % 2
            blk = (t * 128) // BT
            tsl = slice(t * 128, (t + 1) * 128)
            xs_all = [self.xslot[k][blk] for k in range(KC)]
            self.op(ACT, act(Fn.sqf[r], self.xT[:, :, tsl], AF.Square), reads=xs_all, writes=[Fn.sqf_s[r]])
            pbs, pss = self.pbank()
            self.op(PE, seq([mm(pbs[:, 0:1], Fn.sqf[r][:, k, :], self.onesb[:, 0:1], start=(k == 0), stop=(k == KC - 1)) for k in range(KC)]),
                    reads=[Fn.sqf_s[r], self.const2_slot], writes=[pss])
            self.op(ACT, act(Fn.lnt[r][:, 0:1], pbs[:, 0:1], AF.Ln, bias=EPS, scale=1.0 / D), reads=[pss], writes=[Fn.lnt_s[r]])
            self.op(ACT, act(Fn.rst[r][:, 0:1], Fn.lnt[r][:, 0:1], AF.Exp, scale=-0.5), reads=[Fn.lnt_s[r]], writes=[Fn.rst_s[r]])
            for half in range(2):
                pbx, pslx = self.pbank()
                self.op(PE, seq([tr(pbx[:, kk * 128:(kk + 1) * 128], self.xT[:, half * 4 + kk, tsl], ident) for kk in range(4)]),
                        reads=[self.xslot[half * 4 + kk][blk] for kk in range(4)] + [self.cst_slot], writes=[pslx])
                self.op(DVE, stt(Fn.stg[r][:, half * 512:(half + 1) * 512], pbx, Fn.rst[r][:, 0:1], Fn.fnbc[:, half * 512:(half + 1) * 512],
                                 ALU.mult, ALU.mult), reads=[pslx, Fn.rst_s[r], Fn.fnbc_s], writes=[Fn.stg_s[r]])
            self.dma(SP, self.yp[t * 128:(t + 1) * 128, :], Fn.stg[r], self.stage_ds[r], reads=[Fn.stg_s[r]])

    def final_samples(self):
        PE, ACT, DVE, SP = self.PE, self.ACT, self.DVE, self.SP
        self.phase = 'final'
        self.final_setup()
        Fn = self.fin
        ident = self.ident
        pb, pslot = self.pbank()
        self.op(ACT, act(Fn.sqs, self.xsT.rearrange("p k n -> p (k n)"), AF.Square), reads=[self.xs_slot], writes=[Fn.sqs_s])
        self.op(PE, seq([mm(pb[:, 0:NS], self.onesb, Fn.sqs[:, k * NS:(k + 1) * NS], start=(k == 0), stop=(k == KC - 1)) for k in range(KC)]),
                reads=[Fn.sqs_s, self.const2_slot], writes=[pslot])
        self.rstd_from_psum(pb[:, 0:NS], pslot, Fn.rstd16, Fn.rstd16_s, Fn.tmp16, Fn.tmp16_s, 1.0 / D)
        ysn = Fn.ysn
        self.op(DVE, tt(ysn, self.xsT, Fn.rstd16.unsqueeze(1).broadcast_to([128, KC, NS]), ALU.mult),
                reads=[self.xs_slot, Fn.rstd16_s], writes=[Fn.ysn_s])
        self.op(DVE, tt(ysn, ysn, self.vecT[:, V_FN:V_FN + KC].unsqueeze(2).broadcast_to([128, KC, NS]), ALU.mult),
                reads=[self.vec_slot], writes=[Fn.ysn_s])
        pb2, pslot2 = self.pbank()
        pb3, pslot3 = self.pbank()
        self.op(PE, seq([tr((pb2 if k < 4 else pb3)[0:NS, (k % 4) * 128:(k % 4 + 1) * 128], ysn[:, k, :], ident) for k in range(KC)]),
                reads=[Fn.ysn_s, self.cst_slot], writes=[pslot2, pslot3])
        self.op(ACT, acp(Fn.so[0], pb2[0:NS, :]), reads=[pslot2], writes=[Fn.so_s[0]])
        self.op(ACT, acp(Fn.so[1], pb3[0:NS, :]), reads=[pslot3], writes=[Fn.so_s[1]])
        self.dma(SP, self.ys[:, 0:512], Fn.so[0], self.stage_ds[2], reads=[Fn.so_s[0]])
        self.dma(SP, self.ys[:, 512:1024], Fn.so[1], self.stage_ds[3], reads=[Fn.so_s[1]])

    def phase_final(self):
        if self.nl == 0:
            self.final_tiles(0, T // 128)
            self.final_samples()

    def alloc_layer_arenas(self):
        c = self.carve
        self.off = self.arena0
        A = type("A", (), {})()
        self.smx = A
        sl = []
        def S(name):
            s = Slot(name)
            s.rng = self.rng_q.pop(0) if self.rng_q else self.rng_last
            sl.append(s)
            return s
        self.rng_q = []
        A.sq = c([KC * NS], BF16); A.sq_s = S("s_sq")
        A.rstd = c([NS], F32); A.rstd_s = S("s_rstd")
        A.tmp16 = c([NS], F32); A.tmp16_s = S("s_tmp16")
        A.t1 = c([KC, NS], F32); A.t1_s = S("s_t1")
        A.hs = c([KC, NS], BF16); A.hs_s = S("s_hs")
        A.Rq = c([512], F32, parts=NS); A.Rq_s = S("s_Rq")
        A.Rk = c([512], F32, parts=NS); A.Rk_s = S("s_Rk")
        A.A = c([512], F32, parts=NS); A.A_s = S("s_A")
        A.B = c([512], F32, parts=NS); A.B_s = S("s_B")
        A.prod = c([512], F32, parts=NS); A.prod_s = S("s_prod")
        A.s = c([H], F32, parts=NS); A.s_s = S("s_s")
        A.inner = c([512], F32, parts=NS); A.inner_s = S("s_inner")
        A.qexp = c([H, NS, NS], F32); A.qexp_s = S("s_qexp")
        NST = 4
        A.NST = NST
        A.St = [c([8, 128], F32) for _ in range(NST)]
        A.St_s = [S(f"s_St{i}") for i in range(NST)]
        A.St_ds = [self.dsem(f"d_St{i}") for i in range(NST)]
        A.St_ods = [self.dsem(f"d_Sto{i}") for i in range(NST)]
        A.vexp = c([NS, 128], BF16, parts=NS); A.vexp_s = S("s_vexp")
        A.Rkb = c([512], BF16, parts=NS); A.Rkb_s = S("s_Rkb")
        A.o = c([512], F32, parts=NS); A.o_s = S("s_o")
        A.osq = c([512], F32, parts=NS); A.osq_s = S("s_osq")
        A.ssum = c([H], F32, parts=NS); A.ssum_s = S("s_ssum")
        A.tmp4 = c([H], F32, parts=NS); A.tmp4_s = S("s_tmp4")
        A.rstdo = c([H], F32, parts=NS); A.rstdo_s = S("s_rstdo")
        A.on = c([512], F32, parts=NS); A.on_s = S("s_on")
        A.sg = c([H * NS], F32); A.sg_s = S("s_sg")
        A.us = c([H * NS], F32); A.us_s = S("s_us")
        A.buf = c([2, 512], F32, parts=120); A.buf_s = S("s_buf")
        A.buf_ds = self.dsem("d_sbuf")
        A.tq = c([H * NS], F32); A.tq_s = S("s_tq")
        A.pTs = c([H, NS], BF16); A.pTs_s = S("s_pTs")
        A.ustage = c([512], F32, parts=NS); A.ustage_s = S("s_ustage")
        A.ustage_ds = self.dsem("d_ustage")
        assert not self.rng_q
        self.smx_slots = sl
        self.final_ds += A.St_ods + [A.ustage_ds]
        smx_end = self.off
        self.off = self.arena0
        self.rng_q = []
        M = type("M", (), {})()
        self.mx = M
        sl = []
        M.sq = [c([BT], BF16) for _ in range(2)]; M.sq_s = [S("m_sq0"), S("m_sq1")]
        M.rstd = c([BT], F32); M.rstd_s = S("m_rstd")
        M.tmpS = c([BT], F32); M.tmpS_s = S("m_tmpS")
        M.htmp = [c([BT], F32) for _ in range(2)]; M.htmp_s = [S("m_ht0"), S("m_ht1")]
        M.rope = [c([NTB, 256], F32) for _ in range(2)]; M.rope_s = [S("m_rope0"), S("m_rope1")]; M.rope_ds = [self.dsem("d_rope0"), self.dsem("d_rope1")]
        M.R = [c([512], F32) for _ in range(2)]; M.R_s = [S("m_Rq"), S("m_Rk")]
        M.A = c([512], F32); M.A_s = S("m_A")
        M.B = c([512], F32); M.B_s = S("m_B")
        M.v = c([NTB, 512], BF16); M.v_s = [S(f"m_v{t}") for t in range(NTB)]
        M.kd = c([NTB, 512], BF16); M.kd_s = [S(f"m_kd{t}") for t in range(NTB)]
        M.qd = [c([512], BF16) for _ in range(2)]; M.qd_s = [S("m_qd0"), S("m_qd1")]
        M.u = [c([512], BF16) for _ in range(3)]; M.u_s = [S("m_u0"), S("m_u1"), S("m_u2")]
        M.qkT = [c([1024], BF16) for _ in range(2)]; M.qkT_s = [S("m_qkT0"), S("m_qkT1")]
        M.sTm = [c([H, 128], BF16) for _ in range(2)]; M.sTm_s = [S("m_sTm0"), S("m_sTm1")]
        M.Stmp = c([H, 128], F32); M.Stmp_s = S("m_Stmp")
        M.osq = c([512], BF16); M.osq_s = S("m_osq")
        M.rstdo = c([512], F32); M.rstdo_s = S("m_rstdo")
        M.to = c([512], F32); M.to_s = S("m_to")
        M.sgt = c([BT], F32); M.sgt_s = S("m_sgt")
        M.sgg = []
        M.sgg_s = []
        for i in range(2):
            M.sgg.append(c([H, BT], BF16))
            M.sgg_s.append([S(f"m_sgg{i}_{h}") for h in range(H)])
        M.pTt = [c([H, 128], BF16) for _ in range(2)]; M.pTt_s = [S("m_pTt0"), S("m_pTt1")]
        M.mixT = c([KC, BT], BF16); M.mix_s = [[S(f"m_mix{k}_{t}") for t in range(NTB)] for k in range(KC)]
        M.pstage = M.A; M.pstage_s = M.A_s; M.pstage_ds = self.dsem("d_pstage")
        assert not self.rng_q
        self.mx_slots = sl
        self.final_ds.append(M.pstage_ds)
        mx_end = self.off
        self.off = self.arena0
        self.rng_q = []
        Fb = type("F", (), {})()
        self.ff = Fb
        sl = []
        Fb.sq = [c([512], BF16) for _ in range(2)]; Fb.sq_s = [S("f_sq0"), S("f_sq1")]
        Fb.rstd = [c([512], F32) for _ in range(2)]; Fb.rstd_s = [S("f_rstd0"), S("f_rstd1")]
        Fb.tmpS = c([512], F32); Fb.tmpS_s = S("f_tmpS")
        Fb.htmp = c([512], F32); Fb.htmp_s = S("f_ht")
        Fb.sa = [c([512], F32) for _ in range(2)]; Fb.sa_s = [S("f_sa0"), S("f_sa1")]
        Fb.gT = c([8, HALF], BF16); Fb.gT_s = [[S(f"f_g{f}_{tb}") for tb in range(2)] for f in range(8)]
        Fb.wd = c([4, 2048], BF16); Fb.wd_s = [S(f"f_wd{j}") for j in range(4)]
        Fb.wd_ds = [self.dsem(f"d_wd{j}") for j in range(4)]
        Fb.h2s = c([KC, NS], BF16); Fb.h2s_s = S("f_h2s")
        Fb.gTs = c([8, NS], BF16); Fb.gTs_s = S("f_gTs")
        Fb.sas = c([NS], F32); Fb.sas_s = S("f_sas")
        Fb.rstd16 = c([NS], F32); Fb.rstd16_s = S("f_rstd16")
        Fb.tmp16 = c([NS], F32); Fb.tmp16_s = S("f_tmp16")
        Fb.t1 = c([KC, NS], F32); Fb.t1_s = S("f_t1")
        Fb.sqs = c([KC * NS], BF16); Fb.sqs_s = S("f_sqs")
        Fb.modN = c([48, NS + 1], F32); Fb.modN_s = S("f_modN")
        assert not self.rng_q
        self.ff_slots = sl
        ff_end = self.off
        self.arena_peaks = (smx_end, mx_end, ff_end)

    def layer(self, l):
        if l == 0:
            self.alloc_layer_arenas()
        self.enter_arena(self.smx_slots)
        self.phase = f'L{l}.ada'
        self.ada(l)
        self.phase = f'L{l}.smx'
        self.sample_mixer(l)
        self.enter_arena(self.mx_slots)
        self.phase = f'L{l}.mix'
        tiles = [(b, t) for b in range(NBLK) for t in range(NTB)]
        self.blk_front(l, 0)
        self.blk_head(l, 0, 0)
        for i, (b, t) in enumerate(tiles):
            if i + 1 < len(tiles):
                nb, nt = tiles[i + 1]
                if nt == 0:
                    pass
                self.blk_head(l, nb, nt)
            self.blk_tail(l, b, t)
            if t == NTB - 1:
                self.blk_wout(l, b)
                if l == 0 and b == 0:
                    ph_ = self.phase
                    self.phase = 'L0.ada'
                    self.ada_mm(0, self.modT, self.mod_slots, parts=(1,))
                    self.ada_finish(0, 1)
                    self.phase = ph_
                if b + 2 < NBLK:
                    self.blk_front(l, b + 2)
            if i == 0 and NBLK > 1:
                self.blk_front(l, 1)
        if l + 1 < self.nl:
            self.alias_handoff(self.win_slot, self.ada_slot)
        self.enter_arena(self.ff_slots)
        for hf in range(2):
            self.phase = f'L{l}.ffn{hf}'
            self.ffn_half(l, hf)
            if l == self.nl - 1:
                nt = HALF // 128
                self.final_tiles(hf * nt, (hf + 1) * nt)
                if hf == 0:
                    self.final_samples()
            if hf == 0 and l + 1 < self.nl:
                self.alias_handoff(self.ada_slot, self.win_slot)
                self.load_win(l + 1)
        if l + 1 < self.nl:
            self.op(self.DVE, vcp(self.modT.rearrange("p j n -> p (j n)"), self.ff.modN.rearrange("p j n -> p (j n)")),
                    reads=[self.ff.modN_s], writes=list(self.mod_slots))

    def ada_mm(self, l, dst, dst_slots, parts=(0, 1)):
        PE, DVE = self.PE, self.DVE
        n1 = NS + 1
        bada = self.vecT[:, V_BADA + l * 48:V_BADA + (l + 1) * 48]
        for part in parts:
            pb, ps = self.pbank()
            for j in range(12 * part, 12 * part + 12):
                sap, ss = self.ring_next(l, RING_ADA + j)
                w = sap.rearrange("p (k c) -> p k c", k=KC)
                fs = []
                for cc in range(2):
                    jc = 2 * j + cc
                    d_ = pb[:, (jc % 24) * n1:(jc % 24 + 1) * n1]
                    for k in range(KC):
                        fs.append(mm(d_, w[:, k, cc * 128:(cc + 1) * 128], self.scT[:, k, :], start=(k == 0), stop=(k == KC - 1)))
                self.op(PE, seq(fs), reads=[ss, self.sc_slot], writes=[ps])
                self.ring_done()
            self.op(DVE, tt(dst[:, part * 24:(part + 1) * 24, :], pb[:, 0:24 * n1].rearrange("p (j n) -> p j n", j=24),
                            bada[:, part * 24:(part + 1) * 24].unsqueeze(2).broadcast_to([128, 24, n1]), ALU.add),
                    reads=[ps, self.vec_slot], writes=[dst_slots[part]])

    def ada_step_side(self, l, j, dst, dst_slot):
        PE, DVE, POOL = self.PE, self.DVE, self.POOL
        n1 = NS + 1
        i = j % 10
        self.dma(POOL, self.adar[:, i, :], self.wring[l, RING_ADA + j], self.ada_ds[i], writes=[self.ada_slot[i]], max_dma_last_dim=8192)
        w = self.adar[:, i, :].rearrange("p (k c) -> p k c", k=KC)
        pb, ps = self.pbank()
        fs = []
        for cc in range(2):
            for k in range(KC):
                fs.append(mm(pb[:, cc * n1:(cc + 1) * n1], w[:, k, cc * 128:(cc + 1) * 128], self.scT[:, k, :], start=(k == 0), stop=(k == KC - 1)))
        self.op(PE, seq(fs), reads=[self.ada_slot[i], self.sc_slot], writes=[ps])
        bada = self.vecT[:, V_BADA + l * 48 + 2 * j:V_BADA + l * 48 + 2 * j + 2]
        self.op(DVE, tt(dst[:, 2 * j:2 * j + 2, :], pb[:, 0:2 * n1].rearrange("p (j n) -> p j n", j=2),
                        bada.unsqueeze(2).broadcast_to([128, 2, n1]), ALU.add), reads=[ps, self.vec_slot], writes=[dst_slot])

    def ada_finish(self, l, part):
        DVE = self.DVE
        n1 = NS + 1
        m0 = lambda a: self.modT[:, a * 8:(a + 1) * 8, 0]
        nv = self.vecT[:, (V_NM if part == 0 else V_NF) + l * 8:(V_NM if part == 0 else V_NF) + (l + 1) * 8]
        pm = self.pm
        r0 = 3 * part
        fs = [vcp(pm[:, r0, :], m0(r0)), stt(pm[:, r0 + 1, :], m0(r0 + 1), 1.0, nv, ALU.add, ALU.mult), vcp(pm[:, r0 + 2, :], m0(r0 + 2))]
        self.op(DVE, seq(fs), reads=[self.mod_slots[part], self.vec_slot], writes=[self.pm_slots[part]])
        bc = lambda v: v.unsqueeze(2).broadcast_to([128, KC, NS])
        c0 = 8 + 24 * part
        self.op(DVE, stt(self.smG[:, part], self.modT[:, c0:c0 + 8, 1:n1], 1.0, bc(nv), ALU.add, ALU.mult),
                reads=[self.mod_slots[part], self.vec_slot], writes=[self.smG_slots[part]])

    def ada(self, l):
        if l == 0:
            self.ada_mm(0, self.modT, self.mod_slots, parts=(0,))
            self.ada_finish(0, 0)
        else:
            self.ada_finish(l, 0)
            self.ada_finish(l, 1)

    def sample_mixer(self, l):
        PE, ACT, DVE, SP = self.PE, self.ACT, self.DVE, self.SP
        A = self.smx
        n1 = NS + 1
        ident = self.ident
        cstS = self.cst_slot
        self.op(ACT, act(A.sq, self.xsT.rearrange("p k n -> p (k n)"), AF.Square), reads=[self.xs_slot], writes=[A.sq_s])
        pb, ps = self.pbank()
        self.op(PE, seq([mm(pb[:, 0:NS], self.onesb, A.sq[:, k * NS:(k + 1) * NS], start=(k == 0), stop=(k == KC - 1)) for k in range(KC)]),
                reads=[A.sq_s, self.const2_slot], writes=[ps])
        self.rstd_from_psum(pb[:, 0:NS], ps, A.rstd, A.rstd_s, A.tmp16, A.tmp16_s, 1.0 / D)
        self.op(DVE, tt(A.t1, self.xsT, A.rstd.unsqueeze(1).broadcast_to([128, KC, NS]), ALU.mult), reads=[self.xs_slot, A.rstd_s], writes=[A.t1_s])
        self.op(DVE, tt(A.t1, A.t1, self.smG[:, 0], ALU.mult), reads=[self.smG_slots[0]], writes=[A.t1_s])
        self.op(DVE, tt(A.hs, A.t1, self.modT[:, 0:8, 1:n1], ALU.add), reads=[A.t1_s, self.mod_slots[0]], writes=[A.hs_s])
        bq, bk, bv = self.pbank(), self.pbank(), self.pbank(pin=True)
        banks = [bq, bk, bv]
        fs = []
        for k in range(KC):
            for n in range(3):
                fs.append(mm(banks[n][0][0:NS, :], A.hs[:, k, :], self.winb[:, n, k, :], start=(k == 0), stop=(k == KC - 1)))
        self.op(PE, seq(fs), reads=[A.hs_s] + self.win_slot[0:3], writes=[b[1] for b in banks])
        pbg, psg = self.pbank(pin=True)
        fs = []
        for j in range(12, 20):
            for k in range(KC):
                fs.append(mm(pbg[:, (j - 12) * NS:(j - 11) * NS], self.winb[:, j // 4, k, (j % 4) * 128:(j % 4 + 1) * 128], A.hs[:, k, :],
                             start=(k == 0), stop=(k == KC - 1)))
        self.op(PE, seq(fs), reads=[A.hs_s, self.win_slot[3], self.win_slot[4]], writes=[psg])
        ropes = self.cst[0:NS, C_ROPES:C_ROPES + 512].rearrange("p (a d) -> p a d", a=4)
        h3 = lambda ap: ap.rearrange("p (h d) -> p h d", h=H)
        def rope_s(bank, ci, R, R_s):
            x3 = h3(bank[0][0:NS, :])
            self.op(DVE, tt(h3(A.A), x3, ropes[:, ci, :].unsqueeze(1).broadcast_to([NS, H, 128]), ALU.mult), reads=[bank[1], cstS], writes=[A.A_s])
            self.op(DVE, seq([tt(h3(A.B)[:, :, 0:64], x3[:, :, 64:128], ropes[:, ci + 1, 0:64].unsqueeze(1).broadcast_to([NS, H, 64]), ALU.mult),
                              tt(h3(A.B)[:, :, 64:128], x3[:, :, 0:64], ropes[:, ci + 1, 64:128].unsqueeze(1).broadcast_to([NS, H, 64]), ALU.mult)]),
                    reads=[bank[1], cstS], writes=[A.B_s])
            self.op(DVE, tt(R, A.A, A.B, ALU.add), reads=[A.A_s, A.B_s], writes=[R_s])
        rope_s(bq, 0, A.Rq, A.Rq_s)
        rope_s(bk, 2, A.Rk, A.Rk_s)
        self.op(DVE, tt(A.prod, A.Rq, A.Rk, ALU.mult), reads=[A.Rq_s, A.Rk_s], writes=[A.prod_s])
        self.op(DVE, tred(A.s, h3(A.prod), ALU.add), reads=[A.prod_s], writes=[A.s_s])
        self.op(DVE, tt(h3(A.inner), h3(bv[0][0:NS, :]), A.s.unsqueeze(2).broadcast_to([NS, H, 128]), ALU.mult), reads=[bv[1], A.s_s], writes=[A.inner_s])
        self.op(ACT, acp(A.Rkb, A.Rk), reads=[A.Rk_s], writes=[A.Rkb_s])
        pbt, pst = self.pbank()
        self.op(PE, seq([tr(pbt[:, h * NS:(h + 1) * NS], A.Rq[:, h * 128:(h + 1) * 128], ident[0:NS, 0:NS]) for h in range(H)]),
                reads=[A.Rq_s, cstS], writes=[pst])
        id16 = self.cst[:, C_ID16:C_ID16 + 256].rearrange("p (a b) -> p a b", a=NS)
        self.op(DVE, tt(A.qexp, pbt[:, 0:H * NS].rearrange("p (h b) -> p h b", h=H).unsqueeze(3).broadcast_to([128, H, NS, NS]),
                        id16.unsqueeze(1).broadcast_to([128, H, NS, NS]), ALU.mult), reads=[pst, cstS], writes=[A.qexp_s])
        pbc, psc = self.pbank(pin=True)
        vflat = A.vexp.rearrange("p b e -> p (b e)")
        it = 0
        for h in range(H):
            self.op(DVE, tt(A.vexp, bv[0][0:NS, h * 128:(h + 1) * 128].unsqueeze(1).broadcast_to([NS, NS, 128]),
                            ident[0:NS, 0:NS].unsqueeze(2).broadcast_to([NS, NS, 128]), ALU.mult), reads=[bv[1], cstS], writes=[A.vexp_s])
            for bh in range(2):
                r = it % A.NST
                it += 1
                St, St_s = A.St[r], A.St_s[r]
                self.dma(SP, St, self.sret[l, bh * 8:(bh + 1) * 8, h].rearrange("b d e -> d b e"), A.St_ds[r], writes=[St_s])
                self.op(PE, seq([mm(pbc[0:NS, h * 128:(h + 1) * 128], A.qexp[:, h, b, :], St[:, b - bh * 8, :], start=(b == 0), stop=(b == NS - 1))
                                 for b in range(bh * 8, bh * 8 + 8)]), reads=[A.qexp_s, St_s], writes=[psc])
                for q4 in range(2):
                    b0 = bh * 8 + q4 * 4
                    pbu, psu = self.pbank()
                    self.op(PE, mm(pbu, A.Rkb[:, h * 128:(h + 1) * 128], vflat[:, b0 * 128:(b0 + 4) * 128]), reads=[A.Rkb_s, A.vexp_s], writes=[psu])
                    self.op(DVE, stt(St[:, q4 * 4:(q4 + 1) * 4, :], St[:, q4 * 4:(q4 + 1) * 4, :], float(GAM[h]),
                                     pbu.rearrange("p (b e) -> p b e", b=4), ALU.mult, ALU.add), reads=[psu], writes=[St_s])
                self.dma(SP, self.rs[l, bh * 8:(bh + 1) * 8, h].rearrange("b d e -> d b e"), St, A.St_ods[r], reads=[St_s])
        self.unpin(bv[1])
        self.op(DVE, seq([stt(h3(A.o)[:, h, :], pbc[0:NS, h * 128:(h + 1) * 128], float(GAM[h]), h3(A.inner)[:, h, :], ALU.mult, ALU.add) for h in range(H)]),
                reads=[psc, A.inner_s], writes=[A.o_s])
        self.unpin(psc)
        self.op(DVE, tt(A.osq, A.o, A.o, ALU.mult), reads=[A.o_s], writes=[A.osq_s])
        self.op(DVE, tred(A.ssum, h3(A.osq), ALU.add), reads=[A.osq_s], writes=[A.ssum_s])
        self.rstd_from_psum(A.ssum, A.ssum_s, A.rstdo, A.rstdo_s, A.tmp4, A.tmp4_s, 1.0 / 128)
        self.op(DVE, tt(h3(A.on), h3(A.o), A.rstdo.unsqueeze(2).broadcast_to([NS, H, 128]), ALU.mult), reads=[A.o_s, A.rstdo_s], writes=[A.on_s])
        pbo, pso_ = self.pbank()
        self.op(PE, seq([tr(pbo[:, h * NS:(h + 1) * NS], A.on[:, h * 128:(h + 1) * 128], ident[0:NS, 0:NS]) for h in range(H)]),
                reads=[A.on_s, cstS], writes=[pso_])
        self.op(ACT, act(A.sg, pbg[:, 0:H * NS], AF.Silu), reads=[psg], writes=[A.sg_s])
        self.op(ACT, acp(A.us, pbg[:, H * NS:2 * H * NS]), reads=[psg], writes=[A.us_s])
        self.unpin(psg)
        gn = lambda h: self.vecT[:, V_GN + l * 4 + h:V_GN + l * 4 + h + 1]
        self.op(DVE, seq([stt(self.mixTs[:, h, :], pbo[:, h * NS:(h + 1) * NS], gn(h), A.sg[:, h * NS:(h + 1) * NS], ALU.mult, ALU.mult) for h in range(H)]),
                reads=[pso_, A.sg_s, self.vec_slot], writes=[self.mixTs_slot])
        self.dma(SP, A.buf, self.spool[l].rearrange("(t b) r c -> (b r) t c", t=2), A.buf_ds, writes=[A.buf_s])
        self.dma(SP, self.pso[l, :, 0:14, :], self.spool[l, :, 1:15, :], self.out_ds)
        sel = self.cst[0:120, C_SEL:C_SEL + 128].rearrange("p (t g b) -> p t g b", t=2, g=4)
        pbs, pss = self.pbank()
        self.op(PE, seq([mm(pbs[:, g * NS:(g + 1) * NS], A.buf[:, t, g * 128:(g + 1) * 128], sel[:, t, g, :], start=(t == 0), stop=(t == 1))
                         for g in range(4) for t in range(2)]), reads=[A.buf_s, cstS], writes=[pss])
        fs = []
        for g, w in enumerate(POOL_W):
            fs.append(ts(A.tq[:, g * NS:(g + 1) * NS], A.us[:, g * NS:(g + 1) * NS], float(1.0 / w - 1.0), None, ALU.mult))
        self.op(DVE, seq(fs), reads=[A.us_s], writes=[A.tq_s])
        self.op(DVE, seq([stt(A.pTs[:, g, :], pbs[:, g * NS:(g + 1) * NS], float(1.0 / w), A.tq[:, g * NS:(g + 1) * NS], ALU.mult, ALU.add)
                          for g, w in enumerate(POOL_W)]), reads=[pss, A.tq_s], writes=[A.pTs_s])
        pby, psy = self.pbank()
        self.op(PE, seq([mm(pby[:, g * NS:(g + 1) * NS], self.pwb[:, g, :], A.pTs[:, g, :]) for g in range(4)]),
                reads=[A.pTs_s, self.pw_slot], writes=[psy])
        pscale = self.vecT[:, V_PS + l * 4:V_PS + (l + 1) * 4]
        self.op(DVE, tt(self.mixTs[:, 4:8, :], pby[:, 0:4 * NS].rearrange("p (g n) -> p g n", g=4), pscale.unsqueeze(2).broadcast_to([128, 4, NS]), ALU.mult),
                reads=[psy, self.vec_slot], writes=[self.mixTs_slot])
        pbr, psr = self.pbank()
        self.op(PE, seq([tr(pbr[0:NS, g * 128:(g + 1) * 128], A.us[:, g * NS:(g + 1) * NS], ident) for g in range(4)]),
                reads=[A.us_s, cstS], writes=[psr])
        self.op(ACT, acp(A.ustage, pbr[0:NS, :]), reads=[psr], writes=[A.ustage_s])
        self.dma(SP, self.pso[l, :, 14, :], A.ustage, A.ustage_ds, reads=[A.ustage_s])

    def blk_front(self, l, b):
        PE, ACT, DVE, SP = self.PE, self.ACT, self.DVE, self.SP
        M = self.mx
        cstS = self.cst_slot
        t0 = b * BT
        hb = b % 2
        hT = lambda k: self.hbuf[:, k, hb * BT:(hb + 1) * BT]
        hS = lambda k: self.hslot[k][hb]
        pmv = lambda i, k: self.pm[:, i, k:k + 1]
        h3 = lambda ap: ap.rearrange("p (h d) -> p h d", h=H)
        hreads = [hS(k) for k in range(KC)]
        gn = lambda h: self.vecT[:, V_GN + l * 4 + h:V_GN + l * 4 + h + 1]
        sgg = M.sgg[b % 2]
        sgg_s = M.sgg_s[b % 2]
        rope = M.rope[b % 2]
        rope_s = M.rope_s[b % 2]
        self.stats_block([self.xT[:, k, t0:t0 + BT] for k in range(KC)], [[self.xslot[k][b]] for k in range(KC)],
                         BT, M.sq, M.sq_s, M.rstd, M.rstd_s, M.tmpS, M.tmpS_s, pool="m")
        for k in range(KC):
            r = k % 2
            self.op(DVE, stt(M.htmp[r], self.xT[:, k, t0:t0 + BT], pmv(1, k), M.rstd, ALU.mult, ALU.mult),
                    reads=[self.xslot[k][b], M.rstd_s, self.pm_slots[0]], writes=[M.htmp_s[r]])
            self.op(ACT, act(hT(k), M.htmp[r], AF.Identity, bias=pmv(0, k)), reads=[M.htmp_s[r], self.pm_slots[0]], writes=[hS(k)])
        self.dma(SP, rope, self.ropep[b * NTB:(b + 1) * NTB].rearrange("t p c -> p t c"), M.rope_ds[b % 2], writes=[rope_s])
        if b == 0:
            self.op(DVE, mset(self.S.rearrange("p h d -> p (h d)"), 0.0), writes=[self.S_slot])
            self.op(DVE, mset(self.Sb.rearrange("p h d -> p (h d)"), 0.0), writes=[self.Sb_slot])
        for j in range(12, 16):
            h = j - 12
            pb, ps = self.pbank(pool="m")
            self.op(PE, seq([mm(pb[:, 0:BT], self.winb[:, j // 4, k, (j % 4) * 128:(j % 4 + 1) * 128], hT(k), start=(k == 0), stop=(k == KC - 1))
                             for k in range(KC)]), reads=hreads + [self.win_slot[j // 4]], writes=[ps])
            self.op(ACT, act(M.sgt, pb[:, 0:BT], AF.Silu), reads=[ps], writes=[M.sgt_s])
            self.op(DVE, ts(sgg[:, h, :], M.sgt, gn(h), None, ALU.mult), reads=[M.sgt_s, self.vec_slot], writes=[sgg_s[h]])

    def blk_head(self, l, b, t):
        PE, ACT, DVE, SP = self.PE, self.ACT, self.DVE, self.SP
        M = self.mx
        cstS = self.cst_slot
        t0 = b * BT
        hb = b % 2
        hT = lambda k: self.hbuf[:, k, hb * BT:(hb + 1) * BT]
        hS = lambda k: self.hslot[k][hb]
        pmv = lambda i, k: self.pm[:, i, k:k + 1]
        h3 = lambda ap: ap.rearrange("p (h d) -> p h d", h=H)
        hreads = [hS(k) for k in range(KC)]
        gn = lambda h: self.vecT[:, V_GN + l * 4 + h:V_GN + l * 4 + h + 1]
        sgg = M.sgg[b % 2]
        sgg_s = M.sgg_s[b % 2]
        rope = M.rope[b % 2]
        rope_s = M.rope_s[b % 2]
        caus = self.cst[:, C_CAUS:C_CAUS + 128]
        bcur = self.bandb[:, 0:512].rearrange("p (g t) -> p g t", g=4)
        bcur0 = self.bandb[:, 512:1024].rearrange("p (g t) -> p g t", g=4)
        bprev = self.bandb[:, 1024:1088].rearrange("p (g t) -> p g t", g=4)
        pscale = lambda g: self.vecT[:, V_PS + l * 4 + g:V_PS + l * 4 + g + 1]
        wsl = [0, 1, 2, 4]
        c = b * NTB + t
        r2 = c % 2
        r3 = c % 3
        tc = slice(t * 128, (t + 1) * 128)
        banks = [self.fbank(i) for i in range(4)]
        bq, bk, bv, bu = banks
        fs = []
        for k in range(KC):
            for n in range(4):
                fs.append(mm(banks[n][0], self.hbuf[:, k, hb * BT + t * 128:hb * BT + (t + 1) * 128], self.winb[:, wsl[n], k, :],
                             start=(k == 0), stop=(k == KC - 1)))
        self.op(PE, seq(fs), reads=hreads + [self.win_slot[i] for i in wsl], writes=[x[1] for x in banks])
        self.op(ACT, acp(M.v[:, t, :], bv[0]), reads=[bv[1]], writes=[M.v_s[t]])
        self.op(ACT, acp(M.u[r3], bu[0]), reads=[bu[1]], writes=[M.u_s[r3]])
        if c == T // 128 - 1:
            self.op(ACT, acp(M.pstage[96:128, :], bu[0][96:128, :]), reads=[bu[1]], writes=[M.pstage_s])
            self.dma(SP, self.pp[l], M.pstage[113:128, :], M.pstage_ds, reads=[M.pstage_s])
        cos2 = rope[:, t, 0:128].unsqueeze(1).broadcast_to([128, H, 128])
        sinlo = rope[:, t, 128:192].unsqueeze(1).broadcast_to([128, H, 64])
        sinhi = rope[:, t, 192:256].unsqueeze(1).broadcast_to([128, H, 64])
        for qi, bank in enumerate((bq, bk)):
            x3 = h3(bank[0])
            self.op(DVE, seq([tt(h3(M.B)[:, :, 0:64], x3[:, :, 64:128], sinlo, ALU.mult),
                              tt(h3(M.B)[:, :, 64:128], x3[:, :, 0:64], sinhi, ALU.mult)]), reads=[bank[1], rope_s], writes=[M.B_s])
            self.op(DVE, tt(x3, x3, cos2, ALU.mult), reads=[rope_s], writes=[bank[1]])
            self.op(DVE, tt(M.R[qi], bank[0], M.B, ALU.add), reads=[bank[1], M.B_s], writes=[M.R_s[qi]])
        dq3 = self.cst[:, C_DQTM:C_DQTM + 4].unsqueeze(2).broadcast_to([128, H, 128])
        dk3 = self.cst[:, C_DKTM:C_DKTM + 4].unsqueeze(2).broadcast_to([128, H, 128])
        fq_ = tt(h3(M.qd[r2]), h3(M.R[0]), dq3, ALU.mult)
        fq_.cost = 2000.0
        self.op(self.POOL, fq_, reads=[M.R_s[0], cstS], writes=[M.qd_s[r2]])
        fk_ = tt(h3(M.kd[:, t, :]), h3(M.R[1]), dk3, ALU.mult)
        fk_.cost = 2000.0
        self.op(self.POOL, fk_, reads=[M.R_s[1], cstS], writes=[M.kd_s[t]])

    def blk_tail(self, l, b, t):
        PE, ACT, DVE, SP = self.PE, self.ACT, self.DVE, self.SP
        M = self.mx
        cstS = self.cst_slot
        t0 = b * BT
        hb = b % 2
        hT = lambda k: self.hbuf[:, k, hb * BT:(hb + 1) * BT]
        hS = lambda k: self.hslot[k][hb]
        pmv = lambda i, k: self.pm[:, i, k:k + 1]
        h3 = lambda ap: ap.rearrange("p (h d) -> p h d", h=H)
        hreads = [hS(k) for k in range(KC)]
        gn = lambda h: self.vecT[:, V_GN + l * 4 + h:V_GN + l * 4 + h + 1]
        sgg = M.sgg[b % 2]
        sgg_s = M.sgg_s[b % 2]
        rope = M.rope[b % 2]
        rope_s = M.rope_s[b % 2]
        caus = self.cst[:, C_CAUS:C_CAUS + 128]
        bcur = self.bandb[:, 0:512].rearrange("p (g t) -> p g t", g=4)
        bcur0 = self.bandb[:, 512:1024].rearrange("p (g t) -> p g t", g=4)
        bprev = self.bandb[:, 1024:1088].rearrange("p (g t) -> p g t", g=4)
        pscale = lambda g: self.vecT[:, V_PS + l * 4 + g:V_PS + l * 4 + g + 1]
        wsl = [0, 1, 2, 4]
        c = b * NTB + t
        r2 = c % 2
        r3 = c % 3
        tc = slice(t * 128, (t + 1) * 128)
        pbt, pst = self.pbank(pool="m")
        pbt16 = pbt.bitcast(BF16)
        fs = [tr(pbt16[:, h * 128:(h + 1) * 128], M.qd[r2][:, h * 128:(h + 1) * 128], self.identb) for h in range(H)]
        fs += [tr(pbt16[:, 512 + h * 128:512 + (h + 1) * 128], M.kd[:, t, h * 128:(h + 1) * 128], self.identb) for h in range(H)]
        self.op(PE, seq(fs), reads=[M.qd_s[r2], M.kd_s[t], self.const2_slot], writes=[pst])
        self.op(ACT, acp(M.qkT[r2], pbt16), reads=[pst], writes=[M.qkT_s[r2]])
        qdT = M.qkT[r2][:, 0:512].rearrange("p (h d) -> p h d", h=H)
        kdT = M.qkT[r2][:, 512:1024].rearrange("p (h d) -> p h d", h=H)
        pbs, pss = self.pbank(pool="m")
        self.op(PE, seq([mm(pbs[:, h * 128:(h + 1) * 128], kdT[:, h, :], qdT[:, h, :]) for h in range(H)]),
                reads=[M.qkT_s[r2]], writes=[pss])
        sTm = M.sTm[r2]
        self.op(DVE, tt(sTm, h3(pbs), caus.unsqueeze(1).broadcast_to([128, H, 128]), ALU.mult), reads=[pss, cstS], writes=[M.sTm_s[r2]])
        pbo, pso_ = self.pbank(pool="m")
        fs = []
        for h in range(H):
            fs.append(mm(pbo[:, h * 128:(h + 1) * 128], M.v[:, t, h * 128:(h + 1) * 128], sTm[:, h, :], start=True, stop=False))
            fs.append(mm(pbo[:, h * 128:(h + 1) * 128], self.Sb[:, h, :], qdT[:, h, :], start=False, stop=True))
        self.op(PE, seq(fs), reads=[M.v_s[t], M.sTm_s[r2], self.Sb_slot, M.qkT_s[r2]], writes=[pso_])
        pbu, psu = self.pbank(pool="m")
        self.op(PE, seq([mm(pbu[:, h * 128:(h + 1) * 128], M.kd[:, t, h * 128:(h + 1) * 128], M.v[:, t, h * 128:(h + 1) * 128]) for h in range(H)]),
                reads=[M.kd_s[t], M.v_s[t]], writes=[psu])
        self.op(DVE, tt(M.Stmp, h3(pbu), self.S, ALU.add), reads=[psu, self.S_slot], writes=[M.Stmp_s])
        self.op(DVE, seq([ts(self.S[:, h, :], M.Stmp[:, h, :], float(GC[h]), None, ALU.mult) for h in range(H)]),
                reads=[M.Stmp_s], writes=[self.S_slot])
        self.op(ACT, acp(self.Sb.rearrange("p h d -> p (h d)"), self.S.rearrange("p h d -> p (h d)")), reads=[self.S_slot], writes=[self.Sb_slot])
        self.op(ACT, act(M.osq, pbo, AF.Square), reads=[pso_], writes=[M.osq_s])
        pbn, psn = self.pbank(pool="m")
        self.op(PE, mm(pbn, self.onesb, M.osq), reads=[M.osq_s, self.const2_slot], writes=[psn])
        self.rstd_from_psum(pbn, psn, M.rstdo, M.rstdo_s, M.to, M.to_s, 1.0 / 128)
        self.op(DVE, tt(pbo, pbo, M.rstdo, ALU.mult), reads=[M.rstdo_s], writes=[pso_])
        self.op(DVE, tt(M.mixT[:, 0:4, tc], h3(pbo), sgg[:, :, tc], ALU.mult),
                reads=[pso_] + sgg_s, writes=[M.mix_s[k][t] for k in range(4)])
        pbp, psp = self.pbank(pool="m")
        fs = []
        for g in range(4):
            gs = slice(g * 128, (g + 1) * 128)
            fs.append(mm(pbp[:, gs], M.u[r3][:, gs], (bcur0 if c == 0 else bcur)[:, g, :], start=True, stop=(c == 0)))
            if c > 0:
                fs.append(mm(pbp[:, g * 128:g * 128 + 16], M.u[(c - 1) % 3][64:128, gs], bprev[64:128, g, :], start=False, stop=True))
        self.op(PE, seq(fs), reads=[M.u_s[r3], self.band_slot] + ([M.u_s[(c - 1) % 3]] if c > 0 else []), writes=[psp])
        self.op(ACT, acp(M.pTt[r2].rearrange("p g t -> p (g t)"), pbp), reads=[psp], writes=[M.pTt_s[r2]])
        pby, psy = self.pbank(pool="m")
        self.op(PE, seq([mm(pby[:, g * 128:(g + 1) * 128], self.pwb[:, g, :], M.pTt[r2][:, g, :]) for g in range(4)]),
                reads=[M.pTt_s[r2], self.pw_slot], writes=[psy])
        ps4 = self.vecT[:, V_PS + l * 4:V_PS + (l + 1) * 4].unsqueeze(2).broadcast_to([128, 4, 128])
        self.op(DVE, tt(M.mixT[:, 4:8, tc], pby.rearrange("p (g t) -> p g t", g=4), ps4, ALU.mult),
                reads=[psy, self.vec_slot], writes=[M.mix_s[4 + g][t] for g in range(4)])

    def blk_wout(self, l, b):
        PE, ACT, DVE, SP = self.PE, self.ACT, self.DVE, self.SP
        M = self.mx
        cstS = self.cst_slot
        t0 = b * BT
        hb = b % 2
        hT = lambda k: self.hbuf[:, k, hb * BT:(hb + 1) * BT]
        hS = lambda k: self.hslot[k][hb]
        pmv = lambda i, k: self.pm[:, i, k:k + 1]
        h3 = lambda ap: ap.rearrange("p (h d) -> p h d", h=H)
        hreads = [hS(k) for k in range(KC)]
        gn = lambda h: self.vecT[:, V_GN + l * 4 + h:V_GN + l * 4 + h + 1]
        sgg = M.sgg[b % 2]
        sgg_s = M.sgg_s[b % 2]
        rope = M.rope[b % 2]
        rope_s = M.rope_s[b % 2]
        if b == NBLK - 1:
            self.dma(SP, self.rp[l].rearrange("h d e -> d h e"), self.S, self.rp_ds, reads=[self.S_slot])
        mreads = [M.mix_s[k][t] for k in range(KC) for t in range(NTB)]
        if b == 0:
            pbw, psw = self.pbank(pin=True, pool="m")
        for jq in range(4):
            sap, ss = self.ring_next(l, RING_WOUT + jq)
            w = sap.rearrange("p (k c) -> p k c", k=KC)
            for cc in range(2):
                cch = 2 * jq + cc
                pb, ps = self.pbank(pool="m")
                self.op(PE, seq([mm(pb[:, 0:BT], w[:, k, cc * 128:(cc + 1) * 128], M.mixT[:, k, :], start=(k == 0), stop=(k == KC - 1))
                                 for k in range(KC)]), reads=mreads + [ss], writes=[ps])
                xv = self.xT[:, cch, t0:t0 + BT]
                self.op(DVE, stt(xv, pb[:, 0:BT], pmv(2, cch), xv, ALU.mult, ALU.add), reads=[ps, self.pm_slots[0]], writes=[self.xslot[cch][b]])
                if b == 0:
                    self.op(PE, seq([mm(pbw[:, cch * NS:(cch + 1) * NS], w[:, k, cc * 128:(cc + 1) * 128], self.mixTs[:, k, :],
                                        start=(k == 0), stop=(k == KC - 1)) for k in range(KC)]), reads=[self.mixTs_slot, ss], writes=[psw])
            self.ring_done()
        if b == 0:
            n1 = NS + 1
            self.op(DVE, tt(self.tw, pbw[:, 0:KC * NS].rearrange("p (k n) -> p k n", k=KC), self.modT[:, 16:24, 1:n1], ALU.mult),
                    reads=[psw, self.mod_slots[0]], writes=[self.tw_slot])
            self.unpin(psw)
            self.op(DVE, tt(self.xsT, self.xsT, self.tw, ALU.add), reads=[self.tw_slot], writes=[self.xs_slot])

    def ffn_half(self, l, hf):
        PE, ACT, DVE, SP, POOL = self.PE, self.ACT, self.DVE, self.SP, self.POOL
        Fb = self.ff
        pmv = lambda i, k: self.pm[:, i, k:k + 1]
        n1 = NS + 1
        t0 = hf * HALF
        nbb = 512 // BT
        for tb in range(2):
            tt0 = t0 + tb * 512
            self.stats_block([self.xT[:, k, tt0:tt0 + 512] for k in range(KC)], [self.xs_block(k, tt0, 512) for k in range(KC)],
                             512, Fb.sq, Fb.sq_s, Fb.rstd[tb], Fb.rstd_s[tb], Fb.tmpS, Fb.tmpS_s)
        for tb in range(2):
            tt0 = t0 + tb * 512
            for k in range(KC):
                self.op(DVE, stt(Fb.htmp, self.xT[:, k, tt0:tt0 + 512], pmv(4, k), Fb.rstd[tb], ALU.mult, ALU.mult),
                        reads=self.xs_block(k, tt0, 512) + [Fb.rstd_s[tb], self.pm_slots[1]], writes=[Fb.htmp_s])
                self.op(ACT, act(self.hbuf[:, k, tb * 512:(tb + 1) * 512], Fb.htmp, AF.Identity, bias=pmv(3, k)),
                        reads=[Fb.htmp_s, self.pm_slots[1]], writes=[self.hslot[k][tb * nbb + i] for i in range(nbb)])
        hreads = [[self.hslot[k][tb * nbb + i] for k in range(KC) for i in range(nbb)] for tb in range(2)]
        samp = (hf == 0)
        if samp:
            self.op(ACT, act(Fb.sqs, self.xsT.rearrange("p k n -> p (k n)"), AF.Square), reads=[self.xs_slot], writes=[Fb.sqs_s])
            pb, ps = self.pbank()
            self.op(PE, seq([mm(pb[:, 0:NS], self.onesb, Fb.sqs[:, k * NS:(k + 1) * NS], start=(k == 0), stop=(k == KC - 1)) for k in range(KC)]),
                    reads=[Fb.sqs_s, self.const2_slot], writes=[ps])
            self.rstd_from_psum(pb[:, 0:NS], ps, Fb.rstd16, Fb.rstd16_s, Fb.tmp16, Fb.tmp16_s, 1.0 / D)
            self.op(DVE, tt(Fb.t1, self.xsT, Fb.rstd16.unsqueeze(1).broadcast_to([128, KC, NS]), ALU.mult), reads=[self.xs_slot, Fb.rstd16_s], writes=[Fb.t1_s])
            self.op(DVE, tt(Fb.t1, Fb.t1, self.smG[:, 1], ALU.mult), reads=[self.smG_slots[1]], writes=[Fb.t1_s])
            self.op(DVE, tt(Fb.h2s, Fb.t1, self.modT[:, 24:32, 1:n1], ALU.add), reads=[Fb.t1_s, self.mod_slots[1]], writes=[Fb.h2s_s])
        sai = 0
        for (f0, f1) in FGROUPS:
            nf = f1 - f0
            for jp in range(f0 // 2, f1 // 2):
                jl = jp - f0 // 2
                self.dma(POOL, Fb.wd[:, jl, :], self.wring[l, RING_DOWN + jp], Fb.wd_ds[jl], writes=[Fb.wd_s[jl]], max_dma_last_dim=8192)
            for f in range(f0, f1):
                fl = f - f0
                sap, ss = self.ring_next(l, RING_GU + f)
                w = sap.rearrange("p (k c) -> p k c", k=KC)
                pa = [self.pbank() for _ in range(2)]
                pbk = [self.pbank() for _ in range(2)]
                if samp:
                    pbsm, pssm = self.pbank()
                for tb in range(2):
                    for half_ab, banks in ((0, pa), (1, pbk)):
                        fs = []
                        for k in range(KC):
                            lw = w[:, k, half_ab * 128:(half_ab + 1) * 128]
                            fs.append(mm(banks[tb][0], lw, self.hbuf[:, k, tb * 512:(tb + 1) * 512], start=(k == 0), stop=(k == KC - 1)))
                        if samp and tb == 1:
                            for k in range(KC):
                                lw = w[:, k, half_ab * 128:(half_ab + 1) * 128]
                                fs.append(mm(pbsm[:, half_ab * NS:(half_ab + 1) * NS], lw, Fb.h2s[:, k, :], start=(k == 0), stop=(k == KC - 1)))
                        self.op(PE, seq(fs), reads=hreads[tb] + [ss] + ([Fb.h2s_s] if (samp and tb == 1) else []),
                                writes=[banks[tb][1]] + ([pssm] if (samp and tb == 1) else []))
                self.ring_done()
                if hf == 0 and l + 1 < self.nl:
                    ph_ = self.phase
                    self.phase = f'L{l + 1}.ada'
                    if f >= 8:
                        g_ = f - 8
                        js = [2 * g_, 2 * g_ + 1] if g_ < 10 else [20 + (g_ - 10)]
                        for j in js:
                            self.ada_step_side(l + 1, j, self.ff.modN, self.ff.modN_s)
                    self.phase = ph_
                for tb in range(2):
                    r = sai % 2
                    sai += 1
                    self.op(ACT, act(Fb.sa[r], pa[tb][0], AF.Silu), reads=[pa[tb][1]], writes=[Fb.sa_s[r]])
                    self.op(DVE, tt(Fb.gT[:, fl, tb * 512:(tb + 1) * 512], pbk[tb][0], Fb.sa[r], ALU.mult), reads=[pbk[tb][1], Fb.sa_s[r]],
                            writes=[Fb.gT_s[fl][tb]])
                if samp:
                    self.op(ACT, act(Fb.sas, pbsm[:, 0:NS], AF.Silu), reads=[pssm], writes=[Fb.sas_s])
                    self.op(DVE, tt(Fb.gTs[:, fl, :], pbsm[:, NS:2 * NS], Fb.sas, ALU.mult), reads=[pssm, Fb.sas_s], writes=[Fb.gTs_s])
            wdv = Fb.wd.rearrange("p j (i c) -> p j i c", i=2)
            if samp:
                pbd, psd = self.pbank(pin=True)
            for cch in range(KC):
                for tb in range(2):
                    pb, ps = self.pbank()
                    self.op(PE, seq([mm(pb, wdv[:, fl // 2, fl % 2, cch * 128:(cch + 1) * 128], Fb.gT[:, fl, tb * 512:(tb + 1) * 512],
                                        start=(fl == 0), stop=(fl == nf - 1)) for fl in range(nf)]),
                            reads=[Fb.gT_s[fl][tb] for fl in range(nf)] + [Fb.wd_s[j] for j in range(nf // 2)], writes=[ps])
                    tt0 = t0 + tb * 512
                    xv = self.xT[:, cch, tt0:tt0 + 512]
                    self.op(DVE, stt(xv, pb, pmv(5, cch), xv, ALU.mult, ALU.add), reads=[ps, self.pm_slots[1]], writes=self.xs_block(cch, tt0, 512))
                if samp:
                    self.op(PE, seq([mm(pbd[:, cch * NS:(cch + 1) * NS], wdv[:, fl // 2, fl % 2, cch * 128:(cch + 1) * 128], Fb.gTs[:, fl, :],
                                        start=(fl == 0), stop=(fl == nf - 1)) for fl in range(nf)]),
                            reads=[Fb.gTs_s] + [Fb.wd_s[j] for j in range(nf // 2)], writes=[psd])
            if samp:
                self.op(DVE, tt(self.tw, pbd[:, 0:KC * NS].rearrange("p (k n) -> p k n", k=KC), self.modT[:, 40:48, 1:n1], ALU.mult),
                        reads=[psd, self.mod_slots[1]], writes=[self.tw_slot])
                self.unpin(psd)
                self.op(DVE, tt(self.xsT, self.xsT, self.tw, ALU.add), reads=[self.tw_slot], writes=[self.xs_slot])


_CACHE = {}


def _get_nc(nl):
    if nl not in _CACHE:
        b1 = Builder(nl)
        b1.build(emit=False)
        assert b1.n_overcommit == 0, b1.n_overcommit
        b = Builder(nl, plan=b1.bank_plan, bank_seq=b1.bank_seq_out)
        _CACHE[nl] = b.build()
    return _CACHE[nl]


def run(inputs, nl=DEPTH):
    g = lambda k: np.asarray(inputs[k], dtype=np.float32)
    x_prompt, x_sample, c_prompt, c_sample = g("x_prompt"), g("x_sample"), g("c_prompt"), g("c_sample")
    state_ret, state_pool = g("state_ret"), g("state_pool")
    nle = max(nl, 1)
    cst, ropep, bands = _host_consts()
    win, wring, poolw = _host_weights(g("w_in"), g("w_out"), g("w_gu"), g("w_down"), g("w_ada"), g("pool_w"), nle)
    vecs = np.zeros((384, 128), np.float32)
    vecs[V_NM:V_NM + 32] = g("norm_mix").reshape(32, 128)
    vecs[V_NF:V_NF + 32] = g("norm_ffn").reshape(32, 128)
    vecs[V_BADA:V_BADA + 192] = g("b_ada").reshape(192, 128)
    vecs[V_GN:V_GN + 16] = g("ret_gn").reshape(16, 128)
    vecs[V_PS:V_PS + 16] = g("pool_scale").reshape(16, 128)
    vecs[V_FN:V_FN + 8] = g("final_norm").reshape(8, 128)
    in_maps = []
    for c in range(NCORES):
        sl = slice(c * NS, (c + 1) * NS)
        in_maps.append({
            "xp": np.ascontiguousarray(x_prompt[c]),
            "xs": np.ascontiguousarray(x_sample[sl, 0, :]),
            "cc": np.ascontiguousarray(np.concatenate([c_prompt[c:c + 1], c_sample[sl]], 0)),
            "sret": np.ascontiguousarray(state_ret[:nle, sl]),
            "spool": np.ascontiguousarray(state_pool[:nle, sl]),
            "win": win, "wring": wring, "poolw": poolw, "vecs": vecs, "cst": cst, "ropep": ropep, "bands": bands,
        })
    nc = _get_nc(nl)
    res = run_bass_kernel_spmd(nc, in_maps, core_ids=list(range(NCORES)))
    R = res.results
    y_prompt = np.stack([np.asarray(R[c]["yp"]) for c in range(NCORES)], 0).astype(np.float32)
    y_sample = np.concatenate([np.asarray(R[c]["ys"]) for c in range(NCORES)], 0).reshape(NCORES * NS, 1, D).astype(np.float32)
    new_ret_prompt = np.stack([np.asarray(R[c]["rp"]) for c in range(NCORES)], 1).astype(np.float32)
    new_pool_prompt = np.stack([np.asarray(R[c]["pp"]) for c in range(NCORES)], 1).astype(np.float32)
    new_ret_sample = np.concatenate([np.asarray(R[c]["rs"]) for c in range(NCORES)], 1).astype(np.float32)
    new_pool_sample = np.concatenate([np.asarray(R[c]["pso"]) for c in range(NCORES)], 1).astype(np.float32)
    return (y_prompt, y_sample, new_ret_prompt, new_pool_prompt, new_ret_sample, new_pool_sample)


def kernel(**inputs):
    return run(inputs, DEPTH)
```

```python
import numpy as np
from contextlib import ExitStack

import concourse.bass as bass
import concourse.mybir as mybir
from concourse.bass_utils import run_bass_kernel_spmd

F32 = mybir.dt.float32
BF16 = mybir.dt.bfloat16
ALU = mybir.AluOpType
AF = mybir.ActivationFunctionType

D = 1024
KC = 8
T = 2048
NS = 16
NCORES = 8
DEPTH = 4
H = 4
DK = 128
DIN = 2560
FF = 2816
FC = 22
EPS = 1e-6
BT = 256
NTB = BT // 128
NBLK = T // BT
HALF = 1024
FGROUPS = [(0, 8), (8, 16), (16, 22)]
POOL_W = (2, 4, 8, 16)
NRING = 4
RING_ADA, RING_WOUT, RING_GU, RING_DOWN = 0, 24, 28, 50
NRSLOT = 61
GAM = [1.0 - 2.0 ** (-5.0 - h) for h in range(H)]
GC = [g ** 128 for g in GAM]

C_ID, C_CAUS, C_DQTM, C_DKTM, C_ID16, C_SEL, C_ROPES = 0, 128, 256, 260, 264, 520, 648
C_TOT = 1160
NBAND = 1088
V_NM, V_NF, V_BADA, V_GN, V_PS, V_FN = 0, 32, 64, 256, 272, 288


def _host_consts():
    cst = np.zeros((128, C_TOT), np.float32)
    cst[:, C_ID:C_ID + 128] = np.eye(128, dtype=np.float32)
    j = np.arange(128)[:, None]
    i = np.arange(128)[None, :]
    cst[:, C_CAUS:C_CAUS + 128] = (i >= j).astype(np.float32)
    n = np.arange(128, dtype=np.float64)
    for h in range(H):
        lg = np.log(np.float64(np.float32(GAM[h])))
        cst[:, C_DQTM + h] = np.exp((n + 1.0) * lg)
        cst[:, C_DKTM + h] = (DK ** -0.5) * np.exp(-(n + 1.0) * lg)
    cst[:, C_ID16:C_ID16 + 256] = np.eye(16, dtype=np.float32).reshape(1, 256)
    sel = np.zeros((128, 2, 4, 16), np.float32)
    for t in range(2):
        for bl in range(8):
            for r in range(15):
                for g, w in enumerate(POOL_W):
                    if r >= 16 - w:
                        sel[bl * 15 + r, t, g, t * 8 + bl] = 1.0
    cst[:, C_SEL:C_SEL + 128] = sel.reshape(128, 128)
    inv = (1.0 / (np.float32(10000.0) ** (np.arange(0, 128, 2, dtype=np.float32) / np.float32(128)))).astype(np.float32)

    def tables(pos):
        ang = (np.asarray(pos, np.float32)[:, None] * inv[None, :]).astype(np.float32)
        c = np.cos(ang.astype(np.float64)).astype(np.float32)
        s = np.sin(ang.astype(np.float64)).astype(np.float32)
        return np.concatenate([c, c], -1), np.concatenate([-s, s], -1)

    c2, s2 = tables(np.array([16384.0], np.float32))
    sc = np.float32(DK ** -0.5)
    ropes = np.concatenate([c2, s2, c2 * sc, s2 * sc], -1)
    cst[:, C_ROPES:C_ROPES + 512] = ropes
    c2p, s2p = tables(np.arange(T, dtype=np.float32))
    ropep = np.concatenate([c2p, s2p], -1).reshape(T // 128, 128, 256).astype(np.float32)
    bands = np.zeros((128, NBAND), np.float32)
    sidx = np.arange(128)[:, None]
    tidx = np.arange(128)[None, :]
    for g, w in enumerate(POOL_W):
        cur = ((sidx <= tidx) & (sidx > tidx - w)).astype(np.float64) / w - (sidx == tidx)
        cnt = np.minimum(tidx + 1, w).astype(np.float64)
        cur0 = ((sidx <= tidx) & (sidx > tidx - w)).astype(np.float64) / cnt - (sidx == tidx)
        bands[:, g * 128:(g + 1) * 128] = cur
        bands[:, 512 + g * 128:512 + (g + 1) * 128] = cur0
        t16 = np.arange(16)[None, :]
        prev = ((sidx - 128) > (t16 - w)).astype(np.float64) / w
        bands[:, 1024 + g * 16:1024 + (g + 1) * 16] = prev
    return cst, ropep, bands


def _slot(w2d):
    r, c = w2d.shape
    k = r // 128
    return np.ascontiguousarray(w2d.reshape(k, 128, c).transpose(1, 0, 2)).reshape(128, k * c)


def _host_weights(w_in, w_out, w_gu, w_down, w_ada, pool_w, nl):
    win = np.empty((nl, 5, 128, 4096), np.float32)
    wring = np.empty((nl, NRSLOT, 128, 2048), np.float32)
    poolw = np.empty((nl, 128, 512), np.float32)
    for l in range(nl):
        for j in range(5):
            win[l, j] = _slot(w_in[l][:, j * 512:(j + 1) * 512])
        for j in range(24):
            wring[l, RING_ADA + j] = _slot(w_ada[l][:, j * 256:(j + 1) * 256])
        for j in range(4):
            wring[l, RING_WOUT + j] = _slot(w_out[l][:, j * 256:(j + 1) * 256])
        for f in range(FC):
            ab = np.concatenate([w_gu[l][:, f * 128:(f + 1) * 128],
                                 w_gu[l][:, FF + f * 128:FF + (f + 1) * 128]], axis=1)
            wring[l, RING_GU + f] = _slot(ab)
        for j in range(11):
            wring[l, RING_DOWN + j] = _slot(w_down[l][j * 256:(j + 1) * 256, :])
        poolw[l] = np.ascontiguousarray(pool_w[l].transpose(1, 0, 2)).reshape(128, 512)
    return win, wring, poolw


class Op:
    __slots__ = ("idx", "eng", "fn", "sem", "inc", "deps", "cost", "lat", "t0", "t1", "sig", "nun", "users", "ready", "tag", "vs", "phase")

    def __init__(self, idx, eng, fn, sem, inc, deps, cost, lat):
        self.idx = idx
        self.eng = eng
        self.fn = fn
        self.sem = sem
        self.inc = inc
        self.deps = deps
        self.cost = cost
        self.lat = lat
        self.sig = None


class Slot:
    __slots__ = ("name", "w", "r", "ops", "rng")

    def __init__(self, name=""):
        self.name = name
        self.w = None
        self.r = []
        self.ops = None
        self.rng = None


class Eng:
    def __init__(self, name, sem, is_pe=False):
        self.name = name
        self.sem = sem
        self.cnt = 0
        self.waited = {}
        self.prog = []
        self.is_pe = is_pe
        self.ops = []


class DSem:
    def __init__(self, sem):
        self.sem = sem
        self.cnt = 0


def _n(ap):
    r = 1
    for v in ap.shape[1:]:
        r *= v
    return r


def _c(f, cost):
    f.cost = cost
    return f


def tt(out, in0, in1, op):
    return _c(lambda h: h.tensor_tensor(out=out, in0=in0, in1=in1, op=op), 70 + 1.6 * _n(out))


def stt(out, in0, scalar, in1, op0, op1):
    return _c(lambda h: h.scalar_tensor_tensor(out=out, in0=in0, scalar=scalar, in1=in1, op0=op0, op1=op1), 70 + 1.6 * _n(out))


def ts(out, in0, s1, s2, op0, op1=None):
    if op1 is None:
        return _c(lambda h: h.tensor_scalar(out=out, in0=in0, scalar1=s1, scalar2=None, op0=op0), 70 + 1.05 * _n(out))
    return _c(lambda h: h.tensor_scalar(out=out, in0=in0, scalar1=s1, scalar2=s2, op0=op0, op1=op1), 70 + 1.05 * _n(out))


def vcp(out, in_):
    return _c(lambda h: h.tensor_copy(out=out, in_=in_), 70 + 1.05 * _n(out))


def acp(out, in_):
    return _c(lambda h: h.copy(out=out, in_=in_), 200 + 0.75 * _n(out))


def act(out, in_, func, bias=None, scale=None):
    kw = {}
    if bias is not None:
        kw["bias"] = bias
    if scale is not None:
        kw["scale"] = scale
    return _c(lambda h: h.activation(out=out, in_=in_, func=func, **kw), 220 + 0.75 * _n(out))


def mm(out, lhsT, rhs, start=True, stop=True):
    n = _n(rhs)
    c = 20 + max(n, 64) / 2.2
    if lhsT.dtype == F32:
        c *= 4
    return _c(lambda h: h.matmul(out, lhsT, rhs, start=start, stop=stop), c)


def tr(out, in_, identity):
    c = 70.0 * (4 if in_.dtype == F32 else 1)
    return _c(lambda h: h.transpose(out=out, in_=in_, identity=identity), c)


def seq(fs):
    fs = list(fs)

    def run(h):
        i = None
        for f in fs:
            i = f(h)
        return i
    run.cost = sum(f.cost for f in fs)
    return run


def recip(out, in_):
    return _c(lambda h: h.reciprocal(out=out, in_=in_), 70 + 8.4 * _n(out))


def rfast(out, in_):
    return _c(lambda h: h.reciprocal_approx_fast(out=out, in_=in_), 70 + 2.0 * _n(out))


def mset(ap, v):
    return _c(lambda h: h.memset(ap, v), 70 + 1.05 * _n(ap))


def tred(out, in_, op):
    return _c(lambda h: h.tensor_reduce(out=out, in_=in_, axis=mybir.AxisListType.X, op=op), 70 + 1.05 * _n(in_))


def _prod(s):
    r = 1
    for v in s:
        r *= v
    return r


class Builder:
    def __init__(self, nl, plan=None, bank_seq=None):
        self.nl = nl
        self.plan = plan
        self.bank_seq = bank_seq
        self.nalloc = 0
        self.valloc = []
        self.rng_q = []
        self.rng_last = None
        self.nc = bass.Bass("TRN2", target_bir_lowering=False)
        self.es = ExitStack()
        self.nsem = 0

    def sem(self, name):
        self.nsem += 1
        return self.es.enter_context(self.nc.semaphore(name))

    def dsem(self, name):
        return DSem(self.sem(name))

    def carve(self, shape, dtype, parts=128):
        esz = 4 if dtype == F32 else 2
        nb = _prod(shape) * esz
        nb_al = (nb + 31) // 32 * 32
        off = self.off
        self.off += nb_al
        self.rng_q.append((off, off + nb_al))
        self.rng_last = (off, off + nb_al)
        self.last_carve = (off, off + nb_al)
        self.peak = max(self.peak, self.off)
        assert self.off <= self.sb_bytes, f"SBUF overflow {self.off} > {self.sb_bytes}"
        w0 = off // 4
        ap = self.sb[0:parts, w0:w0 + nb_al // 4]
        if dtype != F32:
            ap = ap.bitcast(dtype)
        ap = ap[:, 0:_prod(shape)]
        if len(shape) == 2:
            ap = ap.rearrange("p (a b) -> p a b", a=shape[0])
        elif len(shape) == 3:
            ap = ap.rearrange("p (a b c) -> p a b c", a=shape[0], b=shape[1])
        return ap

    def _deps(self, reads, writes):
        deps = set()
        for s in reads:
            if s.w is not None:
                deps.add(s.w)
        for s in writes:
            if s.w is not None:
                deps.add(s.w)
            deps.update(s.r)
        return deps

    def _commit(self, o, reads, writes):
        for s in writes:
            s.w = o
            s.r = []
            if s.ops is not None:
                s.ops.append(o)
        for s in reads:
            s.r.append(o)
            if s.ops is not None:
                s.ops.append(o)

    def op(self, E, fn, reads=(), writes=()):
        import sys as _sys
        self.tag = 'L%d' % _sys._getframe(1).f_lineno
        o = Op(len(self.ops), E, fn, E, 1, self._deps(reads, writes), float(getattr(fn, "cost", 300.0)), 0.0)
        o.lat = o.cost + 60.0
        o.vs = [x for x in set(list(reads) + list(writes)) if x.ops is not None]
        o.phase = self.phase
        o.tag = getattr(fn, 'tag', '') or self.tag
        self.ops.append(o)
        E.ops.append(o)
        self._commit(o, reads, writes)
        return o

    def dma(self, Q, out, in_, ds, reads=(), writes=(), nbytes=None, **kw):
        if nbytes is None:
            nbytes = _n(out) * out.shape[0] * 4
        fn = (lambda h, o=out, i=in_, kw=kw: h.dma_start(out=o, in_=i, **kw))
        cost = 1200.0 if Q is self.POOL else 120.0
        o = Op(len(self.ops), Q, fn, ds, 16, self._deps(reads, writes), cost, 2200.0 + nbytes / 180.0)
        o.tag = 'dma:' + self.tag
        o.vs = []
        o.phase = self.phase
        self.ops.append(o)
        Q.ops.append(o)
        self._commit(o, reads, writes)
        return o

    def alias_handoff(self, old_slots, new_slots):
        for sn in new_slots:
            toks = set()
            for so in old_slots:
                if sn.rng is not None and so.rng is not None and (so.rng[1] <= sn.rng[0] or sn.rng[1] <= so.rng[0]):
                    continue
                if so.w is not None:
                    toks.add(so.w)
                toks.update(so.r)
            sn.r = list(sn.r) + list(toks)

    def schedule(self, window=96):
        engs = [self.PE, self.ACT, self.DVE, self.POOL, self.SP]
        if self.bank_seq is not None:
            for seqb in self.bank_seq:
                for x, y in zip(seqb[:-1], seqb[1:]):
                    xo = self.valloc[x].ops
                    for oy in self.valloc[y].ops:
                        oy.deps.update(xo)
        for o in self.ops:
            o.nun = len(o.deps)
            o.users = []
            o.ready = 0.0
            o.t0 = None
        for o in self.ops:
            for d in o.deps:
                d.users.append(o)
        pend = {E: list(E.ops) for E in engs}
        order = {E: [] for E in engs}
        etime = {E: 0.0 for E in engs}
        remaining = len(self.ops)
        INF = float("inf")
        bank_free = [0.0] * 8
        vbank = {}
        vleft = {}
        vend = {}
        for sl in self.valloc:
            vleft[id(sl)] = len(sl.ops)
            vend[id(sl)] = 0.0
        use_banks = self.plan is None
        self.n_overcommit = 0
        vorder = sorted([sl for sl in self.valloc if sl.ops], key=lambda sl: (sl.ops[0].idx, sl.name))
        vrank = {id(sl): i for i, sl in enumerate(vorder)}
        alloc_ptr = [0]
        done_ranks = set()
        bseq = [[] for _ in range(8)]
        vidx = {id(sl): i for i, sl in enumerate(self.valloc)}

        def pick(ignore_banks):
            best = None
            for E in engs:
                lst = pend[E]
                if not lst:
                    continue
                et = etime[E]
                cb = None
                for o in lst[:window]:
                    if o.nun:
                        continue
                    st = o.ready if o.ready > et else et
                    if use_banks and o.vs and not ignore_banks:
                        k = 0
                        mn = None
                        for v in o.vs:
                            if id(v) not in vbank:
                                k += 1
                                r_ = vrank[id(v)]
                                if mn is None or r_ < mn:
                                    mn = r_
                        if k and mn > alloc_ptr[0] + 40:
                            continue
                        if k:
                            bt = sorted(bank_free)[k - 1]
                            if bt > st:
                                st = bt
                    if st == INF:
                        continue
                    if cb is None or st < cb[0]:
                        cb = (st, o)
                        if st <= et:
                            break
                if cb is not None and (best is None or cb[0] < best[0]):
                    best = (cb[0], cb[1], E)
            return best

        while remaining:
            best = pick(False)
            if best is None:
                best = pick(True)
            assert best is not None, "scheduler stuck (cyclic deps?)"
            st, o, E = best
            o.t0 = st
            o.t1 = st + max(o.cost, o.lat)
            etime[E] = st + o.cost
            pend[E].remove(o)
            order[E].append(o)
            remaining -= 1
            if use_banks:
                for v in o.vs:
                    if id(v) not in vbank:
                        ok = [b for b in range(8) if bank_free[b] <= st]
                        if ok:
                            bsel = max(ok, key=lambda x: bank_free[x])
                        else:
                            bsel = min(range(8), key=lambda x: (bank_free[x] == INF, bank_free[x]))
                        if not ok:
                            self.n_overcommit += 1
                        vbank[id(v)] = bsel
                        bank_free[bsel] = INF
                        done_ranks.add(vrank[id(v)])
                        while alloc_ptr[0] in done_ranks:
                            alloc_ptr[0] += 1
                        bseq[bsel].append(vidx[id(v)])
                for v in o.vs:
                    vleft[id(v)] -= 1
                    if o.t1 > vend[id(v)]:
                        vend[id(v)] = o.t1
                    if vleft[id(v)] == 0:
                        bank_free[vbank[id(v)]] = vend[id(v)]
            for u in o.users:
                u.nun -= 1
                if o.t1 > u.ready:
                    u.ready = o.t1
        if use_banks:
            self.bank_plan = [vbank.get(id(sl), 0) for sl in self.valloc]
            self.bank_seq_out = bseq
        self.est_ns = max(etime.values())
        for E in engs:
            E.cnt = 0
        allops = sorted(self.ops, key=lambda o: (o.t0, o.idx))
        for o in allops:
            o.sem.cnt += o.inc
            o.sig = o.sem.cnt
        for E in engs:
            waited = {}
            for o in order[E]:
                best = {}
                for d in o.deps:
                    if E.is_pe and d.eng is E and d.sem is E:
                        continue
                    k = id(d.sem)
                    if k not in best or best[k].sig < d.sig:
                        best[k] = d
                for k, d in best.items():
                    if waited.get(k, 0) >= d.sig:
                        continue
                    waited[k] = d.sig
                    E.prog.append(("w", d.sem.sem, d.sig))
                E.prog.append(("o", o.fn, o.sem.sem, o.inc))

    def pbank(self, pin=False, pool=None):
        i = self.nalloc
        self.nalloc += 1
        b = (i % 8) if self.plan is None else self.plan[i]
        sl = Slot(f"vbank{i}")
        sl.ops = []
        self.valloc.append(sl)
        return self.ps[:, b, :], sl

    def fbank(self, b):
        return self.pbank()

    def color_banks(self):
        iv = []
        for i, sl in enumerate(self.valloc):
            if not sl.ops:
                iv.append((0.0, 0.0, i))
                continue
            iv.append((min(o.t0 for o in sl.ops), max(o.t1 for o in sl.ops), i))
        plan = [0] * len(iv)
        free = [0.0] * 8
        for st, en, i in sorted(iv):
            ok = [b for b in range(8) if free[b] <= st]
            if ok:
                b = max(ok, key=lambda x: free[x])
            else:
                b = min(range(8), key=lambda x: free[x])
            plan[i] = b
            free[b] = max(free[b], en)
        return plan

    def unpin(self, slot):
        pass

    def build(self, emit=True):
        nc = self.nc
        nl = self.nl
        es = self.es
        dt = lambda name, shape, kind: nc.dram_tensor(name, shape, F32, kind=kind).ap()
        I, O = "ExternalInput", "ExternalOutput"
        self.xp = dt("xp", [T, D], I)
        self.xs = dt("xs", [NS, D], I)
        self.cc = dt("cc", [NS + 1, D], I)
        self.sret = dt("sret", [max(nl, 1), NS, H, 128, 128], I)
        self.spool = dt("spool", [max(nl, 1), NS, 15, 512], I)
        self.win = dt("win", [max(nl, 1), 5, 128, 4096], I)
        self.wring = dt("wring", [max(nl, 1), NRSLOT, 128, 2048], I)
        self.poolw = dt("poolw", [max(nl, 1), 128, 512], I)
        self.vecs = dt("vecs", [384, 128], I)
        self.cstd = dt("cst", [128, C_TOT], I)
        self.ropep = dt("ropep", [T // 128, 128, 256], I)
        self.bandsd = dt("bands", [128, NBAND], I)
        self.yp = dt("yp", [T, D], O)
        self.ys = dt("ys", [NS, D], O)
        self.rp = dt("rp", [max(nl, 1), H, 128, 128], O)
        self.pp = dt("pp", [max(nl, 1), 15, 512], O)
        self.rs = dt("rs", [max(nl, 1), NS, H, 128, 128], O)
        self.pso = dt("pso", [max(nl, 1), NS, 15, 512], O)

        self.sb_bytes = 207 * 1024
        self.sbt = es.enter_context(nc.sbuf_tensor("sb", [128, self.sb_bytes // 4], F32))
        self.sb = self.sbt[:, :]
        self.off = 0
        self.peak = 0
        self.pst = es.enter_context(nc.psum_tensor("ps", [128, 8, 512], F32))
        self.ps = self.pst[:, :, :]
        self.pslot = [Slot(f"psum{b}") for b in range(8)]
        self.pb_next = 0
        self.pinned = set()
        self.pbm_next = 4

        self.ops = []
        self.tag = ''
        self.phase = 'init'
        self.PE = Eng("pe", self.sem("s_pe"), is_pe=True)
        self.ACT = Eng("act", self.sem("s_act"))
        self.DVE = Eng("dve", self.sem("s_dve"))
        self.POOL = Eng("pool", self.sem("s_pool"))
        self.SP = Eng("sp", self.sem("s_sp"))
        self.out_ds = self.dsem("d_out")
        self.rp_ds = self.dsem("d_rp")

        self.final_ds = []
        self.fin = None
        self.alloc_persistent()
        self.phase_init()
        for l in range(nl):
            self.layer(l)
        self.phase_final()
        self.schedule()
        if not emit:
            self.es.close()
            return None
        self.SP.prog.append(("w", self.out_ds.sem, self.out_ds.cnt))
        for ds in self.stage_ds + [self.rp_ds] + self.final_ds:
            if ds.cnt:
                self.SP.prog.append(("w", ds.sem, ds.cnt))

        def replay(E):
            def run(h):
                for it in E.prog:
                    if it[0] == "w":
                        h.wait_ge(it[1], it[2])
                    else:
                        ins = it[1](h)
                        ins.then_inc(it[2], it[3])
            return run

        with nc.Block() as block:
            block.tensor(replay(self.PE))
            block.scalar(replay(self.ACT))
            block.vector(replay(self.DVE))
            block.gpsimd(replay(self.POOL))
            block.sync(replay(self.SP))
        self.es.close()
        return nc

    def alloc_persistent(self):
        c = self.carve
        self.xT = c([KC, T], F32)
        self.xslot = [[Slot(f"x{k}_{b}") for b in range(NBLK)] for k in range(KC)]
        self.xsT = c([KC, NS], F32)
        self.xs_slot = Slot("xs")
        self.cst = c([C_TOT], F32)
        self.cst_slot = Slot("cst")
        self.bandb = c([NBAND], BF16)
        self.band_slot = Slot("bands")
        self.identb = c([128], BF16)
        self.onesb = c([128], BF16)
        self.zerob = c([H, 128], BF16)
        self.const2_slot = Slot("const2")
        self.vecT = c([384], F32)
        self.vec_slot = Slot("vecT")
        self.scT = c([KC, NS + 1], BF16)
        self.sc_slot = Slot("scT")
        self.modT = c([48, NS + 1], F32)
        self.mod_slots = [Slot("modT0"), Slot("modT1")]
        self.pm = c([6, KC], F32)
        self.pm_slots = [Slot("pm0"), Slot("pm1")]
        self.smG = c([2, KC, NS], F32)
        self.smG_slots = [Slot("smG0"), Slot("smG1")]
        self.hbuf = c([KC, HALF], BF16)
        self.hslot = [[Slot(f"h{k}_{i}") for i in range(HALF // BT)] for k in range(KC)]
        self.winb_off = self.off
        self.winb = c([5, KC, 512], BF16)
        self.win_slot = [Slot(f"win{j}") for j in range(5)]
        self.win_ds = [self.dsem(f"d_win{j}") for j in range(5)]
        self.adar = self.winb.rearrange("p j k c -> p (j k c)").rearrange("p (i x) -> p i x", i=10)
        self.ada_slot = [Slot(f"adar{i}") for i in range(10)]
        self.ada_ds = [self.dsem(f"d_adar{i}") for i in range(10)]
        self.pwb = c([H, 128], BF16)
        self.pw_slot = Slot("poolw")
        self.pw_ds = self.dsem("d_pw")
        self.ring = c([NRING, 2048], BF16)
        self.ring_slot = [Slot(f"ring{j}") for j in range(NRING)]
        self.ring_ds = [self.dsem(f"d_ring{j}") for j in range(NRING)]
        self.S = c([H, 128], F32)
        self.Sb = c([H, 128], BF16)
        self.S_slot = Slot("S")
        self.Sb_slot = Slot("Sb")
        self.stage_ds = [self.dsem(f"d_stage{i}") for i in range(4)]
        self.mixTs = c([KC, NS], BF16)
        self.mixTs_slot = Slot("mixTs")
        self.tw = c([KC, NS], F32)
        self.tw_slot = Slot("tw")
        self.arena0 = self.off
        self.arena_slots = []
        self.ring_seq = []
        for l in range(self.nl):
            if l == 0:
                self.ring_seq += [(l, RING_ADA + j) for j in range(12)]
            for b in range(NBLK):
                self.ring_seq += [(l, RING_WOUT + j) for j in range(4)]
                if l == 0 and b == 0:
                    self.ring_seq += [(l, RING_ADA + j) for j in range(12, 24)]
            for hf in range(2):
                for (f0, f1) in FGROUPS:
                    self.ring_seq += [(l, RING_GU + f) for f in range(f0, f1)]
        self.ring_issued = 0
        self.ring_used = 0

    def ring_issue(self):
        if self.ring_issued >= len(self.ring_seq):
            return
        i = self.ring_issued
        self.ring_issued += 1
        l, si = self.ring_seq[i]
        j = i % NRING
        self.dma(self.POOL, self.ring[:, j, :], self.wring[l, si], self.ring_ds[j],
                 writes=[self.ring_slot[j]], max_dma_last_dim=8192)

    def ring_next(self, l, si):
        i = self.ring_used
        assert self.ring_seq[i] == (l, si), (self.ring_seq[i], l, si)
        while self.ring_issued <= i:
            self.ring_issue()
        self.ring_used += 1
        j = i % NRING
        return self.ring[:, j, :], self.ring_slot[j]

    def ring_done(self):
        while self.ring_issued < min(len(self.ring_seq), self.ring_used + NRING):
            self.ring_issue()

    def load_win(self, l):
        for j in range(5):
            self.dma(self.POOL, self.winb[:, j, :, :].rearrange("p k c -> p (k c)"), self.win[l, j],
                     self.win_ds[j], writes=[self.win_slot[j]], max_dma_last_dim=8192)
        self.dma(self.POOL, self.pwb[:, :, :].rearrange("p g d -> p (g d)"), self.poolw[l], self.pw_ds,
                 writes=[self.pw_slot], max_dma_last_dim=8192)

    def rstd_from_psum(self, ps_ap, ps_slot, out_ap, out_slot, tmp_ap, tmp_slot, inv_n, fast=True):
        self.op(self.ACT, act(tmp_ap, ps_ap, AF.Ln, bias=EPS, scale=inv_n), reads=[ps_slot], writes=[tmp_slot])
        self.op(self.ACT, act(out_ap, tmp_ap, AF.Exp, scale=-0.5), reads=[tmp_slot], writes=[out_slot])

    def enter_arena(self, new_slots):
        self.alias_handoff(self.arena_slots, new_slots)
        self.arena_slots = new_slots

    def phase_init(self):
        PE, ACT, DVE, SP = self.PE, self.ACT, self.DVE, self.SP
        ds_c = self.dsem("d_const")
        ds_c2 = self.dsem("d_const2")
        ds_c3 = self.dsem("d_const3")
        self.dma(SP, self.cst, self.cstd, ds_c, writes=[self.cst_slot])
        ident = self.cst[:, C_ID:C_ID + 128]
        self.ident = ident
        self.op(DVE, vcp(self.identb, ident), reads=[self.cst_slot], writes=[self.const2_slot])
        self.op(DVE, mset(self.onesb, 1.0), writes=[self.const2_slot])
        self.op(DVE, mset(self.zerob.rearrange("p h d -> p (h d)"), 0.0), writes=[self.const2_slot])
        self.dma(self.POOL, self.bandb, self.bandsd, self.dsem("d_band"), writes=[self.band_slot], max_dma_last_dim=4096)
        if self.nl > 0:
            self.load_win(0)
        self.off = self.arena0
        st = [self.carve([D], F32) for _ in range(2)]
        st_slot = [Slot("st0"), Slot("st1")]
        st_ds = [self.dsem("d_st0"), self.dsem("d_st1")]
        vst = self.carve([3, 128], F32)
        vst_slot = Slot("vst")
        c17 = self.carve([D], F32)
        c17_slot = Slot("c17")
        sc17 = self.carve([D], F32)
        sc17_slot = Slot("sc17")
        self.arena_slots = st_slot + [vst_slot, c17_slot, sc17_slot]
        self.dma(SP, vst, self.vecs.rearrange("(t p) c -> p t c", p=128), ds_c2, writes=[vst_slot])
        pb, pslot = self.pbank()
        self.op(PE, seq([tr(pb[:, t * 128:(t + 1) * 128], vst[:, t, :], ident) for t in range(3)]),
                reads=[vst_slot, self.cst_slot], writes=[pslot])
        self.op(ACT, acp(self.vecT, pb[:, 0:384]), reads=[pslot], writes=[self.vec_slot])
        n1 = NS + 1
        self.dma(SP, c17[0:n1, :], self.cc, ds_c3, writes=[c17_slot])
        self.op(ACT, act(sc17[0:n1, :], c17[0:n1, :], AF.Silu), reads=[c17_slot], writes=[sc17_slot])
        pb2, pslot2 = self.pbank()
        self.op(PE, seq([tr(pb2[:, k * n1:(k + 1) * n1], sc17[0:n1, k * 128:(k + 1) * 128], ident[0:n1, 0:n1]) for k in range(KC)]),
                reads=[sc17_slot, self.cst_slot], writes=[pslot2])
        self.op(ACT, acp(self.scT.rearrange("p k n -> p (k n)"), pb2[:, 0:KC * n1]), reads=[pslot2], writes=[self.sc_slot])
        self.dma(SP, st[0][0:NS, :], self.xs, st_ds[0], writes=[st_slot[0]])
        pb3, pslot3 = self.pbank()
        self.op(PE, seq([tr(pb3[:, k * NS:(k + 1) * NS], st[0][0:NS, k * 128:(k + 1) * 128], ident[0:NS, 0:NS]) for k in range(KC)]),
                reads=[st_slot[0], self.cst_slot], writes=[pslot3])
        self.op(ACT, acp(self.xsT.rearrange("p k n -> p (k n)"), pb3[:, 0:KC * NS]), reads=[pslot3], writes=[self.xs_slot])
        for t in range(T // 128):
            b = (t + 1) % 2
            self.dma(SP, st[b], self.xp[t * 128:(t + 1) * 128, :], st_ds[b], writes=[st_slot[b]])
            blk = (t * 128) // BT
            for half in range(2):
                pbx, pslx = self.pbank()
                self.op(PE, seq([tr(pbx[:, kk * 128:(kk + 1) * 128], st[b][:, (half * 4 + kk) * 128:(half * 4 + kk + 1) * 128], ident)
                                 for kk in range(4)]), reads=[st_slot[b], self.cst_slot], writes=[pslx])
                outv = self.xT[:, half * 4:half * 4 + 4, t * 128:(t + 1) * 128]
                inv = pbx.rearrange("p (k c) -> p k c", k=4)
                wr = [self.xslot[k][blk] for k in range(half * 4, half * 4 + 4)]
                if half == 0:
                    self.op(ACT, acp(outv, inv), reads=[pslx], writes=wr)
                else:
                    self.op(DVE, vcp(outv, inv), reads=[pslx], writes=wr)

    def stats_block(self, xviews, xslots, n, sq_bufs, sq_slots, rstd_ap, rstd_slot, tmp_ap, tmp_slot, pool=None):
        PE, ACT = self.PE, self.ACT
        pb, pslot = self.pbank(pool=pool)
        for k in range(KC):
            sq = sq_bufs[k % len(sq_bufs)]
            sqs = sq_slots[k % len(sq_bufs)]
            self.op(ACT, act(sq[:, 0:n], xviews[k], AF.Square), reads=xslots[k], writes=[sqs])
            self.op(PE, mm(pb[:, 0:n], self.onesb, sq[:, 0:n], start=(k == 0), stop=(k == KC - 1)),
                    reads=[sqs, self.const2_slot], writes=[pslot])
        self.rstd_from_psum(pb[:, 0:n], pslot, rstd_ap, rstd_slot, tmp_ap, tmp_slot, 1.0 / D)

    def xs_block(self, k, t0, n):
        return [self.xslot[k][i] for i in range(t0 // BT, (t0 + n + BT - 1) // BT)]

    def final_setup(self):
        if getattr(self, "fin", None) is not None:
            return
        save = self.off
        self.off = self.winb_off
        c = self.carve
        Fn = type("Fn", (), {})()
        sl = []
        def S(name):
            x = Slot(name)
            sl.append(x)
            return x
        Fn.fnbc = c([D], F32); Fn.fnbc_s = S("fn_bc")
        Fn.sqf = [c([KC, 128], BF16) for _ in range(2)]; Fn.sqf_s = [S("fn_sq0"), S("fn_sq1")]
        Fn.stg = [c([D], F32) for _ in range(2)]; Fn.stg_s = [S("fn_stg0"), S("fn_stg1")]
        Fn.lnt = [c([4], F32) for _ in range(2)]; Fn.lnt_s = [S("fn_ln0"), S("fn_ln1")]
        Fn.rst = [c([4], F32) for _ in range(2)]; Fn.rst_s = [S("fn_rs0"), S("fn_rs1")]
        Fn.sqs = c([KC * NS], BF16); Fn.sqs_s = S("fn_sqs")
        Fn.rstd16 = c([NS], F32); Fn.rstd16_s = S("fn_rstd16")
        Fn.tmp16 = c([NS], F32); Fn.tmp16_s = S("fn_tmp16")
        Fn.ysn = c([KC, NS], F32); Fn.ysn_s = S("fn_ysn")
        Fn.so = [c([512], F32, parts=NS) for _ in range(2)]; Fn.so_s = [S("fn_so0"), S("fn_so1")]
        assert self.off <= self.winb_off + 40 * 1024
        self.off = save
        self.alias_handoff(self.win_slot, sl)
        self.fin = Fn
        self.dma(self.SP, Fn.fnbc, self.vecs[V_FN:V_FN + 8, :].rearrange("k c -> (k c)").partition_broadcast(128),
                 self.dsem("d_fnbc"), writes=[Fn.fnbc_s])

    def final_tiles(self, t_lo, t_hi):
        PE, ACT, DVE, SP = self.PE, self.ACT, self.DVE, self.SP
        self.phase = 'final'
        self.final_setup()
        Fn = self.fin
        ident = self.ident
        for t in range(t_lo, t_hi):
            r = t % 2
            blk = (t * 128) // BT
            tsl = slice(t * 128, (t + 1) * 128)
            xs_all = [self.xslot[k][blk] for k in range(KC)]
            self.op(ACT, act(Fn.sqf[r], self.xT[:, :, tsl], AF.Square), reads=xs_all, writes=[Fn.sqf_s[r]])
            pbs, pss = self.pbank()
            self.op(PE, seq([mm(pbs[:, 0:1], Fn.sqf[r][:, k, :], self.onesb[:, 0:1], start=(k == 0), stop=(k == KC - 1)) for k in range(KC)]),
                    reads=[Fn.sqf_s[r], self.const2_slot], writes=[pss])
            self.op(ACT, act(Fn.lnt[r][:, 0:1], pbs[:, 0:1], AF.Ln, bias=EPS, scale=1.0 / D), reads=[pss], writes=[Fn.lnt_s[r]])
            self.op(ACT, act(Fn.rst[r][:, 0:1], Fn.lnt[r][:, 0:1], AF.Exp, scale=-0.5), reads=[Fn.lnt_s[r]], writes=[Fn.rst_s[r]])
            for half in range(2):
                pbx, pslx = self.pbank()
                self.op(PE, seq([tr(pbx[:, kk * 128:(kk + 1) * 128], self.xT[:, half * 4 + kk, tsl], ident) for kk in range(4)]),
                        reads=[self.xslot[half * 4 + kk][blk] for kk in range(4)] + [self.cst_slot], writes=[pslx])
                self.op(DVE, stt(Fn.stg[r][:, half * 512:(half + 1) * 512], pbx, Fn.rst[r][:, 0:1], Fn.fnbc[:, half * 512:(half + 1) * 512],
                                 ALU.mult, ALU.mult), reads=[pslx, Fn.rst_s[r], Fn.fnbc_s], writes=[Fn.stg_s[r]])
            self.dma(SP, self.yp[t * 128:(t + 1) * 128, :], Fn.stg[r], self.stage_ds[r], reads=[Fn.stg_s[r]])

    def final_samples(self):
        PE, ACT, DVE, SP = self.PE, self.ACT, self.DVE, self.SP
        self.phase = 'final'
        self.final_setup()
        Fn = self.fin
        ident = self.ident
        pb, pslot = self.pbank()
        self.op(ACT, act(Fn.sqs, self.xsT.rearrange("p k n -> p (k n)"), AF.Square), reads=[self.xs_slot], writes=[Fn.sqs_s])
        self.op(PE, seq([mm(pb[:, 0:NS], self.onesb, Fn.sqs[:, k * NS:(k + 1) * NS], start=(k == 0), stop=(k == KC - 1)) for k in range(KC)]),
                reads=[Fn.sqs_s, self.const2_slot], writes=[pslot])
        self.rstd_from_psum(pb[:, 0:NS], pslot, Fn.rstd16, Fn.rstd16_s, Fn.tmp16, Fn.tmp16_s, 1.0 / D)
        ysn = Fn.ysn
        self.op(DVE, tt(ysn, self.xsT, Fn.rstd16.unsqueeze(1).broadcast_to([128, KC, NS]), ALU.mult),
                reads=[self.xs_slot, Fn.rstd16_s], writes=[Fn.ysn_s])
        self.op(DVE, tt(ysn, ysn, self.vecT[:, V_FN:V_FN + KC].unsqueeze(2).broadcast_to([128, KC, NS]), ALU.mult),
                reads=[self.vec_slot], writes=[Fn.ysn_s])
        pb2, pslot2 = self.pbank()
        pb3, pslot3 = self.pbank()
        self.op(PE, seq([tr((pb2 if k < 4 else pb3)[0:NS, (k % 4) * 128:(k % 4 + 1) * 128], ysn[:, k, :], ident) for k in range(KC)]),
                reads=[Fn.ysn_s, self.cst_slot], writes=[pslot2, pslot3])
        self.op(ACT, acp(Fn.so[0], pb2[0:NS, :]), reads=[pslot2], writes=[Fn.so_s[0]])
        self.op(ACT, acp(Fn.so[1], pb3[0:NS, :]), reads=[pslot3], writes=[Fn.so_s[1]])
        self.dma(SP, self.ys[:, 0:512], Fn.so[0], self.stage_ds[2], reads=[Fn.so_s[0]])
        self.dma(SP, self.ys[:, 512:1024], Fn.so[1], self.stage_ds[3], reads=[Fn.so_s[1]])

    def phase_final(self):
        if self.nl == 0:
            self.final_tiles(0, T // 128)
            self.final_samples()

    def alloc_layer_arenas(self):
        c = self.carve
        self.off = self.arena0
        A = type("A", (), {})()
        self.smx = A
        sl = []
        def S(name):
            s = Slot(name)
            s.rng = self.rng_q.pop(0) if self.rng_q else self.rng_last
            sl.append(s)
            return s
        self.rng_q = []
        A.sq = c([KC * NS], BF16); A.sq_s = S("s_sq")
        A.rstd = c([NS], F32); A.rstd_s = S("s_rstd")
        A.tmp16 = c([NS], F32); A.tmp16_s = S("s_tmp16")
        A.t1 = c([KC, NS], F32); A.t1_s = S("s_t1")
        A.hs = c([KC, NS], BF16); A.hs_s = S("s_hs")
        A.Rq = c([512], F32, parts=NS); A.Rq_s = S("s_Rq")
        A.Rk = c([512], F32, parts=NS); A.Rk_s = S("s_Rk")
        A.A = c([512], F32, parts=NS); A.A_s = S("s_A")
        A.B = c([512], F32, parts=NS); A.B_s = S("s_B")
        A.prod = c([512], F32, parts=NS); A.prod_s = S("s_prod")
        A.s = c([H], F32, parts=NS); A.s_s = S("s_s")
        A.inner = c([512], F32, parts=NS); A.inner_s = S("s_inner")
        A.qexp = c([H, NS, NS], F32); A.qexp_s = S("s_qexp")
        NST = 4
        A.NST = NST
        A.St = [c([8, 128], F32) for _ in range(NST)]
        A.St_s = [S(f"s_St{i}") for i in range(NST)]
        A.St_ds = [self.dsem(f"d_St{i}") for i in range(NST)]
        A.St_ods = [self.dsem(f"d_Sto{i}") for i in range(NST)]
        A.vexp = c([NS, 128], BF16, parts=NS); A.vexp_s = S("s_vexp")
        A.Rkb = c([512], BF16, parts=NS); A.Rkb_s = S("s_Rkb")
        A.o = c([512], F32, parts=NS); A.o_s = S("s_o")
        A.osq = c([512], F32, parts=NS); A.osq_s = S("s_osq")
        A.ssum = c([H], F32, parts=NS); A.ssum_s = S("s_ssum")
        A.tmp4 = c([H], F32, parts=NS); A.tmp4_s = S("s_tmp4")
        A.rstdo = c([H], F32, parts=NS); A.rstdo_s = S("s_rstdo")
        A.on = c([512], F32, parts=NS); A.on_s = S("s_on")
        A.sg = c([H * NS], F32); A.sg_s = S("s_sg")
        A.us = c([H * NS], F32); A.us_s = S("s_us")
        A.buf = c([2, 512], F32, parts=120); A.buf_s = S("s_buf")
        A.buf_ds = self.dsem("d_sbuf")
        A.tq = c([H * NS], F32); A.tq_s = S("s_tq")
        A.pTs = c([H, NS], BF16); A.pTs_s = S("s_pTs")
        A.ustage = c([512], F32, parts=NS); A.ustage_s = S("s_ustage")
        A.ustage_ds = self.dsem("d_ustage")
        assert not self.rng_q
        self.smx_slots = sl
        self.final_ds += A.St_ods + [A.ustage_ds]
        smx_end = self.off
        self.off = self.arena0
        self.rng_q = []
        M = type("M", (), {})()
        self.mx = M
        sl = []
        M.sq = [c([BT], BF16) for _ in range(2)]; M.sq_s = [S("m_sq0"), S("m_sq1")]
        M.rstd = c([BT], F32); M.rstd_s = S("m_rstd")
        M.tmpS = c([BT], F32); M.tmpS_s = S("m_tmpS")
        M.htmp = [c([BT], F32) for _ in range(2)]; M.htmp_s = [S("m_ht0"), S("m_ht1")]
        M.rope = [c([NTB, 256], F32) for _ in range(2)]; M.rope_s = [S("m_rope0"), S("m_rope1")]; M.rope_ds = [self.dsem("d_rope0"), self.dsem("d_rope1")]
        M.R = [c([512], F32) for _ in range(2)]; M.R_s = [S("m_Rq"), S("m_Rk")]
        M.A = c([512], F32); M.A_s = S("m_A")
        M.B = c([512], F32); M.B_s = S("m_B")
        M.v = c([NTB, 512], BF16); M.v_s = [S(f"m_v{t}") for t in range(NTB)]
        M.kd = c([NTB, 512], BF16); M.kd_s = [S(f"m_kd{t}") for t in range(NTB)]
        M.qd = [c([512], BF16) for _ in range(2)]; M.qd_s = [S("m_qd0"), S("m_qd1")]
        M.u = [c([512], BF16) for _ in range(3)]; M.u_s = [S("m_u0"), S("m_u1"), S("m_u2")]
        M.qkT = [c([1024], BF16) for _ in range(2)]; M.qkT_s = [S("m_qkT0"), S("m_qkT1")]
        M.sTm = [c([H, 128], BF16) for _ in range(2)]; M.sTm_s = [S("m_sTm0"), S("m_sTm1")]
        M.Stmp = c([H, 128], F32); M.Stmp_s = S("m_Stmp")
        M.osq = c([512], BF16); M.osq_s = S("m_osq")
        M.rstdo = c([512], F32); M.rstdo_s = S("m_rstdo")
        M.to = c([512], F32); M.to_s = S("m_to")
        M.sgt = c([BT], F32); M.sgt_s = S("m_sgt")
        M.sgg = []
        M.sgg_s = []
        for i in range(2):
            M.sgg.append(c([H, BT], BF16))
            M.sgg_s.append([S(f"m_sgg{i}_{h}") for h in range(H)])
        M.pTt = [c([H, 128], BF16) for _ in range(2)]; M.pTt_s = [S("m_pTt0"), S("m_pTt1")]
        M.mixT = c([KC, BT], BF16); M.mix_s = [[S(f"m_mix{k}_{t}") for t in range(NTB)] for k in range(KC)]
        M.pstage = M.A; M.pstage_s = M.A_s; M.pstage_ds = self.dsem("d_pstage")
        assert not self.rng_q
        self.mx_slots = sl
        self.final_ds.append(M.pstage_ds)
        mx_end = self.off
        self.off = self.arena0
        self.rng_q = []
        Fb = type("F", (), {})()
        self.ff = Fb
        sl = []
        Fb.sq = [c([512], BF16) for _ in range(2)]; Fb.sq_s = [S("f_sq0"), S("f_sq1")]
        Fb.rstd = [c([512], F32) for _ in range(2)]; Fb.rstd_s = [S("f_rstd0"), S("f_rstd1")]
        Fb.tmpS = c([512], F32); Fb.tmpS_s = S("f_tmpS")
        Fb.htmp = c([512], F32); Fb.htmp_s = S("f_ht")
        Fb.sa = [c([512], F32) for _ in range(2)]; Fb.sa_s = [S("f_sa0"), S("f_sa1")]
        Fb.gT = c([8, HALF], BF16); Fb.gT_s = [[S(f"f_g{f}_{tb}") for tb in range(2)] for f in range(8)]
        Fb.wd = c([4, 2048], BF16); Fb.wd_s = [S(f"f_wd{j}") for j in range(4)]
        Fb.wd_ds = [self.dsem(f"d_wd{j}") for j in range(4)]
        Fb.h2s = c([KC, NS], BF16); Fb.h2s_s = S("f_h2s")
        Fb.gTs = c([8, NS], BF16); Fb.gTs_s = S("f_gTs")
        Fb.sas = c([NS], F32); Fb.sas_s = S("f_sas")
        Fb.rstd16 = c([NS], F32); Fb.rstd16_s = S("f_rstd16")
        Fb.tmp16 = c([NS], F32); Fb.tmp16_s = S("f_tmp16")
        Fb.t1 = c([KC, NS], F32); Fb.t1_s = S("f_t1")
        Fb.sqs = c([KC * NS], BF16); Fb.sqs_s = S("f_sqs")
        Fb.modN = c([48, NS + 1], F32); Fb.modN_s = S("f_modN")
        assert not self.rng_q
        self.ff_slots = sl
        ff_end = self.off
        self.arena_peaks = (smx_end, mx_end, ff_end)

    def layer(self, l):
        if l == 0:
            self.alloc_layer_arenas()
        self.enter_arena(self.smx_slots)
        self.phase = f'L{l}.ada'
        self.ada(l)
        self.phase = f'L{l}.smx'
        self.sample_mixer(l)
        self.enter_arena(self.mx_slots)
        self.phase = f'L{l}.mix'
        tiles = [(b, t) for b in range(NBLK) for t in range(NTB)]
        self.blk_front(l, 0)
        self.blk_head(l, 0, 0)
        for i, (b, t) in enumerate(tiles):
            if i + 1 < len(tiles):
                nb, nt = tiles[i + 1]
                if nt == 0:
                    pass
                self.blk_head(l, nb, nt)
            self.blk_tail(l, b, t)
            if t == NTB - 1:
                self.blk_wout(l, b)
                if l == 0 and b == 0:
                    ph_ = self.phase
                    self.phase = 'L0.ada'
                    self.ada_mm(0, self.modT, self.mod_slots, parts=(1,))
                    self.ada_finish(0, 1)
                    self.phase = ph_
                if b + 2 < NBLK:
                    self.blk_front(l, b + 2)
            if i == 0 and NBLK > 1:
                self.blk_front(l, 1)
        if l + 1 < self.nl:
            self.alias_handoff(self.win_slot, self.ada_slot)
        self.enter_arena(self.ff_slots)
        for hf in range(2):
            self.phase = f'L{l}.ffn{hf}'
            self.ffn_half(l, hf)
            if l == self.nl - 1:
                nt = HALF // 128
                self.final_tiles(hf * nt, (hf + 1) * nt)
                if hf == 0:
                    self.final_samples()
            if hf == 0 and l + 1 < self.nl:
                self.alias_handoff(self.ada_slot, self.win_slot)
                self.load_win(l + 1)
        if l + 1 < self.nl:
            self.op(self.DVE, vcp(self.modT.rearrange("p j n -> p (j n)"), self.ff.modN.rearrange("p j n -> p (j n)")),
                    reads=[self.ff.modN_s], writes=list(self.mod_slots))

    def ada_mm(self, l, dst, dst_slots, parts=(0, 1)):
        PE, DVE = self.PE, self.DVE
        n1 = NS + 1
        bada = self.vecT[:, V_BADA + l * 48:V_BADA + (l + 1) * 48]
        for part in parts:
            pb, ps = self.pbank()
            for j in range(12 * part, 12 * part + 12):
                sap, ss = self.ring_next(l, RING_ADA + j)
                w = sap.rearrange("p (k c) -> p k c", k=KC)
                fs = []
                for cc in range(2):
                    jc = 2 * j + cc
                    d_ = pb[:, (jc % 24) * n1:(jc % 24 + 1) * n1]
                    for k in range(KC):
                        fs.append(mm(d_, w[:, k, cc * 128:(cc + 1) * 128], self.scT[:, k, :], start=(k == 0), stop=(k == KC - 1)))
                self.op(PE, seq(fs), reads=[ss, self.sc_slot], writes=[ps])
                self.ring_done()
            self.op(DVE, tt(dst[:, part * 24:(part + 1) * 24, :], pb[:, 0:24 * n1].rearrange("p (j n) -> p j n", j=24),
                            bada[:, part * 24:(part + 1) * 24].unsqueeze(2).broadcast_to([128, 24, n1]), ALU.add),
                    reads=[ps, self.vec_slot], writes=[dst_slots[part]])

    def ada_step_side(self, l, j, dst, dst_slot):
        PE, DVE, POOL = self.PE, self.DVE, self.POOL
        n1 = NS + 1
        i = j % 10
        self.dma(POOL, self.adar[:, i, :], self.wring[l, RING_ADA + j], self.ada_ds[i], writes=[self.ada_slot[i]], max_dma_last_dim=8192)
        w = self.adar[:, i, :].rearrange("p (k c) -> p k c", k=KC)
        pb, ps = self.pbank()
        fs = []
        for cc in range(2):
            for k in range(KC):
                fs.append(mm(pb[:, cc * n1:(cc + 1) * n1], w[:, k, cc * 128:(cc + 1) * 128], self.scT[:, k, :], start=(k == 0), stop=(k == KC - 1)))
        self.op(PE, seq(fs), reads=[self.ada_slot[i], self.sc_slot], writes=[ps])
        bada = self.vecT[:, V_BADA + l * 48 + 2 * j:V_BADA + l * 48 + 2 * j + 2]
        self.op(DVE, tt(dst[:, 2 * j:2 * j + 2, :], pb[:, 0:2 * n1].rearrange("p (j n) -> p j n", j=2),
                        bada.unsqueeze(2).broadcast_to([128, 2, n1]), ALU.add), reads=[ps, self.vec_slot], writes=[dst_slot])

    def ada_finish(self, l, part):
        DVE = self.DVE
        n1 = NS + 1
        m0 = lambda a: self.modT[:, a * 8:(a + 1) * 8, 0]
        nv = self.vecT[:, (V_NM if part == 0 else V_NF) + l * 8:(V_NM if part == 0 else V_NF) + (l + 1) * 8]
        pm = self.pm
        r0 = 3 * part
        fs = [vcp(pm[:, r0, :], m0(r0)), stt(pm[:, r0 + 1, :], m0(r0 + 1), 1.0, nv, ALU.add, ALU.mult), vcp(pm[:, r0 + 2, :], m0(r0 + 2))]
        self.op(DVE, seq(fs), reads=[self.mod_slots[part], self.vec_slot], writes=[self.pm_slots[part]])
        bc = lambda v: v.unsqueeze(2).broadcast_to([128, KC, NS])
        c0 = 8 + 24 * part
        self.op(DVE, stt(self.smG[:, part], self.modT[:, c0:c0 + 8, 1:n1], 1.0, bc(nv), ALU.add, ALU.mult),
                reads=[self.mod_slots[part], self.vec_slot], writes=[self.smG_slots[part]])

    def ada(self, l):
        if l == 0:
            self.ada_mm(0, self.modT, self.mod_slots, parts=(0,))
            self.ada_finish(0, 0)
        else:
            self.ada_finish(l, 0)
            self.ada_finish(l, 1)

    def sample_mixer(self, l):
        PE, ACT, DVE, SP = self.PE, self.ACT, self.DVE, self.SP
        A = self.smx
        n1 = NS + 1
        ident = self.ident
        cstS = self.cst_slot
        self.op(ACT, act(A.sq, self.xsT.rearrange("p k n -> p (k n)"), AF.Square), reads=[self.xs_slot], writes=[A.sq_s])
        pb, ps = self.pbank()
        self.op(PE, seq([mm(pb[:, 0:NS], self.onesb, A.sq[:, k * NS:(k + 1) * NS], start=(k == 0), stop=(k == KC - 1)) for k in range(KC)]),
                reads=[A.sq_s, self.const2_slot], writes=[ps])
        self.rstd_from_psum(pb[:, 0:NS], ps, A.rstd, A.rstd_s, A.tmp16, A.tmp16_s, 1.0 / D)
        self.op(DVE, tt(A.t1, self.xsT, A.rstd.unsqueeze(1).broadcast_to([128, KC, NS]), ALU.mult), reads=[self.xs_slot, A.rstd_s], writes=[A.t1_s])
        self.op(DVE, tt(A.t1, A.t1, self.smG[:, 0], ALU.mult), reads=[self.smG_slots[0]], writes=[A.t1_s])
        self.op(DVE, tt(A.hs, A.t1, self.modT[:, 0:8, 1:n1], ALU.add), reads=[A.t1_s, self.mod_slots[0]], writes=[A.hs_s])
        bq, bk, bv = self.pbank(), self.pbank(), self.pbank(pin=True)
        banks = [bq, bk, bv]
        fs = []
        for k in range(KC):
            for n in range(3):
                fs.append(mm(banks[n][0][0:NS, :], A.hs[:, k, :], self.winb[:, n, k, :], start=(k == 0), stop=(k == KC - 1)))
        self.op(PE, seq(fs), reads=[A.hs_s] + self.win_slot[0:3], writes=[b[1] for b in banks])
        pbg, psg = self.pbank(pin=True)
        fs = []
        for j in range(12, 20):
            for k in range(KC):
                fs.append(mm(pbg[:, (j - 12) * NS:(j - 11) * NS], self.winb[:, j // 4, k, (j % 4) * 128:(j % 4 + 1) * 128], A.hs[:, k, :],
                             start=(k == 0), stop=(k == KC - 1)))
        self.op(PE, seq(fs), reads=[A.hs_s, self.win_slot[3], self.win_slot[4]], writes=[psg])
        ropes = self.cst[0:NS, C_ROPES:C_ROPES + 512].rearrange("p (a d) -> p a d", a=4)
        h3 = lambda ap: ap.rearrange("p (h d) -> p h d", h=H)
        def rope_s(bank, ci, R, R_s):
            x3 = h3(bank[0][0:NS, :])
            self.op(DVE, tt(h3(A.A), x3, ropes[:, ci, :].unsqueeze(1).broadcast_to([NS, H, 128]), ALU.mult), reads=[bank[1], cstS], writes=[A.A_s])
            self.op(DVE, seq([tt(h3(A.B)[:, :, 0:64], x3[:, :, 64:128], ropes[:, ci + 1, 0:64].unsqueeze(1).broadcast_to([NS, H, 64]), ALU.mult),
                              tt(h3(A.B)[:, :, 64:128], x3[:, :, 0:64], ropes[:, ci + 1, 64:128].unsqueeze(1).broadcast_to([NS, H, 64]), ALU.mult)]),
                    reads=[bank[1], cstS], writes=[A.B_s])
            self.op(DVE, tt(R, A.A, A.B, ALU.add), reads=[A.A_s, A.B_s], writes=[R_s])
        rope_s(bq, 0, A.Rq, A.Rq_s)
        rope_s(bk, 2, A.Rk, A.Rk_s)
        self.op(DVE, tt(A.prod, A.Rq, A.Rk, ALU.mult), reads=[A.Rq_s, A.Rk_s], writes=[A.prod_s])
        self.op(DVE, tred(A.s, h3(A.prod), ALU.add), reads=[A.prod_s], writes=[A.s_s])
        self.op(DVE, tt(h3(A.inner), h3(bv[0][0:NS, :]), A.s.unsqueeze(2).broadcast_to([NS, H, 128]), ALU.mult), reads=[bv[1], A.s_s], writes=[A.inner_s])
        self.op(ACT, acp(A.Rkb, A.Rk), reads=[A.Rk_s], writes=[A.Rkb_s])
        pbt, pst = self.pbank()
        self.op(PE, seq([tr(pbt[:, h * NS:(h + 1) * NS], A.Rq[:, h * 128:(h + 1) * 128], ident[0:NS, 0:NS]) for h in range(H)]),
                reads=[A.Rq_s, cstS], writes=[pst])
        id16 = self.cst[:, C_ID16:C_ID16 + 256].rearrange("p (a b) -> p a b", a=NS)
        self.op(DVE, tt(A.qexp, pbt[:, 0:H * NS].rearrange("p (h b) -> p h b", h=H).unsqueeze(3).broadcast_to([128, H, NS, NS]),
                        id16.unsqueeze(1).broadcast_to([128, H, NS, NS]), ALU.mult), reads=[pst, cstS], writes=[A.qexp_s])
        pbc, psc = self.pbank(pin=True)
        vflat = A.vexp.rearrange("p b e -> p (b e)")
        it = 0
        for h in range(H):
            self.op(DVE, tt(A.vexp, bv[0][0:NS, h * 128:(h + 1) * 128].unsqueeze(1).broadcast_to([NS, NS, 128]),
                            ident[0:NS, 0:NS].unsqueeze(2).broadcast_to([NS, NS, 128]), ALU.mult), reads=[bv[1], cstS], writes=[A.vexp_s])
            for bh in range(2):
                r = it % A.NST
                it += 1
                St, St_s = A.St[r], A.St_s[r]
                self.dma(SP, St, self.sret[l, bh * 8:(bh + 1) * 8, h].rearrange("b d e -> d b e"), A.St_ds[r], writes=[St_s])
                self.op(PE, seq([mm(pbc[0:NS, h * 128:(h + 1) * 128], A.qexp[:, h, b, :], St[:, b - bh * 8, :], start=(b == 0), stop=(b == NS - 1))
                                 for b in range(bh * 8, bh * 8 + 8)]), reads=[A.qexp_s, St_s], writes=[psc])
                for q4 in range(2):
                    b0 = bh * 8 + q4 * 4
                    pbu, psu = self.pbank()
                    self.op(PE, mm(pbu, A.Rkb[:, h * 128:(h + 1) * 128], vflat[:, b0 * 128:(b0 + 4) * 128]), reads=[A.Rkb_s, A.vexp_s], writes=[psu])
                    self.op(DVE, stt(St[:, q4 * 4:(q4 + 1) * 4, :], St[:, q4 * 4:(q4 + 1) * 4, :], float(GAM[h]),
                                     pbu.rearrange("p (b e) -> p b e", b=4), ALU.mult, ALU.add), reads=[psu], writes=[St_s])
                self.dma(SP, self.rs[l, bh * 8:(bh + 1) * 8, h].rearrange("b d e -> d b e"), St, A.St_ods[r], reads=[St_s])
        self.unpin(bv[1])
        self.op(DVE, seq([stt(h3(A.o)[:, h, :], pbc[0:NS, h * 128:(h + 1) * 128], float(GAM[h]), h3(A.inner)[:, h, :], ALU.mult, ALU.add) for h in range(H)]),
                reads=[psc, A.inner_s], writes=[A.o_s])
        self.unpin(psc)
        self.op(DVE, tt(A.osq, A.o, A.o, ALU.mult), reads=[A.o_s], writes=[A.osq_s])
        self.op(DVE, tred(A.ssum, h3(A.osq), ALU.add), reads=[A.osq_s], writes=[A.ssum_s])
        self.rstd_from_psum(A.ssum, A.ssum_s, A.rstdo, A.rstdo_s, A.tmp4, A.tmp4_s, 1.0 / 128)
        self.op(DVE, tt(h3(A.on), h3(A.o), A.rstdo.unsqueeze(2).broadcast_to([NS, H, 128]), ALU.mult), reads=[A.o_s, A.rstdo_s], writes=[A.on_s])
        pbo, pso_ = self.pbank()
        self.op(PE, seq([tr(pbo[:, h * NS:(h + 1) * NS], A.on[:, h * 128:(h + 1) * 128], ident[0:NS, 0:NS]) for h in range(H)]),
                reads=[A.on_s, cstS], writes=[pso_])
        self.op(ACT, act(A.sg, pbg[:, 0:H * NS], AF.Silu), reads=[psg], writes=[A.sg_s])
        self.op(ACT, acp(A.us, pbg[:, H * NS:2 * H * NS]), reads=[psg], writes=[A.us_s])
        self.unpin(psg)
        gn = lambda h: self.vecT[:, V_GN + l * 4 + h:V_GN + l * 4 + h + 1]
        self.op(DVE, seq([stt(self.mixTs[:, h, :], pbo[:, h * NS:(h + 1) * NS], gn(h), A.sg[:, h * NS:(h + 1) * NS], ALU.mult, ALU.mult) for h in range(H)]),
                reads=[pso_, A.sg_s, self.vec_slot], writes=[self.mixTs_slot])
        self.dma(SP, A.buf, self.spool[l].rearrange("(t b) r c -> (b r) t c", t=2), A.buf_ds, writes=[A.buf_s])
        self.dma(SP, self.pso[l, :, 0:14, :], self.spool[l, :, 1:15, :], self.out_ds)
        sel = self.cst[0:120, C_SEL:C_SEL + 128].rearrange("p (t g b) -> p t g b", t=2, g=4)
        pbs, pss = self.pbank()
        self.op(PE, seq([mm(pbs[:, g * NS:(g + 1) * NS], A.buf[:, t, g * 128:(g + 1) * 128], sel[:, t, g, :], start=(t == 0), stop=(t == 1))
                         for g in range(4) for t in range(2)]), reads=[A.buf_s, cstS], writes=[pss])
        fs = []
        for g, w in enumerate(POOL_W):
            fs.append(ts(A.tq[:, g * NS:(g + 1) * NS], A.us[:, g * NS:(g + 1) * NS], float(1.0 / w - 1.0), None, ALU.mult))
        self.op(DVE, seq(fs), reads=[A.us_s], writes=[A.tq_s])
        self.op(DVE, seq([stt(A.pTs[:, g, :], pbs[:, g * NS:(g + 1) * NS], float(1.0 / w), A.tq[:, g * NS:(g + 1) * NS], ALU.mult, ALU.add)
                          for g, w in enumerate(POOL_W)]), reads=[pss, A.tq_s], writes=[A.pTs_s])
        pby, psy = self.pbank()
        self.op(PE, seq([mm(pby[:, g * NS:(g + 1) * NS], self.pwb[:, g, :], A.pTs[:, g, :]) for g in range(4)]),
                reads=[A.pTs_s, self.pw_slot], writes=[psy])
        pscale = self.vecT[:, V_PS + l * 4:V_PS + (l + 1) * 4]
        self.op(DVE, tt(self.mixTs[:, 4:8, :], pby[:, 0:4 * NS].rearrange("p (g n) -> p g n", g=4), pscale.unsqueeze(2).broadcast_to([128, 4, NS]), ALU.mult),
                reads=[psy, self.vec_slot], writes=[self.mixTs_slot])
        pbr, psr = self.pbank()
        self.op(PE, seq([tr(pbr[0:NS, g * 128:(g + 1) * 128], A.us[:, g * NS:(g + 1) * NS], ident) for g in range(4)]),
                reads=[A.us_s, cstS], writes=[psr])
        self.op(ACT, acp(A.ustage, pbr[0:NS, :]), reads=[psr], writes=[A.ustage_s])
        self.dma(SP, self.pso[l, :, 14, :], A.ustage, A.ustage_ds, reads=[A.ustage_s])

    def blk_front(self, l, b):
        PE, ACT, DVE, SP = self.PE, self.ACT, self.DVE, self.SP
        M = self.mx
        cstS = self.cst_slot
        t0 = b * BT
        hb = b % 2
        hT = lambda k: self.hbuf[:, k, hb * BT:(hb + 1) * BT]
        hS = lambda k: self.hslot[k][hb]
        pmv = lambda i, k: self.pm[:, i, k:k + 1]
        h3 = lambda ap: ap.rearrange("p (h d) -> p h d", h=H)
        hreads = [hS(k) for k in range(KC)]
        gn = lambda h: self.vecT[:, V_GN + l * 4 + h:V_GN + l * 4 + h + 1]
        sgg = M.sgg[b % 2]
        sgg_s = M.sgg_s[b % 2]
        rope = M.rope[b % 2]
        rope_s = M.rope_s[b % 2]
        self.stats_block([self.xT[:, k, t0:t0 + BT] for k in range(KC)], [[self.xslot[k][b]] for k in range(KC)],
                         BT, M.sq, M.sq_s, M.rstd, M.rstd_s, M.tmpS, M.tmpS_s, pool="m")
        for k in range(KC):
            r = k % 2
            self.op(DVE, stt(M.htmp[r], self.xT[:, k, t0:t0 + BT], pmv(1, k), M.rstd, ALU.mult, ALU.mult),
                    reads=[self.xslot[k][b], M.rstd_s, self.pm_slots[0]], writes=[M.htmp_s[r]])
            self.op(ACT, act(hT(k), M.htmp[r], AF.Identity, bias=pmv(0, k)), reads=[M.htmp_s[r], self.pm_slots[0]], writes=[hS(k)])
        self.dma(SP, rope, self.ropep[b * NTB:(b + 1) * NTB].rearrange("t p c -> p t c"), M.rope_ds[b % 2], writes=[rope_s])
        if b == 0:
            self.op(DVE, mset(self.S.rearrange("p h d -> p (h d)"), 0.0), writes=[self.S_slot])
            self.op(DVE, mset(self.Sb.rearrange("p h d -> p (h d)"), 0.0), writes=[self.Sb_slot])
        for j in range(12, 16):
            h = j - 12
            pb, ps = self.pbank(pool="m")
            self.op(PE, seq([mm(pb[:, 0:BT], self.winb[:, j // 4, k, (j % 4) * 128:(j % 4 + 1) * 128], hT(k), start=(k == 0), stop=(k == KC - 1))
                             for k in range(KC)]), reads=hreads + [self.win_slot[j // 4]], writes=[ps])
            self.op(ACT, act(M.sgt, pb[:, 0:BT], AF.Silu), reads=[ps], writes=[M.sgt_s])
            fg_ = ts(sgg[:, h, :], M.sgt, gn(h), 1.0, ALU.mult, ALU.mult)
            fg_.cost = 800.0
            self.op(self.POOL, fg_, reads=[M.sgt_s, self.vec_slot], writes=[sgg_s[h]])

    def blk_head(self, l, b, t):
        PE, ACT, DVE, SP = self.PE, self.ACT, self.DVE, self.SP
        M = self.mx
        cstS = self.cst_slot
        t0 = b * BT
        hb = b % 2
        hT = lambda k: self.hbuf[:, k, hb * BT:(hb + 1) * BT]
        hS = lambda k: self.hslot[k][hb]
        pmv = lambda i, k: self.pm[:, i, k:k + 1]
        h3 = lambda ap: ap.rearrange("p (h d) -> p h d", h=H)
        hreads = [hS(k) for k in range(KC)]
        gn = lambda h: self.vecT[:, V_GN + l * 4 + h:V_GN + l * 4 + h + 1]
        sgg = M.sgg[b % 2]
        sgg_s = M.sgg_s[b % 2]
        rope = M.rope[b % 2]
        rope_s = M.rope_s[b % 2]
        caus = self.cst[:, C_CAUS:C_CAUS + 128]
        bcur = self.bandb[:, 0:512].rearrange("p (g t) -> p g t", g=4)
        bcur0 = self.bandb[:, 512:1024].rearrange("p (g t) -> p g t", g=4)
        bprev = self.bandb[:, 1024:1088].rearrange("p (g t) -> p g t", g=4)
        pscale = lambda g: self.vecT[:, V_PS + l * 4 + g:V_PS + l * 4 + g + 1]
        wsl = [0, 1, 2, 4]
        c = b * NTB + t
        r2 = c % 2
        r3 = c % 3
        tc = slice(t * 128, (t + 1) * 128)
        banks = [self.fbank(i) for i in range(4)]
        bq, bk, bv, bu = banks
        fs = []
        for k in range(KC):
            for n in range(4):
                fs.append(mm(banks[n][0], self.hbuf[:, k, hb * BT + t * 128:hb * BT + (t + 1) * 128], self.winb[:, wsl[n], k, :],
                             start=(k == 0), stop=(k == KC - 1)))
        self.op(PE, seq(fs), reads=hreads + [self.win_slot[i] for i in wsl], writes=[x[1] for x in banks])
        self.op(ACT, acp(M.v[:, t, :], bv[0]), reads=[bv[1]], writes=[M.v_s[t]])
        self.op(ACT, acp(M.u[r3], bu[0]), reads=[bu[1]], writes=[M.u_s[r3]])
        if c == T // 128 - 1:
            self.op(ACT, acp(M.pstage[96:128, :], bu[0][96:128, :]), reads=[bu[1]], writes=[M.pstage_s])
            self.dma(SP, self.pp[l], M.pstage[113:128, :], M.pstage_ds, reads=[M.pstage_s])
        cos2 = rope[:, t, 0:128].unsqueeze(1).broadcast_to([128, H, 128])
        sinlo = rope[:, t, 128:192].unsqueeze(1).broadcast_to([128, H, 64])
        sinhi = rope[:, t, 192:256].unsqueeze(1).broadcast_to([128, H, 64])
        for qi, bank in enumerate((bq, bk)):
            x3 = h3(bank[0])
            self.op(DVE, tt(h3(M.A), x3, cos2, ALU.mult), reads=[bank[1], rope_s], writes=[M.A_s])
            self.op(DVE, seq([tt(h3(M.B)[:, :, 0:64], x3[:, :, 64:128], sinlo, ALU.mult),
                              tt(h3(M.B)[:, :, 64:128], x3[:, :, 0:64], sinhi, ALU.mult)]), reads=[bank[1], rope_s], writes=[M.B_s])
            self.op(DVE, tt(M.R[qi], M.A, M.B, ALU.add), reads=[M.A_s, M.B_s], writes=[M.R_s[qi]])
        dq3 = self.cst[:, C_DQTM:C_DQTM + 4].unsqueeze(2).broadcast_to([128, H, 128])
        dk3 = self.cst[:, C_DKTM:C_DKTM + 4].unsqueeze(2).broadcast_to([128, H, 128])
        fq_ = tt(h3(M.qd[r2]), h3(M.R[0]), dq3, ALU.mult)
        fq_.cost = 2000.0
        self.op(self.POOL, fq_, reads=[M.R_s[0], cstS], writes=[M.qd_s[r2]])
        fk_ = tt(h3(M.kd[:, t, :]), h3(M.R[1]), dk3, ALU.mult)
        fk_.cost = 2000.0
        self.op(self.POOL, fk_, reads=[M.R_s[1], cstS], writes=[M.kd_s[t]])

    def blk_tail(self, l, b, t):
        PE, ACT, DVE, SP = self.PE, self.ACT, self.DVE, self.SP
        M = self.mx
        cstS = self.cst_slot
        t0 = b * BT
        hb = b % 2
        hT = lambda k: self.hbuf[:, k, hb * BT:(hb + 1) * BT]
        hS = lambda k: self.hslot[k][hb]
        pmv = lambda i, k: self.pm[:, i, k:k + 1]
        h3 = lambda ap: ap.rearrange("p (h d) -> p h d", h=H)
        hreads = [hS(k) for k in range(KC)]
        gn = lambda h: self.vecT[:, V_GN + l * 4 + h:V_GN + l * 4 + h + 1]
        sgg = M.sgg[b % 2]
        sgg_s = M.sgg_s[b % 2]
        rope = M.rope[b % 2]
        rope_s = M.rope_s[b % 2]
        caus = self.cst[:, C_CAUS:C_CAUS + 128]
        bcur = self.bandb[:, 0:512].rearrange("p (g t) -> p g t", g=4)
        bcur0 = self.bandb[:, 512:1024].rearrange("p (g t) -> p g t", g=4)
        bprev = self.bandb[:, 1024:1088].rearrange("p (g t) -> p g t", g=4)
        pscale = lambda g: self.vecT[:, V_PS + l * 4 + g:V_PS + l * 4 + g + 1]
        wsl = [0, 1, 2, 4]
        c = b * NTB + t
        r2 = c % 2
        r3 = c % 3
        tc = slice(t * 128, (t + 1) * 128)
        pbt, pst = self.pbank(pool="m")
        pbt16 = pbt.bitcast(BF16)
        fs = [tr(pbt16[:, h * 128:(h + 1) * 128], M.qd[r2][:, h * 128:(h + 1) * 128], self.identb) for h in range(H)]
        fs += [tr(pbt16[:, 512 + h * 128:512 + (h + 1) * 128], M.kd[:, t, h * 128:(h + 1) * 128], self.identb) for h in range(H)]
        self.op(PE, seq(fs), reads=[M.qd_s[r2], M.kd_s[t], self.const2_slot], writes=[pst])
        self.op(ACT, acp(M.qkT[r2], pbt16), reads=[pst], writes=[M.qkT_s[r2]])
        qdT = M.qkT[r2][:, 0:512].rearrange("p (h d) -> p h d", h=H)
        kdT = M.qkT[r2][:, 512:1024].rearrange("p (h d) -> p h d", h=H)
        pbs, pss = self.pbank(pool="m")
        self.op(PE, seq([mm(pbs[:, h * 128:(h + 1) * 128], kdT[:, h, :], qdT[:, h, :]) for h in range(H)]),
                reads=[M.qkT_s[r2]], writes=[pss])
        sTm = M.sTm[r2]
        self.op(DVE, tt(sTm, h3(pbs), caus.unsqueeze(1).broadcast_to([128, H, 128]), ALU.mult), reads=[pss, cstS], writes=[M.sTm_s[r2]])
        pbo, pso_ = self.pbank(pool="m")
        fs = []
        for h in range(H):
            fs.append(mm(pbo[:, h * 128:(h + 1) * 128], M.v[:, t, h * 128:(h + 1) * 128], sTm[:, h, :], start=True, stop=False))
            fs.append(mm(pbo[:, h * 128:(h + 1) * 128], self.Sb[:, h, :], qdT[:, h, :], start=False, stop=True))
        self.op(PE, seq(fs), reads=[M.v_s[t], M.sTm_s[r2], self.Sb_slot, M.qkT_s[r2]], writes=[pso_])
        pbu, psu = self.pbank(pool="m")
        self.op(PE, seq([mm(pbu[:, h * 128:(h + 1) * 128], M.kd[:, t, h * 128:(h + 1) * 128], M.v[:, t, h * 128:(h + 1) * 128]) for h in range(H)]),
                reads=[M.kd_s[t], M.v_s[t]], writes=[psu])
        self.op(DVE, tt(M.Stmp, h3(pbu), self.S, ALU.add), reads=[psu, self.S_slot], writes=[M.Stmp_s])
        self.op(DVE, seq([ts(self.S[:, h, :], M.Stmp[:, h, :], float(GC[h]), None, ALU.mult) for h in range(H)]),
                reads=[M.Stmp_s], writes=[self.S_slot])
        self.op(ACT, acp(self.Sb.rearrange("p h d -> p (h d)"), self.S.rearrange("p h d -> p (h d)")), reads=[self.S_slot], writes=[self.Sb_slot])
        self.op(ACT, act(M.osq, pbo, AF.Square), reads=[pso_], writes=[M.osq_s])
        pbn, psn = self.pbank(pool="m")
        self.op(PE, mm(pbn, self.onesb, M.osq), reads=[M.osq_s, self.const2_slot], writes=[psn])
        self.rstd_from_psum(pbn, psn, M.rstdo, M.rstdo_s, M.to, M.to_s, 1.0 / 128)
        self.op(DVE, tt(M.to, pbo, M.rstdo, ALU.mult), reads=[pso_, M.rstdo_s], writes=[M.to_s])
        self.op(DVE, tt(M.mixT[:, 0:4, tc], h3(M.to), sgg[:, :, tc], ALU.mult),
                reads=[M.to_s] + sgg_s, writes=[M.mix_s[k][t] for k in range(4)])
        pbp, psp = self.pbank(pool="m")
        fs = []
        for g in range(4):
            gs = slice(g * 128, (g + 1) * 128)
            fs.append(mm(pbp[:, gs], M.u[r3][:, gs], (bcur0 if c == 0 else bcur)[:, g, :], start=True, stop=(c == 0)))
            if c > 0:
                fs.append(mm(pbp[:, g * 128:g * 128 + 16], M.u[(c - 1) % 3][64:128, gs], bprev[64:128, g, :], start=False, stop=True))
        self.op(PE, seq(fs), reads=[M.u_s[r3], self.band_slot] + ([M.u_s[(c - 1) % 3]] if c > 0 else []), writes=[psp])
        self.op(ACT, acp(M.pTt[r2].rearrange("p g t -> p (g t)"), pbp), reads=[psp], writes=[M.pTt_s[r2]])
        pby, psy = self.pbank(pool="m")
        self.op(PE, seq([mm(pby[:, g * 128:(g + 1) * 128], self.pwb[:, g, :], M.pTt[r2][:, g, :]) for g in range(4)]),
                reads=[M.pTt_s[r2], self.pw_slot], writes=[psy])
        ps4 = self.vecT[:, V_PS + l * 4:V_PS + (l + 1) * 4].unsqueeze(2).broadcast_to([128, 4, 128])
        self.op(DVE, tt(M.mixT[:, 4:8, tc], pby.rearrange("p (g t) -> p g t", g=4), ps4, ALU.mult),
                reads=[psy, self.vec_slot], writes=[M.mix_s[4 + g][t] for g in range(4)])

    def blk_wout(self, l, b):
        PE, ACT, DVE, SP = self.PE, self.ACT, self.DVE, self.SP
        M = self.mx
        cstS = self.cst_slot
        t0 = b * BT
        hb = b % 2
        hT = lambda k: self.hbuf[:, k, hb * BT:(hb + 1) * BT]
        hS = lambda k: self.hslot[k][hb]
        pmv = lambda i, k: self.pm[:, i, k:k + 1]
        h3 = lambda ap: ap.rearrange("p (h d) -> p h d", h=H)
        hreads = [hS(k) for k in range(KC)]
        gn = lambda h: self.vecT[:, V_GN + l * 4 + h:V_GN + l * 4 + h + 1]
        sgg = M.sgg[b % 2]
        sgg_s = M.sgg_s[b % 2]
        rope = M.rope[b % 2]
        rope_s = M.rope_s[b % 2]
        if b == NBLK - 1:
            self.dma(SP, self.rp[l].rearrange("h d e -> d h e"), self.S, self.rp_ds, reads=[self.S_slot])
        mreads = [M.mix_s[k][t] for k in range(KC) for t in range(NTB)]
        if b == 0:
            pbw, psw = self.pbank(pin=True, pool="m")
        for jq in range(4):
            sap, ss = self.ring_next(l, RING_WOUT + jq)
            w = sap.rearrange("p (k c) -> p k c", k=KC)
            for cc in range(2):
                cch = 2 * jq + cc
                pb, ps = self.pbank(pool="m")
                self.op(PE, seq([mm(pb[:, 0:BT], w[:, k, cc * 128:(cc + 1) * 128], M.mixT[:, k, :], start=(k == 0), stop=(k == KC - 1))
                                 for k in range(KC)]), reads=mreads + [ss], writes=[ps])
                xv = self.xT[:, cch, t0:t0 + BT]
                self.op(DVE, stt(xv, pb[:, 0:BT], pmv(2, cch), xv, ALU.mult, ALU.add), reads=[ps, self.pm_slots[0]], writes=[self.xslot[cch][b]])
                if b == 0:
                    self.op(PE, seq([mm(pbw[:, cch * NS:(cch + 1) * NS], w[:, k, cc * 128:(cc + 1) * 128], self.mixTs[:, k, :],
                                        start=(k == 0), stop=(k == KC - 1)) for k in range(KC)]), reads=[self.mixTs_slot, ss], writes=[psw])
            self.ring_done()
        if b == 0:
            n1 = NS + 1
            self.op(DVE, tt(self.tw, pbw[:, 0:KC * NS].rearrange("p (k n) -> p k n", k=KC), self.modT[:, 16:24, 1:n1], ALU.mult),
                    reads=[psw, self.mod_slots[0]], writes=[self.tw_slot])
            self.unpin(psw)
            self.op(DVE, tt(self.xsT, self.xsT, self.tw, ALU.add), reads=[self.tw_slot], writes=[self.xs_slot])

    def ffn_half(self, l, hf):
        PE, ACT, DVE, SP, POOL = self.PE, self.ACT, self.DVE, self.SP, self.POOL
        Fb = self.ff
        pmv = lambda i, k: self.pm[:, i, k:k + 1]
        n1 = NS + 1
        t0 = hf * HALF
        nbb = 512 // BT
        for tb in range(2):
            tt0 = t0 + tb * 512
            self.stats_block([self.xT[:, k, tt0:tt0 + 512] for k in range(KC)], [self.xs_block(k, tt0, 512) for k in range(KC)],
                             512, Fb.sq, Fb.sq_s, Fb.rstd[tb], Fb.rstd_s[tb], Fb.tmpS, Fb.tmpS_s)
        for tb in range(2):
            tt0 = t0 + tb * 512
            for k in range(KC):
                self.op(DVE, stt(Fb.htmp, self.xT[:, k, tt0:tt0 + 512], pmv(4, k), Fb.rstd[tb], ALU.mult, ALU.mult),
                        reads=self.xs_block(k, tt0, 512) + [Fb.rstd_s[tb], self.pm_slots[1]], writes=[Fb.htmp_s])
                self.op(ACT, act(self.hbuf[:, k, tb * 512:(tb + 1) * 512], Fb.htmp, AF.Identity, bias=pmv(3, k)),
                        reads=[Fb.htmp_s, self.pm_slots[1]], writes=[self.hslot[k][tb * nbb + i] for i in range(nbb)])
        hreads = [[self.hslot[k][tb * nbb + i] for k in range(KC) for i in range(nbb)] for tb in range(2)]
        samp = (hf == 0)
        if samp:
            self.op(ACT, act(Fb.sqs, self.xsT.rearrange("p k n -> p (k n)"), AF.Square), reads=[self.xs_slot], writes=[Fb.sqs_s])
            pb, ps = self.pbank()
            self.op(PE, seq([mm(pb[:, 0:NS], self.onesb, Fb.sqs[:, k * NS:(k + 1) * NS], start=(k == 0), stop=(k == KC - 1)) for k in range(KC)]),
                    reads=[Fb.sqs_s, self.const2_slot], writes=[ps])
            self.rstd_from_psum(pb[:, 0:NS], ps, Fb.rstd16, Fb.rstd16_s, Fb.tmp16, Fb.tmp16_s, 1.0 / D)
            self.op(DVE, tt(Fb.t1, self.xsT, Fb.rstd16.unsqueeze(1).broadcast_to([128, KC, NS]), ALU.mult), reads=[self.xs_slot, Fb.rstd16_s], writes=[Fb.t1_s])
            self.op(DVE, tt(Fb.t1, Fb.t1, self.smG[:, 1], ALU.mult), reads=[self.smG_slots[1]], writes=[Fb.t1_s])
            self.op(DVE, tt(Fb.h2s, Fb.t1, self.modT[:, 24:32, 1:n1], ALU.add), reads=[Fb.t1_s, self.mod_slots[1]], writes=[Fb.h2s_s])
        sai = 0
        for (f0, f1) in FGROUPS:
            nf = f1 - f0
            for jp in range(f0 // 2, f1 // 2):
                jl = jp - f0 // 2
                self.dma(POOL, Fb.wd[:, jl, :], self.wring[l, RING_DOWN + jp], Fb.wd_ds[jl], writes=[Fb.wd_s[jl]], max_dma_last_dim=8192)
            for f in range(f0, f1):
                fl = f - f0
                sap, ss = self.ring_next(l, RING_GU + f)
                w = sap.rearrange("p (k c) -> p k c", k=KC)
                pa = [self.pbank() for _ in range(2)]
                pbk = [self.pbank() for _ in range(2)]
                if samp:
                    pbsm, pssm = self.pbank()
                for tb in range(2):
                    for half_ab, banks in ((0, pa), (1, pbk)):
                        fs = []
                        for k in range(KC):
                            lw = w[:, k, half_ab * 128:(half_ab + 1) * 128]
                            fs.append(mm(banks[tb][0], lw, self.hbuf[:, k, tb * 512:(tb + 1) * 512], start=(k == 0), stop=(k == KC - 1)))
                        if samp and tb == 1:
                            for k in range(KC):
                                lw = w[:, k, half_ab * 128:(half_ab + 1) * 128]
                                fs.append(mm(pbsm[:, half_ab * NS:(half_ab + 1) * NS], lw, Fb.h2s[:, k, :], start=(k == 0), stop=(k == KC - 1)))
                        self.op(PE, seq(fs), reads=hreads[tb] + [ss] + ([Fb.h2s_s] if (samp and tb == 1) else []),
                                writes=[banks[tb][1]] + ([pssm] if (samp and tb == 1) else []))
                self.ring_done()
                if hf == 0 and l + 1 < self.nl:
                    ph_ = self.phase
                    self.phase = f'L{l + 1}.ada'
                    if f >= 8:
                        g_ = f - 8
                        js = [2 * g_, 2 * g_ + 1] if g_ < 10 else [20 + (g_ - 10)]
                        for j in js:
                            self.ada_step_side(l + 1, j, self.ff.modN, self.ff.modN_s)
                    self.phase = ph_
                for tb in range(2):
                    r = sai % 2
                    sai += 1
                    self.op(ACT, act(Fb.sa[r], pa[tb][0], AF.Silu), reads=[pa[tb][1]], writes=[Fb.sa_s[r]])
                    self.op(DVE, tt(Fb.gT[:, fl, tb * 512:(tb + 1) * 512], pbk[tb][0], Fb.sa[r], ALU.mult), reads=[pbk[tb][1], Fb.sa_s[r]],
                            writes=[Fb.gT_s[fl][tb]])
                if samp:
                    self.op(ACT, act(Fb.sas, pbsm[:, 0:NS], AF.Silu), reads=[pssm], writes=[Fb.sas_s])
                    self.op(DVE, tt(Fb.gTs[:, fl, :], pbsm[:, NS:2 * NS], Fb.sas, ALU.mult), reads=[pssm, Fb.sas_s], writes=[Fb.gTs_s])
            wdv = Fb.wd.rearrange("p j (i c) -> p j i c", i=2)
            if samp:
                pbd, psd = self.pbank(pin=True)
            for cch in range(KC):
                for tb in range(2):
                    pb, ps = self.pbank()
                    self.op(PE, seq([mm(pb, wdv[:, fl // 2, fl % 2, cch * 128:(cch + 1) * 128], Fb.gT[:, fl, tb * 512:(tb + 1) * 512],
                                        start=(fl == 0), stop=(fl == nf - 1)) for fl in range(nf)]),
                            reads=[Fb.gT_s[fl][tb] for fl in range(nf)] + [Fb.wd_s[j] for j in range(nf // 2)], writes=[ps])
                    tt0 = t0 + tb * 512
                    xv = self.xT[:, cch, tt0:tt0 + 512]
                    self.op(DVE, stt(xv, pb, pmv(5, cch), xv, ALU.mult, ALU.add), reads=[ps, self.pm_slots[1]], writes=self.xs_block(cch, tt0, 512))
                if samp:
                    self.op(PE, seq([mm(pbd[:, cch * NS:(cch + 1) * NS], wdv[:, fl // 2, fl % 2, cch * 128:(cch + 1) * 128], Fb.gTs[:, fl, :],
                                        start=(fl == 0), stop=(fl == nf - 1)) for fl in range(nf)]),
                            reads=[Fb.gTs_s] + [Fb.wd_s[j] for j in range(nf // 2)], writes=[psd])
            if samp:
                self.op(DVE, tt(self.tw, pbd[:, 0:KC * NS].rearrange("p (k n) -> p k n", k=KC), self.modT[:, 40:48, 1:n1], ALU.mult),
                        reads=[psd, self.mod_slots[1]], writes=[self.tw_slot])
                self.unpin(psd)
                self.op(DVE, tt(self.xsT, self.xsT, self.tw, ALU.add), reads=[self.tw_slot], writes=[self.xs_slot])


_CACHE = {}


def _get_nc(nl):
    if nl not in _CACHE:
        b1 = Builder(nl)
        b1.build(emit=False)
        assert b1.n_overcommit == 0, b1.n_overcommit
        b = Builder(nl, plan=b1.bank_plan, bank_seq=b1.bank_seq_out)
        _CACHE[nl] = b.build()
    return _CACHE[nl]


def run(inputs, nl=DEPTH):
    g = lambda k: np.asarray(inputs[k], dtype=np.float32)
    x_prompt, x_sample, c_prompt, c_sample = g("x_prompt"), g("x_sample"), g("c_prompt"), g("c_sample")
    state_ret, state_pool = g("state_ret"), g("state_pool")
    nle = max(nl, 1)
    cst, ropep, bands = _host_consts()
    win, wring, poolw = _host_weights(g("w_in"), g("w_out"), g("w_gu"), g("w_down"), g("w_ada"), g("pool_w"), nle)
    vecs = np.zeros((384, 128), np.float32)
    vecs[V_NM:V_NM + 32] = g("norm_mix").reshape(32, 128)
    vecs[V_NF:V_NF + 32] = g("norm_ffn").reshape(32, 128)
    vecs[V_BADA:V_BADA + 192] = g("b_ada").reshape(192, 128)
    vecs[V_GN:V_GN + 16] = g("ret_gn").reshape(16, 128)
    vecs[V_PS:V_PS + 16] = g("pool_scale").reshape(16, 128)
    vecs[V_FN:V_FN + 8] = g("final_norm").reshape(8, 128)
    in_maps = []
    for c in range(NCORES):
        sl = slice(c * NS, (c + 1) * NS)
        in_maps.append({
            "xp": np.ascontiguousarray(x_prompt[c]),
            "xs": np.ascontiguousarray(x_sample[sl, 0, :]),
            "cc": np.ascontiguousarray(np.concatenate([c_prompt[c:c + 1], c_sample[sl]], 0)),
            "sret": np.ascontiguousarray(state_ret[:nle, sl]),
            "spool": np.ascontiguousarray(state_pool[:nle, sl]),
            "win": win, "wring": wring, "poolw": poolw, "vecs": vecs, "cst": cst, "ropep": ropep, "bands": bands,
        })
    nc = _get_nc(nl)
    res = run_bass_kernel_spmd(nc, in_maps, core_ids=list(range(NCORES)))
    R = res.results
    y_prompt = np.stack([np.asarray(R[c]["yp"]) for c in range(NCORES)], 0).astype(np.float32)
    y_sample = np.concatenate([np.asarray(R[c]["ys"]) for c in range(NCORES)], 0).reshape(NCORES * NS, 1, D).astype(np.float32)
    new_ret_prompt = np.stack([np.asarray(R[c]["rp"]) for c in range(NCORES)], 1).astype(np.float32)
    new_pool_prompt = np.stack([np.asarray(R[c]["pp"]) for c in range(NCORES)], 1).astype(np.float32)
    new_ret_sample = np.concatenate([np.asarray(R[c]["rs"]) for c in range(NCORES)], 1).astype(np.float32)
    new_pool_sample = np.concatenate([np.asarray(R[c]["pso"]) for c in range(NCORES)], 1).astype(np.float32)
    return (y_prompt, y_sample, new_ret_prompt, new_pool_prompt, new_ret_sample, new_pool_sample)


def kernel(**inputs):
    return run(inputs, DEPTH)
```
